# Optimizing a Trainium2 kernel written in Bass

```python
import jax, jax.numpy as jnp
from jax import lax
import numpy as np

D_MODEL = 1024
BATCH = 8
SEQ = 2048
DEPTH = 4

PLE_DIM = 256
NORM_EPS = 1e-6

ATTN_HEADS = 4
ATTN_HEAD_DIM = 64
ATTN_WIDTH = ATTN_HEADS * ATTN_HEAD_DIM
MOBA_BLOCK = 256
MOBA_TOPK = 3
MOBA_QCHUNK = 64

RET_HEADS = 4
RET_KEY_DIM = 64
RET_VALUE_DIM = 128
RET_QK_WIDTH = RET_HEADS * RET_KEY_DIM
RET_V_WIDTH = RET_HEADS * RET_VALUE_DIM
RET_CHUNK = 128
ROPE_BASE = 10000.0

POOL_WINDOWS = (2, 4, 8, 16)
POOL_GROUPS = len(POOL_WINDOWS)
POOL_WIDTH = D_MODEL // 4
POOL_GROUP_DIM = POOL_WIDTH // POOL_GROUPS

N_BRANCHES = 3
IN_SPLIT_WIDTHS = (ATTN_WIDTH, ATTN_WIDTH, ATTN_WIDTH,
                   RET_QK_WIDTH, RET_QK_WIDTH, RET_V_WIDTH, RET_V_WIDTH,
                   POOL_WIDTH,
                   D_MODEL, D_MODEL, D_MODEL)
IN_WIDTH = sum(IN_SPLIT_WIDTHS)
IN_SPLIT_POINTS = tuple(int(s) for s in np.cumsum(IN_SPLIT_WIDTHS)[:-1])

FFN_DIM = 2816
CONV_WIDTH = 3

kernel_name = "hybrid_moba_pool_retention_trunk"


def rms_norm(x, g):
    xf = x.astype(jnp.float32)
    y = xf * lax.rsqrt(jnp.mean(xf * xf, axis=-1, keepdims=True) + NORM_EPS)
    return (y * g.astype(jnp.float32)).astype(x.dtype)


def moba_attention(q, k, v):
    B, T, H, dh = q.shape
    dt = q.dtype
    n_blk = -(-T // MOBA_BLOCK)
    t_pad = n_blk * MOBA_BLOCK
    pad = ((0, 0), (0, t_pad - T), (0, 0), (0, 0))
    q, k, v = jnp.pad(q, pad), jnp.pad(k, pad), jnp.pad(v, pad)
    kb = k.reshape(B, n_blk, MOBA_BLOCK, H, dh).transpose(0, 3, 1, 2, 4)
    vb = v.reshape(B, n_blk, MOBA_BLOCK, H, dh).transpose(0, 3, 1, 2, 4)
    k_mean = jnp.mean(kb.astype(jnp.float32), axis=3)
    n_chunks = t_pad // MOBA_QCHUNK
    qc = q.reshape(B, n_chunks, MOBA_QCHUNK, H, dh).transpose(1, 0, 3, 2, 4)
    top_k = min(MOBA_TOPK, n_blk)
    scale = ATTN_HEAD_DIM ** -0.5
    blk_ids = jnp.arange(n_blk)
    b_idx = jnp.arange(B)[:, None, None, None]
    h_idx = jnp.arange(H)[None, :, None, None]
    q_off = jnp.arange(MOBA_QCHUNK)
    k_off = jnp.arange(MOBA_BLOCK)

    def chunk_fn(args):
        q_c, c = args
        start = c * MOBA_QCHUNK
        own = start // MOBA_BLOCK
        gate = jnp.einsum('bhqd,bhnd->bhqn', q_c.astype(jnp.float32), k_mean)
        gate = jnp.where(blk_ids[None, None, None, :] < own, gate, -jnp.inf)
        _, sel = lax.top_k(gate, top_k)
        valid = sel < own
        k_sel = kb[b_idx, h_idx, sel]
        v_sel = vb[b_idx, h_idx, sel]
        s_sel = jnp.einsum('bhqd,bhqjld->bhqjl', q_c, k_sel).astype(jnp.float32) * scale
        s_sel = jnp.where(valid[..., None], s_sel, -jnp.inf)
        s_sel = s_sel.reshape(B, H, MOBA_QCHUNK, top_k * MOBA_BLOCK)
        k_own = lax.dynamic_index_in_dim(kb, own, axis=2, keepdims=False)
        v_own = lax.dynamic_index_in_dim(vb, own, axis=2, keepdims=False)
        s_own = jnp.einsum('bhqd,bhld->bhql', q_c, k_own).astype(jnp.float32) * scale
        causal = (own * MOBA_BLOCK + k_off)[None, :] <= (start + q_off)[:, None]
        s_own = jnp.where(causal[None, None], s_own, -jnp.inf)
        probs = jax.nn.softmax(jnp.concatenate([s_sel, s_own], axis=-1), axis=-1).astype(dt)
        p_sel = probs[..., :top_k * MOBA_BLOCK].reshape(B, H, MOBA_QCHUNK, top_k, MOBA_BLOCK)
        p_own = probs[..., top_k * MOBA_BLOCK:]
        return (jnp.einsum('bhqjl,bhqjld->bhqd', p_sel, v_sel)
                + jnp.einsum('bhql,bhld->bhqd', p_own, v_own))

    out = lax.map(chunk_fn, (qc, jnp.arange(n_chunks, dtype=jnp.int32)))
    out = out.transpose(1, 0, 3, 2, 4).reshape(B, t_pad, H * dh)
    return out[:, :T].astype(dt)


def rotary(x, pos):
    half = x.shape[-1] // 2
    inv_freq = ROPE_BASE ** (-jnp.arange(half, dtype=jnp.float32) / half)
    ang = pos[:, None] * inv_freq[None, :]
    cos = jnp.cos(ang)[None, :, None, :]
    sin = jnp.sin(ang)[None, :, None, :]
    x1, x2 = x[..., :half], x[..., half:]
    return jnp.concatenate([x1 * cos - x2 * sin, x1 * sin + x2 * cos], axis=-1)


def retention(q, k, v, g):
    B, T, H, dk = q.shape
    dv = v.shape[-1]
    dt = v.dtype
    f32 = jnp.float32
    pos = jnp.arange(T, dtype=f32)
    q = rotary(q.astype(f32), pos)
    k = rotary(k.astype(f32), pos) * (dk ** -0.5)
    v = v.astype(f32)
    log_gamma = jnp.log1p(-jnp.exp2(-5.0 - jnp.arange(H, dtype=f32)))
    C = RET_CHUNK
    nc = T // C
    qc = q.reshape(B, nc, C, H, dk)
    kc = k.reshape(B, nc, C, H, dk)
    vc = v.reshape(B, nc, C, H, dv)
    idx = jnp.arange(C, dtype=f32)
    rel = idx[:, None] - idx[None, :]
    decay = jnp.where(rel[None] >= 0, jnp.exp(jnp.maximum(rel, 0.0)[None] * log_gamma[:, None, None]), 0.0)
    scores = jnp.einsum('bnihd,bnjhd->bnhij', qc, kc) * decay[None, None]
    inner = jnp.einsum('bnhij,bnjhe->bnihe', scores, vc)
    zeta = jnp.exp((C - 1.0 - idx)[None, :] * log_gamma[:, None])
    xi = jnp.exp((idx + 1.0)[None, :] * log_gamma[:, None])
    gamma_chunk = jnp.exp(C * log_gamma)
    u = jnp.einsum('bnjhd,bnjhe,hj->bnhde', kc, vc, zeta)

    def step(state, u_n):
        return u_n + gamma_chunk[None, :, None, None] * state, state

    _, r_prev = lax.scan(step, jnp.zeros((B, H, dk, dv), f32), jnp.moveaxis(u, 1, 0))
    r_prev = jnp.moveaxis(r_prev, 0, 1)
    cross = jnp.einsum('bnihd,bnhde,hi->bnihe', qc, r_prev, xi)
    y = (inner + cross).reshape(B, T, H, dv)
    mu = jnp.mean(y, axis=-1, keepdims=True)
    var = jnp.mean(jnp.square(y - mu), axis=-1, keepdims=True)
    y = ((y - mu) * lax.rsqrt(var + NORM_EPS)).reshape(B, T, H * dv)
    return (jax.nn.silu(g.astype(f32)) * y).astype(dt)


def multiscale_pool(u, w_group, scale):
    B, T, _ = u.shape
    f32 = jnp.float32
    uf = u.astype(f32).reshape(B, T, POOL_GROUPS, POOL_GROUP_DIM)
    cs = jnp.cumsum(uf, axis=1)
    cs = jnp.concatenate([jnp.zeros_like(cs[:, :1]), cs], axis=1)
    t = jnp.arange(T)
    win = jnp.array(POOL_WINDOWS, dtype=jnp.int32)
    lo = jnp.maximum(t[:, None] + 1 - win[None, :], 0)
    g_ids = jnp.arange(POOL_GROUPS)[None, :]
    window_sum = cs[:, 1:] - cs[:, lo, g_ids]
    count = (t[:, None] + 1 - lo).astype(f32)
    mixed = window_sum / count[None, :, :, None] - uf
    y = jnp.einsum('btgc,gcd->btgd', mixed, w_group.astype(f32)).reshape(B, T, POOL_WIDTH)
    return (y * scale.astype(f32)).astype(u.dtype)


def causal_dwconv(u, w, b):
    C = u.shape[-1]
    y = lax.conv_general_dilated(u, w.astype(u.dtype)[:, None, :], window_strides=(1,),
                                 padding=[(CONV_WIDTH - 1, 0)],
                                 dimension_numbers=('NWC', 'WIO', 'NWC'),
                                 feature_group_count=C)
    return y + b.astype(u.dtype)


def setup_inputs(seed: int = 0) -> dict:
    key = jax.random.key(seed)
    ks = jax.random.split(key, 20)
    f32 = jnp.float32

    def w(k, shape, fan_in):
        return jax.random.normal(k, shape, f32) * (fan_in ** -0.5)

    def gain(k, shape):
        return 1.0 + 0.05 * jax.random.normal(k, shape, f32)

    return {
        "x": jax.random.normal(ks[0], (BATCH, SEQ, D_MODEL), f32),
        "p": jax.random.normal(ks[1], (DEPTH, BATCH, SEQ, PLE_DIM), f32),
        "norm_mix_g": gain(ks[2], (DEPTH, D_MODEL)),
        "w_in": w(ks[3], (DEPTH, D_MODEL, IN_WIDTH), D_MODEL),
        "w_branch_attn": w(ks[4], (DEPTH, ATTN_WIDTH, D_MODEL), ATTN_WIDTH),
        "w_branch_ret": w(ks[5], (DEPTH, RET_V_WIDTH, D_MODEL), RET_V_WIDTH),
        "w_branch_pool": w(ks[6], (DEPTH, POOL_WIDTH, D_MODEL), POOL_WIDTH),
        "pool_w": w(ks[7], (DEPTH, POOL_GROUPS, POOL_GROUP_DIM, POOL_GROUP_DIM), POOL_GROUP_DIM),
        "pool_scale": gain(ks[8], (DEPTH, POOL_WIDTH)),
        "w_out": w(ks[9], (DEPTH, D_MODEL, D_MODEL), D_MODEL),
        "norm_ffn_g": gain(ks[10], (DEPTH, D_MODEL)),
        "w_up": w(ks[11], (DEPTH, D_MODEL, 2 * FFN_DIM), D_MODEL),
        "conv_w": w(ks[12], (DEPTH, CONV_WIDTH, 2 * FFN_DIM), CONV_WIDTH),
        "conv_b": 0.02 * jax.random.normal(ks[13], (DEPTH, 2 * FFN_DIM), f32),
        "w_down": w(ks[14], (DEPTH, FFN_DIM, D_MODEL), FFN_DIM),
        "norm_ple_g": gain(ks[15], (DEPTH, D_MODEL)),
        "w_ple_gate": w(ks[16], (DEPTH, D_MODEL, D_MODEL), D_MODEL),
        "w_ple_proj": w(ks[17], (DEPTH, PLE_DIM, D_MODEL), PLE_DIM),
        "norm_final_g": gain(ks[18], (D_MODEL,)),
    }


def reference(x, p, norm_mix_g, w_in, w_branch_attn, w_branch_ret, w_branch_pool, pool_w,
              pool_scale, w_out, norm_ffn_g, w_up, conv_w, conv_b, w_down, norm_ple_g,
              w_ple_gate, w_ple_proj, norm_final_g):
    B, T, _ = x.shape
    for i in range(DEPTH):
        h = rms_norm(x, norm_mix_g[i])
        proj = h @ w_in[i]
        (qa, ka, va, qr, kr, vr, gr, u_pool, gate_a, gate_r, gate_p) = jnp.split(proj, IN_SPLIT_POINTS, axis=-1)
        attn = moba_attention(qa.reshape(B, T, ATTN_HEADS, ATTN_HEAD_DIM),
                              ka.reshape(B, T, ATTN_HEADS, ATTN_HEAD_DIM),
                              va.reshape(B, T, ATTN_HEADS, ATTN_HEAD_DIM))
        ret = retention(qr.reshape(B, T, RET_HEADS, RET_KEY_DIM),
                        kr.reshape(B, T, RET_HEADS, RET_KEY_DIM),
                        vr.reshape(B, T, RET_HEADS, RET_VALUE_DIM), gr)
        pooled = multiscale_pool(u_pool, pool_w[i], pool_scale[i])
        merged = (jax.nn.sigmoid(gate_a) * (attn @ w_branch_attn[i])
                  + jax.nn.sigmoid(gate_r) * (ret @ w_branch_ret[i])
                  + jax.nn.sigmoid(gate_p) * (pooled @ w_branch_pool[i]))
        x = x + merged @ w_out[i]
        h = rms_norm(x, norm_ffn_g[i])
        up = causal_dwconv(h @ w_up[i], conv_w[i], conv_b[i])
        a, b = jnp.split(up, 2, axis=-1)
        x = x + (jax.nn.gelu(a) * b) @ w_down[i]
        h = rms_norm(x, norm_ple_g[i])
        x = x + jax.nn.sigmoid(h @ w_ple_gate[i]) * (p[i] @ w_ple_proj[i])
    return rms_norm(x, norm_final_g)
```

```python
import numpy as np
from collections import deque
from contextlib import ExitStack
import concourse.bass as bass
import concourse.mybir as mybir
from concourse.bass_utils import run_bass_kernel_spmd

F32 = mybir.dt.float32
BF16 = mybir.dt.bfloat16
AF = mybir.ActivationFunctionType
ALU = mybir.AluOpType
AX = mybir.AxisListType

ENGS = ('pe', 'act', 'dve', 'pool', 'sp')

T = 2048
D = 1024
L = 4
TG = 512
NG = T // TG
KC = D // 128
FF = 2816
INW = 5632
EPS = 1e-6
NW = 4


class Buf:
    __slots__ = ('name', 'w', 'r', 'rd', 'excl')

    def __init__(self, name='', excl=False):
        self.name = name
        self.excl = excl
        self.w = None
        self.r = {}
        self.rd = []


class Op:
    __slots__ = ('eng', 'fn', 'deps', 'idx', 'inc', 'semval', 'dma_key', 'dma_val', 'dwait', 'ep')


class Sched:
    def __init__(self):
        self.q = {e: [] for e in ENGS}
        self.dma_cnt = {}
        self.epoch = 0

    def _track(self, op, reads, writes):
        deps = []
        for b in reads:
            if b.w is not None:
                deps.append(b.w)
            if b.excl:
                for e_, o_ in b.r.items():
                    if e_ != op.eng:
                        deps.append(o_)
        for b in writes:
            if b.w is not None:
                deps.append(b.w)
            deps.extend(b.r.values())
            deps.extend(b.rd)
        for b in reads:
            if op.dma_key is not None:
                b.rd.append(op)
            else:
                b.r[op.eng] = op
        for b in writes:
            b.w = op
            b.r = {}
            b.rd = []
        return deps

    def op(self, eng, fn, r=(), w=(), key=None):
        o = Op()
        o.eng = eng
        o.ep = self.epoch
        o.fn = fn
        o.inc = False
        o.semval = 0
        o.dma_key = key
        o.dma_val = 0
        if key is not None:
            self.dma_cnt[key] = self.dma_cnt.get(key, 0) + 1
            o.dma_val = 16 * self.dma_cnt[key]
        o.deps = self._track(o, r, w)
        o.dwait = {}
        for d in o.deps:
            if d.dma_key is not None:
                o.dwait[d.dma_key] = 16 * self.dma_cnt[d.dma_key] - (16 if d.dma_key == key else 0)
        o.idx = len(self.q[eng])
        self.q[eng].append(o)
        return o

    def emit(self, nc, es):
        for e in ENGS:
            for o in self.q[e]:
                for d in o.deps:
                    if d.dma_key is None and not (d.eng == o.eng == 'pe'):
                        d.inc = True
        semh = {}
        for e in ENGS:
            cnt = {}
            for o in self.q[e]:
                if o.dma_key is None and o.inc:
                    cnt[o.ep] = cnt.get(o.ep, 0) + 1
                    o.semval = cnt[o.ep]
                    if ('eng', e, o.ep) not in semh:
                        semh[('eng', e, o.ep)] = es.enter_context(nc.semaphore(f's_{e}_{o.ep}'))
        for k in self.dma_cnt:
            semh[('dma', k)] = es.enter_context(nc.semaphore('d_' + str(k)))
        self.nwaits = 0

        def run(eobj, ename):
            seen = {}
            for o in self.q[ename]:
                waits = {}
                for d in o.deps:
                    if d.dma_key is not None:
                        k = ('dma', d.dma_key)
                        v = o.dwait[d.dma_key]
                    else:
                        if d.eng == ename == 'pe':
                            continue
                        k = ('eng', d.eng, d.ep)
                        v = d.semval
                    if v > waits.get(k, 0):
                        waits[k] = v
                for k, v in waits.items():
                    if seen.get(k, 0) >= v:
                        continue
                    seen[k] = v
                    eobj.wait_ge(semh[k], v)
                    self.nwaits += 1
                ins = o.fn(eobj)
                if o.dma_key is not None:
                    ins.then_inc(semh[('dma', o.dma_key)], 16)
                elif o.inc:
                    ins.then_inc(semh[('eng', ename, o.ep)], 1)
            last = {}
            for o in self.q[ename]:
                if o.dma_key is not None:
                    last[o.dma_key] = max(last.get(o.dma_key, 0), o.dma_val)
            for k, v in last.items():
                if seen.get(('dma', k), 0) < v:
                    eobj.wait_ge(semh[('dma', k)], v)

        with nc.Block() as block:
            @block.tensor
            def _(e):
                run(e, 'pe')

            @block.scalar
            def _(e):
                run(e, 'act')

            @block.vector
            def _(e):
                run(e, 'dve')

            @block.gpsimd
            def _(e):
                run(e, 'pool')

            @block.sync
            def _(e):
                run(e, 'sp')


class Tile:
    __slots__ = ('ap', 'buf', 'lo', 'hi', 'arena')


class Arena:
    def __init__(self, tensor, ncols, name):
        self.t = tensor
        self.n = ncols
        self.free = [(0, ncols)]
        self.ghosts = []
        self.name = name
        self.peak = 0
        self.used = 0
        self.rover = 0

    def alloc(self, n, name='', rot=None):
        n0 = n
        n = (n + 31) // 32 * 32
        if rot is None:
            rot = n <= 640
        cand = [(lo, hi) for (lo, hi) in self.free if hi - lo >= n]
        pick = None
        for (lo, hi) in (cand if rot else []):
            if hi > self.rover and hi - max(lo, self.rover) >= n:
                pick = (lo, hi, max(lo, self.rover))
                break
        if pick is None and cand:
            pick = (cand[0][0], cand[0][1], cand[0][0])
        if pick is not None:
            flo, fhi, lo = pick
            self.free.remove((flo, fhi))
            if lo > flo:
                self.free.append((flo, lo))
            if lo + n < fhi:
                self.free.append((lo + n, fhi))
            self.free.sort()
            if rot:
                self.rover = lo + n
            for _ in (0,):
                t = Tile()
                t.lo, t.hi, t.arena = lo, lo + n, self
                t.buf = Buf(name)
                t.ap = self.t[:, lo:lo + n0]
                keep = []
                for (glo, ghi, ops) in self.ghosts:
                    if glo < t.hi and ghi > t.lo:
                        for o in ops:
                            if o.dma_key is not None:
                                t.buf.rd.append(o)
                            else:
                                p = t.buf.r.get(o.eng)
                                if p is None or p.idx < o.idx:
                                    t.buf.r[o.eng] = o
                        if glo >= t.lo and ghi <= t.hi:
                            continue
                    keep.append((glo, ghi, ops))
                self.ghosts = keep
                self.used += n
                self.peak = max(self.peak, self.used)
                return t
        raise RuntimeError(f"arena {self.name} out of space for {name} n={n} used={self.used} free={self.free}")

    def release(self, t):
        ops = list(t.buf.r.values()) + list(t.buf.rd)
        if t.buf.w is not None:
            ops.append(t.buf.w)
        self.ghosts.append((t.lo, t.hi, ops))
        self.used -= (t.hi - t.lo)
        fl = self.free + [(t.lo, t.hi)]
        fl.sort()
        out = []
        for lo, hi in fl:
            if out and out[-1][1] == lo:
                out[-1] = (out[-1][0], hi)
            else:
                out.append((lo, hi))
        self.free = out


def v3(ap, b):
    return ap.rearrange("p (a b) -> p a b", b=b)


C_DEC = 0
C_XI = C_DEC + 512
C_ZS = C_XI + 256
C_COS = C_ZS + 4
C_SIN = C_COS + 512
C_RC = C_SIN + 512
C_RCN = C_RC + 2
C_EPS = C_RCN + 32
C_NEG = C_EPS + 1
NCF = C_NEG + 32
B_ID = 0
B_TRI = 128
B_ONE = 256
B_E64 = 384
NCB = B_E64 + 1024
P_GMIX = 0
P_GFFN = P_GMIX + L * 8
P_GPLE = P_GFFN + L * 8
P_GFIN = P_GPLE + L * 8
P_CW = P_GFIN + 8
P_CB = P_CW + L * 3 * 44
P_PS = P_CB + L * 44
NPAR = P_PS + L * 2


def make_consts():
    f64 = np.float64
    H = 4
    lg = np.log1p(-np.exp2(-5.0 - np.arange(H, dtype=f64)))
    cf = np.zeros((128, NCF), np.float32)
    i = np.arange(128)
    rel = i[None, :] - i[:, None]
    dec = np.zeros((128, H, 128), f64)
    for h in range(H):
        dec[:, h, :] = np.where(rel >= 0, np.exp(np.maximum(rel, 0) * lg[h]), 0.0) * 0.125
    cf[:, C_DEC:C_DEC + 512] = dec.reshape(128, 512)
    xi = np.zeros((128, 2, 128), f64)
    for p in range(128):
        for c in range(2):
            h = 2 * c + p // 64
            xi[p, c, :] = np.exp((i + 1.0) * lg[h]) * 0.125
    cf[:, C_XI:C_XI + 256] = xi.reshape(128, 256)
    for h in range(H):
        cf[:, C_ZS + h] = np.exp((127.0 - i) * lg[h])
    half = 32
    inv_freq = (np.float32(10000.0) ** (-(np.arange(half, dtype=np.float32)) / np.float32(half))).astype(np.float32)
    pos = np.arange(T, dtype=np.float32)
    ang = (pos[:, None] * inv_freq[None, :]).astype(np.float32).astype(f64)
    cos = np.cos(ang)
    sin = np.sin(ang)
    cf[:, C_COS:C_COS + 512] = cos.reshape(16, 128, 32).transpose(1, 0, 2).reshape(128, 512)
    cf[:, C_SIN:C_SIN + 512] = sin.reshape(16, 128, 32).transpose(1, 0, 2).reshape(128, 512)
    wins = [2, 4, 8, 16]
    for p in range(128):
        for c in range(2):
            w = wins[2 * c + p // 64]
            cf[p, C_RC + c] = 1.0 / w
            for t in range(16):
                cf[p, C_RCN + c * 16 + t] = 1.0 / min(t + 1, w)
    cf[:, C_EPS] = EPS
    cf[:, C_NEG:C_NEG + 32] = -1e30
    cb = np.zeros((128, NCB), np.float32)
    cb[:, B_ID:B_ID + 128] = np.eye(128)
    cb[:, B_TRI:B_TRI + 128] = (rel >= 0).astype(np.float32)
    cb[:, B_ONE:B_ONE + 128] = 1.0
    for p in range(128):
        n = p % 64
        if n < 8:
            cb[p, B_E64 + n * 128:B_E64 + (n + 1) * 128] = -30000.0
    rot = np.zeros((2, 128, T), np.float32)
    for p in range(128):
        rot[0, p, :] = cos[:, p % 32]
        rot[1, p, :] = sin[:, p % 32]
    return cf, cb, rot


def pack_params(inp):
    par = np.zeros((128, NPAR), np.float32)

    def fm(v):
        v = np.asarray(v, np.float32)
        lead = v.shape[:-1]
        c = v.shape[-1] // 128
        return np.moveaxis(v.reshape(lead + (c, 128)), -1, 0)

    par[:, P_GMIX:P_GMIX + L * 8] = fm(inp["norm_mix_g"]).reshape(128, -1)
    par[:, P_GFFN:P_GFFN + L * 8] = fm(inp["norm_ffn_g"]).reshape(128, -1)
    par[:, P_GPLE:P_GPLE + L * 8] = fm(inp["norm_ple_g"]).reshape(128, -1)
    par[:, P_GFIN:P_GFIN + 8] = fm(inp["norm_final_g"]).reshape(128, -1)
    par[:, P_CW:P_CW + L * 3 * 44] = fm(inp["conv_w"]).reshape(128, -1)
    par[:, P_CB:P_CB + L * 44] = fm(inp["conv_b"]).reshape(128, -1)
    par[:, P_PS:P_PS + L * 2] = fm(inp["pool_scale"]).reshape(128, -1)
    pw = np.asarray(inp["pool_w"], np.float32)
    bd = np.zeros((128, L, 2, 128), np.float32)
    for l in range(L):
        for c in range(2):
            for gl in range(2):
                bd[gl * 64:(gl + 1) * 64, l, c, gl * 64:(gl + 1) * 64] = pw[l, 2 * c + gl]
    return par, bd.reshape(128, L * 2 * 128)


class StopBuild(Exception):
    pass


STOP_AT = [None]
SKIP = {}


def build_program(layers, final_norm, ngrun=NG, dbg=False):
    nc = bass.Bass("TRN2", target_bir_lowering=False)
    es = ExitStack()
    S = Sched()
    NL = len(layers)

    def din(name, shape):
        return nc.dram_tensor(name, shape, F32, kind="ExternalInput").ap()

    xT_d = din("xT", [D, T])
    pT_d = din("pT", [NL, 256, T])
    w_in_d = din("w_in", [NL, D, INW])
    w_ba_d = din("w_ba", [NL, 256, D])
    w_br_d = din("w_br", [NL, 512, D])
    w_bp_d = din("w_bp", [NL, 256, D])
    w_out_d = din("w_out", [NL, D, D])
    w_up_d = din("w_up", [NL, D, 2 * FF])
    w_dn_d = din("w_dn", [NL, FF, D])
    w_pg_d = din("w_pg", [NL, D, D])
    w_pp_d = din("w_pp", [NL, 256, D])
    par_d = din("par", [128, NPAR])
    bd_d = din("bd", [128, L * 256])
    cf_d = din("cf", [128, NCF])
    cb_d = din("cb", [128, NCB])
    rot_d = din("rot", [2, 128, T])
    out_d = nc.dram_tensor("outT", [D, T], F32, kind="ExternalOutput").ap()
    dbg_outs = {}

    def sb(name, shape, dt):
        return es.enter_context(nc.sbuf_tensor(name, shape, dt))

    xT = sb("xT_sb", [128, KC * T], F32)
    xT3 = v3(xT[:], T)
    xbuf = [[Buf(f"x{c}_{g}") for g in range(NG)] for c in range(KC)]
    kaT = sb("kaT", [128, 2 * T], BF16)
    kaT3 = v3(kaT[:], T)
    kabuf = [Buf(f"ka{g}") for g in range(NG)]
    vaS = sb("vaS", [128, 16 * 256], BF16)
    va3 = v3(vaS[:], 256)
    vabuf = [Buf(f"va{g}") for g in range(NG)]
    kmT = sb("kmT", [128, 2 * 16], BF16)
    kmT3 = v3(kmT[:], 16)
    kmbuf = Buf("km")
    Rst = sb("Rst", [128, 4 * 128], F32)
    Rb = sb("Rb", [128, 4 * 128], BF16)
    Rstbuf = [Buf(f"Rst{h}") for h in range(4)]
    Rbbuf = [Buf(f"Rb{h}") for h in range(4)]
    convh = sb("convh", [128, 44 * 2], F32)
    convh3 = v3(convh[:], 2)
    convhbuf = [Buf(f"ch{i}") for i in range(44)]
    ubuf_t = sb("ubuf", [128, 2 * 528], F32)
    u3 = v3(ubuf_t[:], 528)
    ubuf = [Buf("u0"), Buf("u1")]
    wslot = [sb(f"wslot{i}", [128, 8 * 512], BF16) for i in range(NW)]
    wbuf = [Buf(f"w{i}") for i in range(NW)]
    cF = sb("cF", [128, NCF], F32)
    cFb = Buf("cF")
    cB = sb("cB", [128, NCB], BF16)
    cBb = Buf("cB")
    par = sb("par_sb", [128, NPAR], F32)
    parb = Buf("par")
    bdS = sb("bdS", [128, L * 256], BF16)
    bdb = Buf("bd")
    AFC = 4352
    ABC = 20480
    aF_t = sb("arenaF", [128, AFC], F32)
    aB_t = sb("arenaB", [128, ABC], BF16)
    aF = Arena(aF_t, AFC, "F")
    aB = Arena(aB_t, ABC, "B")
    psb = []
    for i in range(8):
        t = es.enter_context(nc.psum_tensor(f"ps{i}", [128, 512], F32))
        psb.append((t, Buf(f"ps{i}", excl=True)))
    psfree = deque(range(8))

    def psalloc():
        i = psfree.popleft()
        return i, psb[i][0], psb[i][1]

    def psrel(i):
        psfree.append(i)

    def mm(ps_ap, lhsT, rhs, start, stop, r, w):
        S.op('pe', lambda e, a=ps_ap, b=lhsT, c=rhs, s=start, t=stop: e.matmul(a, b, c, start=s, stop=t), r=r, w=[w])

    def act(out, in_, func, r, w, bias=None, scale=None):
        kw = {}
        if bias is not None:
            kw['bias'] = bias
        if scale is not None:
            kw['scale'] = scale
        S.op('act', lambda e, o=out, i=in_, f=func, kw=kw: e.activation(out=o, in_=i, func=f, **kw), r=r, w=w)

    def tt(out, in0, in1, op, r, w, eng='dve'):
        S.op(eng, lambda e, o=out, a=in0, b=in1, p=op: e.tensor_tensor(out=o, in0=a, in1=b, op=p), r=r, w=w)

    def ts(out, in0, s1, s2, op0, op1, r, w, eng='dve'):
        if op1 is None:
            S.op(eng, lambda e, o=out, a=in0, x=s1, p=op0: e.tensor_scalar(out=o, in0=a, scalar1=x, scalar2=None, op0=p), r=r, w=w)
        else:
            S.op(eng, lambda e, o=out, a=in0, x=s1, y=s2, p=op0, q=op1: e.tensor_scalar(out=o, in0=a, scalar1=x, scalar2=y, op0=p, op1=q), r=r, w=w)

    def stt(out, in0, scalar, in1, op0, op1, r, w):
        S.op('dve', lambda e, o=out, a=in0, s=scalar, b=in1, p=op0, q=op1: e.scalar_tensor_tensor(out=o, in0=a, scalar=s, in1=b, op0=p, op1=q), r=r, w=w)

    def cp(out, in_, r, w, eng='dve'):
        S.op(eng, lambda e, o=out, i=in_: e.tensor_copy(out=o, in_=i), r=r, w=w)

    def recip(out, in_, r, w):
        S.op('dve', lambda e, o=out, i=in_: e.reciprocal(out=o, in_=i), r=r, w=w)

    def memset(ap, val, w, eng='dve'):
        S.op(eng, lambda e, a=ap, v=val: e.memset(a, v), w=w)

    def dma(eng, out, in_, key, r=(), w=()):
        S.op(eng, lambda e, o=out, i=in_: e.dma_start(out=o, in_=i), r=r, w=w, key=key)

    wctr = [0]

    def wload(src, P, kc, ncols):
        i = wctr[0] % NW
        wctr[0] += 1
        view = wslot[i][0:P, 0:kc * ncols].rearrange("p (k n) -> p k n", n=ncols)
        dma('pool', view, src, f'w{i}', w=[wbuf[i]])
        return view, wbuf[i]

    def dump(name, ap, buf, shape, dt=F32):
        if not dbg:
            return
        d = nc.dram_tensor(name, shape, dt, kind="ExternalOutput").ap()
        dbg_outs[name] = d
        dma('sp', d, ap, 'dbg', r=[buf])

    dma('sp', cF[:], cf_d[:, :], 'c0', w=[cFb])
    dma('sp', par[:], par_d[:, :], 'c1', w=[parb])
    dma('pool', cB[:], cb_d[:, :], 'c2', w=[cBb])
    dma('pool', bdS[:], bd_d[:, :], 'c3', w=[bdb])
    xsrc = xT_d.rearrange("(c p) t -> p c t", p=128)
    for c in range(KC):
        dma('sp', xT3[:, c, :], xsrc[:, c, :], f'x{c}', w=[xbuf[c][g] for g in range(NG)])
    memset(kmT[:], 0.0, w=[kmbuf])
    ident = cB[:, B_ID:B_ID + 128]
    tri = cB[:, B_TRI:B_TRI + 128]
    ones = cB[:, B_ONE:B_ONE + 128]
    E64 = v3(cB[:, B_E64:B_E64 + 1024], 128)
    decT = cF[:, C_DEC:C_DEC + 512]
    xiT = v3(cF[:, C_XI:C_XI + 256], 128)
    zs = cF[:, C_ZS:C_ZS + 4]
    costm = v3(cF[:, C_COS:C_COS + 512], 32)
    sintm = v3(cF[:, C_SIN:C_SIN + 512], 32)
    epsc = cF[:, C_EPS:C_EPS + 1]
    gam = [float(np.exp(128.0 * np.log1p(-np.exp2(-5.0 - h)))) for h in range(4)]

    def rmsnorm(g, gcol):
        t0 = g * TG
        pi, ps, pb = psalloc()
        for c in range(KC):
            sq = aB.alloc(TG, "sq")
            act(sq.ap, xT3[:, c, t0:t0 + TG], AF.Square, r=[xbuf[c][g]], w=[sq.buf])
            mm(ps[:, :], ones, sq.ap, c == 0, c == KC - 1, r=[sq.buf, cBb], w=pb)
            aB.release(sq)
        rstd = aF.alloc(TG, "rstd")
        act(rstd.ap, ps[:, :], AF.Sqrt, r=[pb, cFb], w=[rstd.buf], bias=epsc, scale=1.0 / D)
        recip(rstd.ap, rstd.ap, r=[rstd.buf], w=[rstd.buf])
        psrel(pi)
        hT = aB.alloc(KC * TG, "hT")
        h3 = v3(hT.ap, TG)
        for c in range(KC):
            stt(h3[:, c, :], xT3[:, c, t0:t0 + TG], par[:, gcol + c:gcol + c + 1], rstd.ap, ALU.mult, ALU.mult,
                r=[xbuf[c][g], parb, rstd.buf], w=[hT.buf])
        aF.release(rstd)
        return hT, h3

    def ckpt(name):
        if STOP_AT[0] == name:
            raise StopBuild()

    try:
      for li, l in enumerate(layers):
          wi = w_in_d[li].rearrange("(k p) n -> p k n", p=128)
          for h in range(4):
              hb = 64 * (h % 2)
              memset(Rst[hb:hb + 64, h * 128:(h + 1) * 128], 0.0, w=[Rstbuf[h]])
              memset(Rb[hb:hb + 64, h * 128:(h + 1) * 128], 0.0, w=[Rbbuf[h]])
          for i in range(44):
              memset(convh3[:, i, :], 0.0, w=[convhbuf[i]])
          for c in range(2):
              memset(u3[:, c, 0:16], 0.0, w=[ubuf[c]])
          for g in range(ngrun):
              t0 = g * TG
              S.epoch += 1
              hT, h3 = rmsnorm(g, P_GMIX + l * 8)
              if dbg and li == 0 and g == 0:
                  dump("d_h", hT.ap, hT.buf, [128, KC * TG], BF16)
              ckpt('norm')
              wv, wb = wload(wi[:, :, 0:512], 128, 8, 512)
              qaT = aB.alloc(2 * TG, "qaT")
              qa3 = v3(qaT.ap, TG)
              for c in range(2):
                  pi, ps, pb = psalloc()
                  for k in range(KC):
                      mm(ps[:, :], wv[:, k, c * 128:(c + 1) * 128], h3[:, k, :], k == 0, k == KC - 1, r=[wb, hT.buf], w=pb)
                  act(qa3[:, c, :], ps[:, :], AF.Identity, r=[pb], w=[qaT.buf])
                  psrel(pi)
              ckpt('w0')
              for c in range(2):
                  pi, ps, pb = psalloc()
                  for k in range(KC):
                      mm(ps[:, :], wv[:, k, 256 + c * 128:256 + (c + 1) * 128], h3[:, k, :], k == 0, k == KC - 1, r=[wb, hT.buf], w=pb)
                  act(kaT3[:, c, t0:t0 + TG], ps[:, :], AF.Identity, r=[pb], w=[kabuf[g]])
                  if SKIP.get('km'):
                      psrel(pi)
                      continue
                  km = aF.alloc(2, "km")
                  S.op('dve', lambda e, o=km.ap, i=v3(kaT3[:, c, t0:t0 + TG], 256): e.tensor_reduce(out=o, in_=i, axis=AX.X, op=ALU.add),
                       r=[kabuf[g]], w=[km.buf])
                  ts(kmT3[0:64, c, 2 * g:2 * g + 2], km.ap[0:64, :], 1.0 / 256.0, None, ALU.mult, None, r=[km.buf], w=[kmbuf])
                  ts(kmT3[64:128, c, 8 + 2 * g:8 + 2 * g + 2], km.ap[64:128, :], 1.0 / 256.0, None, ALU.mult, None, r=[km.buf], w=[kmbuf])
                  aF.release(km)
                  psrel(pi)
              ckpt('ka')
              wv, wb = wload(wi[:, :, 512:768], 128, 8, 256)
              for tl in range(4):
                  pi, ps, pb = psalloc()
                  for k in range(KC):
                      mm(ps[:, 0:256], h3[:, k, tl * 128:(tl + 1) * 128], wv[:, k, :], k == 0, k == KC - 1, r=[wb, hT.buf], w=pb)
                  act(va3[:, 4 * g + tl, :], ps[:, 0:256], AF.Identity, r=[pb], w=[vabuf[g]])
                  psrel(pi)
              if dbg and li == 0 and g == 0:
                  dump("d_qa", qaT.ap, qaT.buf, [128, 2 * TG], BF16)
              ckpt('qkv')
              nbT = None
              if g >= 2 and not SKIP.get('sel'):
                  nbT = [aB.alloc(TG, "nbT0"), aB.alloc(TG, "nbT1")]
                  gi, gps, gpb = psalloc()
                  for qt in range(4):
                      for c in range(2):
                          mm(gps[:, qt * 32 + c * 16:qt * 32 + c * 16 + 16], qa3[:, c, qt * 128:(qt + 1) * 128],
                             kmT3[:, c, :], True, True, r=[qaT.buf, kmbuf], w=gpb)
                  ckpt('sel1')
                  for qt in range(4):
                      b = 2 * g + qt // 2
                      gsb = aF.alloc(32, "gsb")
                      m8 = aF.alloc(32, "m8")
                      cp(gsb.ap, cF[:, C_NEG:C_NEG + 32], r=[cFb], w=[gsb.buf])
                      cp(v3(gsb.ap, 8)[:, :, 0:b], v3(gps[:, qt * 32:(qt + 1) * 32], 8)[:, :, 0:b], r=[gpb], w=[gsb.buf])
                      for h in range(4):
                          S.op('dve', lambda e, o=m8.ap[:, h * 8:(h + 1) * 8], i=gsb.ap[:, h * 8:(h + 1) * 8]: e.max(out=o, in_=i),
                               r=[gsb.buf], w=[m8.buf])
                      negb = aB.alloc(256, "negb")
                      memset(negb.ap, 0.0, w=[negb.buf])
                      tt(negb.ap.rearrange("p (h e) -> p h e", e=64)[:, :, 0:8], v3(gsb.ap, 8),
                         v3(m8.ap, 8)[:, :, 2:3].to_broadcast([128, 4, 8]), ALU.is_lt, r=[gsb.buf, m8.buf], w=[negb.buf])
                      ckpt('sel2')
                      for tl in range(2):
                          pi, ps, pb = psalloc()
                          mm(ps[:, 0:128], negb.ap[:, tl * 128:(tl + 1) * 128], ident, True, True, r=[negb.buf, cBb], w=pb)
                          act(nbT[tl].ap[:, qt * 128:(qt + 1) * 128], ps[:, 0:128], AF.Identity, r=[pb], w=[nbT[tl].buf])
                          psrel(pi)
                      aB.release(negb)
                      aF.release(gsb)
                      aF.release(m8)
                      ckpt('sel3')
                  psrel(gi)
              attnT = aB.alloc(4 * TG, "attnT")
              at3 = v3(attnT.ap, TG)
              nkt = 4 * g + 4
              for h in range(4):
                  c, base = h // 2, 64 * (h % 2)
                  oi, ops_, opb = psalloc()
                  li_, lps, lpb = psalloc()
                  for kt in range(nkt):
                      r_ = kt - 4 * g
                      q0 = 128 * r_ if r_ > 0 else 0
                      n = kt // 2
                      bias_c0 = None
                      if g >= 2 and not SKIP.get('sel') and not SKIP.get('bias'):
                          if n < 2 * g:
                              bias_c0 = 0
                          elif n == 2 * g:
                              bias_c0 = 256
                      pi, ps, pb = psalloc()
                      mm(ps[:, q0:TG], kaT3[base:base + 64, c, kt * 128:(kt + 1) * 128], qa3[base:base + 64, c, q0:TG],
                         True, bias_c0 is None, r=[kabuf[kt // 4], qaT.buf], w=pb)
                      if bias_c0 is not None:
                          tl, sl = h // 2, 64 * (h % 2)
                          c0 = max(bias_c0, q0)
                          mm(ps[:, c0:TG], E64[sl:sl + 64, n, :], nbT[tl].ap[sl:sl + 64, c0:TG], False, True,
                             r=[cBb, nbT[tl].buf], w=pb)
                      pt = aB.alloc(TG, "pt")
                      act(pt.ap[:, q0:TG], ps[:, q0:TG], AF.Exp, r=[pb], w=[pt.buf], scale=0.125)
                      psrel(pi)
                      if r_ >= 0:
                          tt(pt.ap[:, q0:q0 + 128], pt.ap[:, q0:q0 + 128], tri, ALU.mult, r=[pt.buf, cBb], w=[pt.buf])
                      mm(ops_[0:64, q0:TG], va3[:, kt, h * 64:(h + 1) * 64], pt.ap[:, q0:TG], kt == 0, kt == nkt - 1,
                         r=[vabuf[kt // 4], pt.buf], w=opb)
                      mm(lps[0:64, q0:TG], ones[:, 0:64], pt.ap[:, q0:TG], kt == 0, kt == nkt - 1, r=[cBb, pt.buf], w=lpb)
                      aB.release(pt)
                  rc_ = aF.alloc(TG, "rcp")
                  recip(rc_.ap[0:64, :], lps[0:64, :], r=[lpb], w=[rc_.buf])
                  tt(at3[0:64, h, :], ops_[0:64, :], rc_.ap[0:64, :], ALU.mult, r=[opb, rc_.buf], w=[attnT.buf])
                  aF.release(rc_)
                  psrel(oi)
                  psrel(li_)
              aB.release(qaT)
              if nbT is not None:
                  aB.release(nbT[0])
                  aB.release(nbT[1])
              if dbg and li == 0 and g == 0:
                  dump("d_attn", attnT.ap, attnT.buf, [128, 4 * TG], BF16)
              ckpt('attn')
              wv, wb = wload(wi[:, :, 768:1280], 128, 8, 512)
              wrot = aB.alloc(8 * 512, "wrot")
              wr3 = v3(wrot.ap, 512)
              w5 = wv.rearrange("p k (h t f) -> p k h t f", t=2, f=32)
              r5 = wr3.rearrange("p k (h t f) -> p k h t f", t=2, f=32)
              for k in range(KC):
                  S.op('act', lambda e, o=r5[:, k, :, 0, :], i=w5[:, k, :, 1, :]: e.mul(out=o, in_=i, mul=-1.0), r=[wb], w=[wrot.buf])
                  cp(r5[:, k, :, 1, :], w5[:, k, :, 0, :], r=[wb], w=[wrot.buf])
              rt = aF.alloc(2 * TG, "rot")
              rt3 = v3(rt.ap, TG)
              dma('sp', rt3[:, 0, :], rot_d[0, :, t0:t0 + TG], 'rot', w=[rt.buf])
              dma('sp', rt3[:, 1, :], rot_d[1, :, t0:t0 + TG], 'rot', w=[rt.buf])
              qk = aB.alloc(4 * TG, "qk")
              qk3 = v3(qk.ap, TG)
              qx = aB.alloc(2 * TG, "qx")
              qx3 = v3(qx.ap, TG)
              for c4 in range(4):
                  pi, ps, pb = psalloc()
                  pj, ps2, pb2 = psalloc()
                  for k in range(KC):
                      mm(ps[:, :], wv[:, k, c4 * 128:(c4 + 1) * 128], h3[:, k, :], k == 0, k == KC - 1, r=[wb, hT.buf], w=pb)
                  for k in range(KC):
                      mm(ps2[:, :], wr3[:, k, c4 * 128:(c4 + 1) * 128], h3[:, k, :], k == 0, k == KC - 1, r=[wrot.buf, hT.buf], w=pb2)
                  t1 = aF.alloc(TG, "t1")
                  t2 = aF.alloc(TG, "t2")
                  tt(t1.ap, ps[:, :], rt3[:, 0, :], ALU.mult, r=[pb, rt.buf], w=[t1.buf])
                  tt(t2.ap, ps2[:, :], rt3[:, 1, :], ALU.mult, r=[pb2, rt.buf], w=[t2.buf])
                  psrel(pi)
                  psrel(pj)
                  tt(qk3[:, c4, :], t1.ap, t2.ap, ALU.add, r=[t1.buf, t2.buf], w=[qk.buf])
                  if c4 < 2:
                      tt(t1.ap, t1.ap, t2.ap, ALU.add, r=[t1.buf, t2.buf], w=[t1.buf])
                      tt(v3(qx3[:, c4, :], 128), v3(t1.ap, 128), xiT[:, c4, :].unsqueeze(1).to_broadcast([128, 4, 128]), ALU.mult,
                         r=[t1.buf, cFb], w=[qx.buf])
                  aF.release(t1)
                  aF.release(t2)
              aF.release(rt)
              kt_t = aB.alloc(4 * 256, "ktm")
              ktm3 = v3(kt_t.ap, 256)
              for tl in range(4):
                  pi, ps, pb = psalloc()
                  for k in range(KC):
                      mm(ps[:, 0:256], h3[:, k, tl * 128:(tl + 1) * 128], wv[:, k, 256:512], k == 0, k == KC - 1, r=[wb, hT.buf], w=pb)
                  for k in range(KC):
                      mm(ps[:, 256:512], h3[:, k, tl * 128:(tl + 1) * 128], wr3[:, k, 256:512], k == 0, k == KC - 1, r=[wrot.buf, hT.buf], w=pb)
                  t1 = aF.alloc(256, "k1")
                  t2 = aF.alloc(256, "k2")
                  cb_ = costm[:, 4 * g + tl, :].unsqueeze(1).to_broadcast([128, 8, 32])
                  sb_ = sintm[:, 4 * g + tl, :].unsqueeze(1).to_broadcast([128, 8, 32])
                  tt(v3(t1.ap, 32), v3(ps[:, 0:256], 32), cb_, ALU.mult, r=[pb, cFb], w=[t1.buf])
                  tt(v3(t2.ap, 32), v3(ps[:, 256:512], 32), sb_, ALU.mult, r=[pb, cFb], w=[t2.buf])
                  psrel(pi)
                  tt(t1.ap, t1.ap, t2.ap, ALU.add, r=[t1.buf, t2.buf], w=[t1.buf])
                  tt(v3(ktm3[:, tl, :], 64), v3(t1.ap, 64), zs.unsqueeze(2).to_broadcast([128, 4, 64]), ALU.mult,
                     r=[t1.buf, cFb], w=[kt_t.buf])
                  aF.release(t1)
                  aF.release(t2)
              aB.release(wrot)
              wv, wb = wload(wi[:, :, 1280:1792], 128, 8, 512)
              vr = aB.alloc(4 * 512, "vr")
              vr3 = v3(vr.ap, 512)
              for tl in range(4):
                  pi, ps, pb = psalloc()
                  for k in range(KC):
                      mm(ps[:, :], h3[:, k, tl * 128:(tl + 1) * 128], wv[:, k, :], k == 0, k == KC - 1, r=[wb, hT.buf], w=pb)
                  act(vr3[:, tl, :], ps[:, :], AF.Identity, r=[pb], w=[vr.buf])
                  psrel(pi)
              wv, wb = wload(wi[:, :, 1792:2304], 128, 8, 512)
              sg = aB.alloc(4 * TG, "silug")
              sg3 = v3(sg.ap, TG)
              for c in range(4):
                  pi, ps, pb = psalloc()
                  for k in range(KC):
                      mm(ps[:, :], wv[:, k, c * 128:(c + 1) * 128], h3[:, k, :], k == 0, k == KC - 1, r=[wb, hT.buf], w=pb)
                  act(sg3[:, c, :], ps[:, :], AF.Silu, r=[pb], w=[sg.buf])
                  psrel(pi)
              retT = aB.alloc(4 * TG, "retT")
              rT3 = v3(retT.ap, TG)
              for h in range(4):
                  c, base = h // 2, 64 * (h % 2)
                  si, sps, spb = psalloc()
                  for r_ in range(4):
                      mm(sps[:, r_ * 128:(r_ + 1) * 128], qk3[base:base + 64, 2 + c, r_ * 128:(r_ + 1) * 128],
                         qk3[base:base + 64, c, r_ * 128:(r_ + 1) * 128], True, True, r=[qk.buf], w=spb)
                  sc = aB.alloc(TG, "sc")
                  tt(v3(sc.ap, 128), v3(sps[:, :], 128), decT[:, h * 128:(h + 1) * 128].unsqueeze(1).to_broadcast([128, 4, 128]),
                     ALU.mult, r=[spb, cFb], w=[sc.buf])
                  psrel(si)
                  yi, yps, ypb = psalloc()
                  for r_ in range(4):
                      cs = slice(r_ * 128, (r_ + 1) * 128)
                      mm(yps[:, cs], vr3[:, r_, h * 128:(h + 1) * 128], sc.ap[:, cs], True, False, r=[vr.buf, sc.buf], w=ypb)
                      mm(yps[:, cs], Rb[base:base + 64, h * 128:(h + 1) * 128], qx3[base:base + 64, c, cs], False, True,
                         r=[Rbbuf[h], qx.buf], w=ypb)
                      ui, ups, upb = psalloc()
                      mm(ups[:, 0:128], ktm3[:, r_, c * 128:(c + 1) * 128], vr3[:, r_, h * 128:(h + 1) * 128], True, True,
                         r=[kt_t.buf, vr.buf], w=upb)
                      stt(Rst[base:base + 64, h * 128:(h + 1) * 128], Rst[base:base + 64, h * 128:(h + 1) * 128], gam[h],
                          ups[base:base + 64, 0:128], ALU.mult, ALU.add, r=[Rstbuf[h], upb], w=[Rstbuf[h]])
                      psrel(ui)
                      cp(Rb[base:base + 64, h * 128:(h + 1) * 128], Rst[base:base + 64, h * 128:(h + 1) * 128],
                         r=[Rstbuf[h]], w=[Rbbuf[h]])
                  aB.release(sc)
                  yb = aB.alloc(TG, "yb")
                  ysq = aB.alloc(TG, "ysq")
                  act(yb.ap, yps[:, :], AF.Identity, r=[ypb], w=[yb.buf])
                  act(ysq.ap, yps[:, :], AF.Square, r=[ypb], w=[ysq.buf])
                  s1i, s1, s1b = psalloc()
                  s2i, s2, s2b = psalloc()
                  mm(s1[:, :], ones, yb.ap, True, True, r=[cBb, yb.buf], w=s1b)
                  mm(s2[:, :], ones, ysq.ap, True, True, r=[cBb, ysq.buf], w=s2b)
                  aB.release(yb)
                  aB.release(ysq)
                  mean = aF.alloc(TG, "mean")
                  var = aF.alloc(TG, "var")
                  ts(mean.ap, s1[:, :], 1.0 / 128.0, None, ALU.mult, None, r=[s1b], w=[mean.buf])
                  tt(var.ap, mean.ap, mean.ap, ALU.mult, r=[mean.buf], w=[var.buf])
                  stt(var.ap, s2[:, :], 1.0 / 128.0, var.ap, ALU.mult, ALU.subtract, r=[s2b, var.buf], w=[var.buf])
                  psrel(s1i)
                  psrel(s2i)
                  ts(var.ap, var.ap, 0.0, None, ALU.max, None, r=[var.buf], w=[var.buf])
                  act(var.ap, var.ap, AF.Sqrt, r=[var.buf, cFb], w=[var.buf], bias=epsc, scale=1.0)
                  recip(var.ap, var.ap, r=[var.buf], w=[var.buf])
                  tt(mean.ap, yps[:, :], mean.ap, ALU.subtract, r=[ypb, mean.buf], w=[mean.buf])
                  psrel(yi)
                  tt(mean.ap, mean.ap, var.ap, ALU.mult, r=[mean.buf, var.buf], w=[mean.buf])
                  tt(rT3[:, h, :], mean.ap, sg3[:, h, :], ALU.mult, r=[mean.buf, sg.buf], w=[retT.buf])
                  aF.release(mean)
                  aF.release(var)
              for t_ in (qk, qx, kt_t, vr, sg):
                  aB.release(t_)
              if dbg and li == 0 and g == 0:
                  dump("d_ret", retT.ap, retT.buf, [128, 4 * TG], BF16)
              ckpt('ret')
              wv, wb = wload(wi[:, :, 2304:2560], 128, 8, 256)
              mixed = aB.alloc(2 * TG, "mixed")
              mx3 = v3(mixed.ap, TG)
              for c in range(2):
                  pi, ps, pb = psalloc()
                  for k in range(KC):
                      mm(ps[:, :], wv[:, k, c * 128:(c + 1) * 128], h3[:, k, :], k == 0, k == KC - 1, r=[wb, hT.buf], w=pb)
                  act(u3[:, c, 16:528], ps[:, :], AF.Identity, r=[pb], w=[ubuf[c]])
                  psrel(pi)
                  sa = aF.alloc(528, "sa")
                  sbb = aF.alloc(528, "sb")
                  tt(sa.ap[:, 1:528], u3[:, c, 1:528], u3[:, c, 0:527], ALU.add, r=[ubuf[c]], w=[sa.buf])
                  if c == 0:
                      tt(sbb.ap[64:128, 3:528], sa.ap[64:128, 3:528], sa.ap[64:128, 1:526], ALU.add, r=[sa.buf], w=[sbb.buf])
                      lo_src, hi_src = sa, sbb
                  else:
                      tt(sbb.ap[:, 3:528], sa.ap[:, 3:528], sa.ap[:, 1:526], ALU.add, r=[sa.buf], w=[sbb.buf])
                      tt(sa.ap[:, 7:528], sbb.ap[:, 7:528], sbb.ap[:, 3:524], ALU.add, r=[sbb.buf], w=[sa.buf])
                      tt(sbb.ap[64:128, 15:528], sa.ap[64:128, 15:528], sa.ap[64:128, 7:520], ALU.add, r=[sa.buf], w=[sbb.buf])
                      lo_src, hi_src = sa, sbb
                  rcs = cF[:, C_RC + c:C_RC + c + 1]
                  for (p0, p1, src) in ((0, 64, lo_src), (64, 128, hi_src)):
                      stt(mx3[p0:p1, c, :], src.ap[p0:p1, 16:528], rcs[p0:p1, :], u3[p0:p1, c, 16:528], ALU.mult, ALU.subtract,
                          r=[src.buf, ubuf[c], cFb], w=[mixed.buf])
                      if g == 0:
                          tf = aF.alloc(16, "tf")
                          tt(tf.ap[p0:p1, :], src.ap[p0:p1, 16:32], cF[p0:p1, C_RCN + c * 16:C_RCN + (c + 1) * 16], ALU.mult,
                             r=[src.buf, cFb], w=[tf.buf])
                          tt(mx3[p0:p1, c, 0:15], tf.ap[p0:p1, 0:15], u3[p0:p1, c, 16:31], ALU.subtract, r=[tf.buf, ubuf[c]], w=[mixed.buf])
                          aF.release(tf)
                  aF.release(sa)
                  aF.release(sbb)
                  th = aF.alloc(16, "th")
                  cp(th.ap, u3[:, c, 512:528], r=[ubuf[c]], w=[th.buf])
                  cp(u3[:, c, 0:16], th.ap, r=[th.buf], w=[ubuf[c]])
                  aF.release(th)
              poolT = aB.alloc(2 * TG, "poolT")
              pl3 = v3(poolT.ap, TG)
              for c in range(2):
                  pi, ps, pb = psalloc()
                  mm(ps[:, :], bdS[:, (l * 2 + c) * 128:(l * 2 + c + 1) * 128], mx3[:, c, :], True, True, r=[bdb, mixed.buf], w=pb)
                  act(pl3[:, c, :], ps[:, :], AF.Identity, r=[pb, parb], w=[poolT.buf], scale=par[:, P_PS + l * 2 + c:P_PS + l * 2 + c + 1])
                  psrel(pi)
              aB.release(mixed)
              if dbg and li == 0 and g == 0:
                  dump("d_pool", poolT.ap, poolT.buf, [128, 2 * TG], BF16)
              ckpt('pool')
              merged = aB.alloc(KC * TG, "merged")
              mg3 = v3(merged.ap, TG)
              for ct in range(2):
                  macc = aF.alloc(4 * TG, "macc")
                  ma3 = v3(macc.ap, TG)
                  for br in range(3):
                      gv, gb = wload(wi[:, :, 2560 + br * 1024 + ct * 512:2560 + br * 1024 + (ct + 1) * 512], 128, 8, 512)
                      if br == 0:
                          src = w_ba_d[li].rearrange("(h p) n -> p h n", p=64)[:, :, ct * 512:(ct + 1) * 512]
                          bv, bb = wload(src, 64, 4, 512)
                          nk, bin_, bbuf, bp = 4, at3, attnT.buf, 64
                      elif br == 1:
                          src = w_br_d[li].rearrange("(h p) n -> p h n", p=128)[:, :, ct * 512:(ct + 1) * 512]
                          bv, bb = wload(src, 128, 4, 512)
                          nk, bin_, bbuf, bp = 4, rT3, retT.buf, 128
                      else:
                          src = w_bp_d[li].rearrange("(h p) n -> p h n", p=128)[:, :, ct * 512:(ct + 1) * 512]
                          bv, bb = wload(src, 128, 2, 512)
                          nk, bin_, bbuf, bp = 2, pl3, poolT.buf, 128
                      for j in range(4):
                          dc = ct * 4 + j
                          gi, gps, gpb = psalloc()
                          for k in range(KC):
                              mm(gps[:, :], gv[:, k, j * 128:(j + 1) * 128], h3[:, k, :], k == 0, k == KC - 1, r=[gb, hT.buf], w=gpb)
                          bi, bps, bpb = psalloc()
                          for k in range(nk):
                              mm(bps[:, :], bv[0:bp, k, j * 128:(j + 1) * 128], bin_[0:bp, k, :], k == 0, k == nk - 1, r=[bb, bbuf], w=bpb)
                          sig = aF.alloc(TG, "sig")
                          act(sig.ap, gps[:, :], AF.Sigmoid, r=[gpb], w=[sig.buf])
                          psrel(gi)
                          if br == 0:
                              tt(ma3[:, j, :], sig.ap, bps[:, :], ALU.mult, r=[sig.buf, bpb], w=[macc.buf])
                          else:
                              tt(sig.ap, sig.ap, bps[:, :], ALU.mult, r=[sig.buf, bpb], w=[sig.buf])
                              if br == 1:
                                  tt(ma3[:, j, :], ma3[:, j, :], sig.ap, ALU.add, r=[sig.buf, macc.buf], w=[macc.buf])
                              else:
                                  tt(mg3[:, dc, :], ma3[:, j, :], sig.ap, ALU.add, r=[sig.buf, macc.buf], w=[merged.buf])
                          psrel(bi)
                          aF.release(sig)
                  aF.release(macc)
              for t_ in (attnT, retT, poolT, hT):
                  aB.release(t_)
              if dbg and li == 0 and g == 0:
                  dump("d_merged", merged.ap, merged.buf, [128, KC * TG], BF16)
              wo = w_out_d[li].rearrange("(k p) n -> p k n", p=128)
              for ct in range(2):
                  wv, wb = wload(wo[:, :, ct * 512:(ct + 1) * 512], 128, 8, 512)
                  for j in range(4):
                      dc = ct * 4 + j
                      pi, ps, pb = psalloc()
                      for k in range(KC):
                          mm(ps[:, :], wv[:, k, j * 128:(j + 1) * 128], mg3[:, k, :], k == 0, k == KC - 1, r=[wb, merged.buf], w=pb)
                      tt(xT3[:, dc, t0:t0 + TG], xT3[:, dc, t0:t0 + TG], ps[:, :], ALU.add, r=[xbuf[dc][g], pb], w=[xbuf[dc][g]])
                      psrel(pi)
              aB.release(merged)
              ckpt('mix')
              hT, h3 = rmsnorm(g, P_GFFN + l * 8)
              wu = w_up_d[li].rearrange("(k p) n -> p k n", p=128)
              mT = aB.alloc(22 * TG, "mT")
              m3 = v3(mT.ap, TG)
              mbuf = [Buf(f"m{i}") for i in range(22)]

              def ffn_final(items):
                  for (ch_, y_) in items:
                      if ch_ < 22:
                          act(m3[:, ch_, :], y_.ap, AF.Gelu_apprx_tanh, r=[y_.buf], w=[mbuf[ch_]])
                      else:
                          tt(m3[:, ch_ - 22, :], m3[:, ch_ - 22, :], y_.ap, ALU.mult, r=[y_.buf, mbuf[ch_ - 22]], w=[mbuf[ch_ - 22]],
                             eng='pool')
                      aF.release(y_)

              prev = []
              for it in range(22):
                  if it % 2 == 0:
                      wv, wb = wload(wu[:, :, (it // 2) * 512:(it // 2 + 1) * 512], 128, 8, 512)
                  chs = [2 * it, 2 * it + 1]
                  pss = []
                  for ch in chs:
                      j = ch % 4
                      pi, ps, pb = psalloc()
                      for k in range(KC):
                          mm(ps[:, :], wv[:, k, j * 128:(j + 1) * 128], h3[:, k, :], k == 0, k == KC - 1, r=[wb, hT.buf], w=pb)
                      pss.append((pi, ps, pb))
                  Us = []
                  for ch, (pi, ps, pb) in zip(chs, pss):
                      U = aF.alloc(514, "U", rot=True)
                      act(U.ap[:, 2:514], ps[:, :], AF.Identity, r=[pb], w=[U.buf])
                      psrel(pi)
                      Us.append(U)
                  ffn_final(prev)
                  for ch, U in zip(chs, Us):
                      cp(U.ap[:, 0:2], convh3[:, ch, :], r=[convhbuf[ch]], w=[U.buf], eng='pool')
                      cp(convh3[:, ch, :], U.ap[:, 512:514], r=[U.buf], w=[convhbuf[ch]], eng='pool')
                  cw = lambda kk, ch: par[:, P_CW + (l * 3 + kk) * 44 + ch:P_CW + (l * 3 + kk) * 44 + ch + 1]
                  ys = []
                  for ch, U in zip(chs, Us):
                      y = aF.alloc(TG, "y", rot=True)
                      cbias = par[:, P_CB + l * 44 + ch:P_CB + l * 44 + ch + 1]
                      act(y.ap, U.ap[:, 2:514], AF.Identity, r=[U.buf, parb], w=[y.buf], bias=cbias, scale=cw(2, ch))
                      ys.append(y)
                  for ch, U, y in zip(chs, Us, ys):
                      stt(y.ap, U.ap[:, 1:513], cw(1, ch), y.ap, ALU.mult, ALU.add, r=[U.buf, y.buf, parb], w=[y.buf])
                  for ch, U, y in zip(chs, Us, ys):
                      stt(y.ap, U.ap[:, 0:512], cw(0, ch), y.ap, ALU.mult, ALU.add, r=[U.buf, y.buf, parb], w=[y.buf])
                  for U in Us:
                      aF.release(U)
                  prev = list(zip(chs, ys))
              ffn_final(prev)
              aB.release(hT)
              wd = w_dn_d[li].rearrange("(k p) n -> p k n", p=128)
              for ct in range(2):
                  accs = [psalloc() for _ in range(4)]
                  for kg, (k0, k1) in enumerate(((0, 8), (8, 16), (16, 22))):
                      wv, wb = wload(wd[:, k0:k1, ct * 512:(ct + 1) * 512], 128, k1 - k0, 512)
                      for j in range(4):
                          for kk in range(k1 - k0):
                              mm(accs[j][1][:, :], wv[:, kk, j * 128:(j + 1) * 128], m3[:, k0 + kk, :], k0 + kk == 0, k0 + kk == 21,
                                 r=[wb, mbuf[k0 + kk]], w=accs[j][2])
                  for j in range(4):
                      dc = ct * 4 + j
                      tt(xT3[:, dc, t0:t0 + TG], xT3[:, dc, t0:t0 + TG], accs[j][1][:, :], ALU.add, r=[xbuf[dc][g], accs[j][2]], w=[xbuf[dc][g]])
                      psrel(accs[j][0])
              for b_ in mbuf:
                  for e_, o_ in b_.r.items():
                      p_ = mT.buf.r.get(e_)
                      if p_ is None or p_.idx < o_.idx:
                          mT.buf.r[e_] = o_
                  if b_.w is not None:
                      p_ = mT.buf.r.get(b_.w.eng)
                      if p_ is None or p_.idx < b_.w.idx:
                          mT.buf.r[b_.w.eng] = b_.w
              aB.release(mT)
              ckpt('ffn')
              hT, h3 = rmsnorm(g, P_GPLE + l * 8)
              pt_ = aB.alloc(2 * TG, "pT")
              p3 = v3(pt_.ap, TG)
              dma('pool', p3, pT_d[li].rearrange("(c p) t -> p c t", p=128)[:, :, t0:t0 + TG], 'pT', w=[pt_.buf])
              wg = w_pg_d[li].rearrange("(k p) n -> p k n", p=128)
              wp = w_pp_d[li].rearrange("(k p) n -> p k n", p=128)
              for ct in range(2):
                  gv, gb = wload(wg[:, :, ct * 512:(ct + 1) * 512], 128, 8, 512)
                  pv, pbb = wload(wp[:, :, ct * 512:(ct + 1) * 512], 128, 2, 512)
                  for j in range(4):
                      dc = ct * 4 + j
                      gi, gps, gpb = psalloc()
                      for k in range(KC):
                          mm(gps[:, :], gv[:, k, j * 128:(j + 1) * 128], h3[:, k, :], k == 0, k == KC - 1, r=[gb, hT.buf], w=gpb)
                      bi, bps, bpb = psalloc()
                      for k in range(2):
                          mm(bps[:, :], pv[:, k, j * 128:(j + 1) * 128], p3[:, k, :], k == 0, k == 1, r=[pbb, pt_.buf], w=bpb)
                      sig = aF.alloc(TG, "sig")
                      act(sig.ap, gps[:, :], AF.Sigmoid, r=[gpb], w=[sig.buf])
                      psrel(gi)
                      tt(sig.ap, sig.ap, bps[:, :], ALU.mult, r=[sig.buf, bpb], w=[sig.buf])
                      psrel(bi)
                      tt(xT3[:, dc, t0:t0 + TG], xT3[:, dc, t0:t0 + TG], sig.ap, ALU.add, r=[xbuf[dc][g], sig.buf], w=[xbuf[dc][g]])
                      aF.release(sig)
              aB.release(pt_)
              aB.release(hT)
    except StopBuild:
        pass
    S.epoch += 1
    osrc = out_d.rearrange("(c p) t -> p c t", p=128)
    for g in range(NG):
        t0 = g * TG
        if final_norm and g < ngrun:
            pi, ps, pb = psalloc()
            for c in range(KC):
                sq = aB.alloc(TG, "sq")
                act(sq.ap, xT3[:, c, t0:t0 + TG], AF.Square, r=[xbuf[c][g]], w=[sq.buf])
                mm(ps[:, :], ones, sq.ap, c == 0, c == KC - 1, r=[sq.buf, cBb], w=pb)
                aB.release(sq)
            rstd = aF.alloc(TG, "rstd")
            act(rstd.ap, ps[:, :], AF.Sqrt, r=[pb, cFb], w=[rstd.buf], bias=epsc, scale=1.0 / D)
            recip(rstd.ap, rstd.ap, r=[rstd.buf], w=[rstd.buf])
            psrel(pi)
            for c in range(KC):
                o = aF.alloc(TG, "o")
                stt(o.ap, xT3[:, c, t0:t0 + TG], par[:, P_GFIN + c:P_GFIN + c + 1], rstd.ap, ALU.mult, ALU.mult,
                    r=[xbuf[c][g], parb, rstd.buf], w=[o.buf])
                dma('sp', osrc[:, c, t0:t0 + TG], o.ap, 'out', r=[o.buf])
                aF.release(o)
            aF.release(rstd)
        else:
            for c in range(KC):
                dma('sp', osrc[:, c, t0:t0 + TG], xT3[:, c, t0:t0 + TG], 'out', r=[xbuf[c][g]])
    S.emit(nc, es)
    es.close()
    stats = dict(n_ops={e: len(S.q[e]) for e in ENGS}, nwaits=S.nwaits, aF_peak=aF.peak, aB_peak=aB.peak)
    return nc, stats, dbg_outs


_CONSTS = None


def prep_inputs(inp, layers):
    global _CONSTS
    if _CONSTS is None:
        _CONSTS = make_consts()
    cf, cb, rot = _CONSTS
    par, bd = pack_params(inp)
    ls = list(layers)
    f = lambda k: np.ascontiguousarray(np.asarray(inp[k], np.float32)[ls])
    shared = dict(
        w_in=f("w_in"), w_ba=f("w_branch_attn"), w_br=f("w_branch_ret"), w_bp=f("w_branch_pool"),
        w_out=f("w_out"), w_up=f("w_up"), w_dn=f("w_down"), w_pg=f("w_ple_gate"), w_pp=f("w_ple_proj"),
        par=par, bd=bd, cf=cf, cb=cb, rot=rot,
    )
    return shared


def run_layers(xT_all, inp, layers, final_norm, ngrun=NG, dbg=False, ncores=8):
    nc, stats, dbg_outs = build_program(layers, final_norm, ngrun, dbg)
    shared = prep_inputs(inp, layers)
    p = np.asarray(inp["p"], np.float32)
    in_maps = []
    for b in range(ncores):
        m = dict(shared)
        m["xT"] = np.ascontiguousarray(xT_all[b])
        m["pT"] = np.ascontiguousarray(p[list(layers), b].transpose(0, 2, 1))
        in_maps.append(m)
    res = run_bass_kernel_spmd(nc, in_maps, core_ids=list(range(ncores)))
    outs = np.stack([np.asarray(r["outT"]) for r in res.results], axis=0)
    return outs, res, stats


def kernel(**inputs):
    x = np.asarray(inputs["x"], np.float32)
    xT = np.ascontiguousarray(x.transpose(0, 2, 1))
    outT, _, _ = run_layers(xT, inputs, range(L), True)
    return np.ascontiguousarray(outT.transpose(0, 2, 1)).astype(np.float32)
```

```python
import numpy as np
from collections import deque
from contextlib import ExitStack
import concourse.bass as bass
import concourse.mybir as mybir
from concourse.bass_utils import run_bass_kernel_spmd

F32 = mybir.dt.float32
BF16 = mybir.dt.bfloat16
AF = mybir.ActivationFunctionType
ALU = mybir.AluOpType
AX = mybir.AxisListType

ENGS = ('pe', 'act', 'dve', 'pool', 'sp')

T = 2048
D = 1024
L = 4
TG = 512
NG = T // TG
KC = D // 128
FF = 2816
INW = 5632
EPS = 1e-6
NW = 4


class Buf:
    __slots__ = ('name', 'w', 'r', 'rd', 'excl')

    def __init__(self, name='', excl=False):
        self.name = name
        self.excl = excl
        self.w = None
        self.r = {}
        self.rd = []


class Op:
    __slots__ = ('eng', 'fn', 'deps', 'idx', 'inc', 'semval', 'dma_key', 'dma_val', 'dwait', 'ep')


class Sched:
    def __init__(self):
        self.q = {e: [] for e in ENGS}
        self.dma_cnt = {}
        self.epoch = 0

    def _track(self, op, reads, writes):
        deps = []
        for b in reads:
            if b.w is not None:
                deps.append(b.w)
            if b.excl:
                for e_, o_ in b.r.items():
                    if e_ != op.eng:
                        deps.append(o_)
        for b in writes:
            if b.w is not None:
                deps.append(b.w)
            deps.extend(b.r.values())
            deps.extend(b.rd)
        for b in reads:
            if op.dma_key is not None:
                b.rd.append(op)
            else:
                b.r[op.eng] = op
        for b in writes:
            b.w = op
            b.r = {}
            b.rd = []
        return deps

    def op(self, eng, fn, r=(), w=(), key=None):
        o = Op()
        o.eng = eng
        o.ep = self.epoch
        o.fn = fn
        o.inc = False
        o.semval = 0
        o.dma_key = key
        o.dma_val = 0
        if key is not None:
            self.dma_cnt[key] = self.dma_cnt.get(key, 0) + 1
            o.dma_val = 16 * self.dma_cnt[key]
        o.deps = self._track(o, r, w)
        o.dwait = {}
        for d in o.deps:
            if d.dma_key is not None:
                o.dwait[d.dma_key] = 16 * self.dma_cnt[d.dma_key] - (16 if d.dma_key == key else 0)
        o.idx = len(self.q[eng])
        self.q[eng].append(o)
        return o

    def emit(self, nc, es):
        for e in ENGS:
            for o in self.q[e]:
                for d in o.deps:
                    if d.dma_key is None and not (d.eng == o.eng == 'pe'):
                        d.inc = True
        semh = {}
        for e in ENGS:
            cnt = {}
            for o in self.q[e]:
                if o.dma_key is None and o.inc:
                    cnt[o.ep] = cnt.get(o.ep, 0) + 1
                    o.semval = cnt[o.ep]
                    if ('eng', e, o.ep) not in semh:
                        semh[('eng', e, o.ep)] = es.enter_context(nc.semaphore(f's_{e}_{o.ep}'))
        for k in self.dma_cnt:
            semh[('dma', k)] = es.enter_context(nc.semaphore('d_' + str(k)))
        self.nwaits = 0

        def run(eobj, ename):
            seen = {}
            for o in self.q[ename]:
                waits = {}
                for d in o.deps:
                    if d.dma_key is not None:
                        k = ('dma', d.dma_key)
                        v = o.dwait[d.dma_key]
                    else:
                        if d.eng == ename == 'pe':
                            continue
                        k = ('eng', d.eng, d.ep)
                        v = d.semval
                    if v > waits.get(k, 0):
                        waits[k] = v
                for k, v in waits.items():
                    if seen.get(k, 0) >= v:
                        continue
                    seen[k] = v
                    eobj.wait_ge(semh[k], v)
                    self.nwaits += 1
                ins = o.fn(eobj)
                if o.dma_key is not None:
                    ins.then_inc(semh[('dma', o.dma_key)], 16)
                elif o.inc:
                    ins.then_inc(semh[('eng', ename, o.ep)], 1)
            last = {}
            for o in self.q[ename]:
                if o.dma_key is not None:
                    last[o.dma_key] = max(last.get(o.dma_key, 0), o.dma_val)
            for k, v in last.items():
                if seen.get(('dma', k), 0) < v:
                    eobj.wait_ge(semh[('dma', k)], v)

        with nc.Block() as block:
            @block.tensor
            def _(e):
                run(e, 'pe')

            @block.scalar
            def _(e):
                run(e, 'act')

            @block.vector
            def _(e):
                run(e, 'dve')

            @block.gpsimd
            def _(e):
                run(e, 'pool')

            @block.sync
            def _(e):
                run(e, 'sp')


class Tile:
    __slots__ = ('ap', 'buf', 'lo', 'hi', 'arena')


class Arena:
    def __init__(self, tensor, ncols, name):
        self.t = tensor
        self.n = ncols
        self.free = [(0, ncols)]
        self.ghosts = []
        self.name = name
        self.peak = 0
        self.used = 0
        self.rover = 0

    def alloc(self, n, name='', rot=None):
        n0 = n
        n = (n + 31) // 32 * 32
        if rot is None:
            rot = n <= 640
        cand = [(lo, hi) for (lo, hi) in self.free if hi - lo >= n]
        pick = None
        for (lo, hi) in (cand if rot else []):
            if hi > self.rover and hi - max(lo, self.rover) >= n:
                pick = (lo, hi, max(lo, self.rover))
                break
        if pick is None and cand:
            pick = (cand[0][0], cand[0][1], cand[0][0])
        if pick is not None:
            flo, fhi, lo = pick
            self.free.remove((flo, fhi))
            if lo > flo:
                self.free.append((flo, lo))
            if lo + n < fhi:
                self.free.append((lo + n, fhi))
            self.free.sort()
            if rot:
                self.rover = lo + n
            for _ in (0,):
                t = Tile()
                t.lo, t.hi, t.arena = lo, lo + n, self
                t.buf = Buf(name)
                t.ap = self.t[:, lo:lo + n0]
                keep = []
                for (glo, ghi, ops) in self.ghosts:
                    if glo < t.hi and ghi > t.lo:
                        for o in ops:
                            if o.dma_key is not None:
                                t.buf.rd.append(o)
                            else:
                                p = t.buf.r.get(o.eng)
                                if p is None or p.idx < o.idx:
                                    t.buf.r[o.eng] = o
                        if glo >= t.lo and ghi <= t.hi:
                            continue
                    keep.append((glo, ghi, ops))
                self.ghosts = keep
                self.used += n
                self.peak = max(self.peak, self.used)
                return t
        raise RuntimeError(f"arena {self.name} out of space for {name} n={n} used={self.used} free={self.free}")

    def release(self, t):
        ops = list(t.buf.r.values()) + list(t.buf.rd)
        if t.buf.w is not None:
            ops.append(t.buf.w)
        self.ghosts.append((t.lo, t.hi, ops))
        self.used -= (t.hi - t.lo)
        fl = self.free + [(t.lo, t.hi)]
        fl.sort()
        out = []
        for lo, hi in fl:
            if out and out[-1][1] == lo:
                out[-1] = (out[-1][0], hi)
            else:
                out.append((lo, hi))
        self.free = out


def v3(ap, b):
    return ap.rearrange("p (a b) -> p a b", b=b)


C_DEC = 0
C_XI = C_DEC + 512
C_ZS = C_XI + 256
C_COS = C_ZS + 4
C_SIN = C_COS + 512
C_RC = C_SIN + 512
C_RCN = C_RC + 2
C_EPS = C_RCN + 32
C_NEG = C_EPS + 1
NCF = C_NEG + 32
B_ID = 0
B_TRI = 128
B_ONE = 256
B_E64 = 384
NCB = B_E64 + 1024
P_GMIX = 0
P_GFFN = P_GMIX + L * 8
P_GPLE = P_GFFN + L * 8
P_GFIN = P_GPLE + L * 8
P_CW = P_GFIN + 8
P_CB = P_CW + L * 3 * 44
P_PS = P_CB + L * 44
NPAR = P_PS + L * 2


def make_consts():
    f64 = np.float64
    H = 4
    lg = np.log1p(-np.exp2(-5.0 - np.arange(H, dtype=f64)))
    cf = np.zeros((128, NCF), np.float32)
    i = np.arange(128)
    rel = i[None, :] - i[:, None]
    dec = np.zeros((128, H, 128), f64)
    for h in range(H):
        dec[:, h, :] = np.where(rel >= 0, np.exp(np.maximum(rel, 0) * lg[h]), 0.0) * 0.125
    cf[:, C_DEC:C_DEC + 512] = dec.reshape(128, 512)
    xi = np.zeros((128, 2, 128), f64)
    for p in range(128):
        for c in range(2):
            h = 2 * c + p // 64
            xi[p, c, :] = np.exp((i + 1.0) * lg[h]) * 0.125
    cf[:, C_XI:C_XI + 256] = xi.reshape(128, 256)
    for h in range(H):
        cf[:, C_ZS + h] = np.exp((127.0 - i) * lg[h])
    half = 32
    inv_freq = (np.float32(10000.0) ** (-(np.arange(half, dtype=np.float32)) / np.float32(half))).astype(np.float32)
    pos = np.arange(T, dtype=np.float32)
    ang = (pos[:, None] * inv_freq[None, :]).astype(np.float32).astype(f64)
    cos = np.cos(ang)
    sin = np.sin(ang)
    cf[:, C_COS:C_COS + 512] = cos.reshape(16, 128, 32).transpose(1, 0, 2).reshape(128, 512)
    cf[:, C_SIN:C_SIN + 512] = sin.reshape(16, 128, 32).transpose(1, 0, 2).reshape(128, 512)
    wins = [2, 4, 8, 16]
    for p in range(128):
        for c in range(2):
            w = wins[2 * c + p // 64]
            cf[p, C_RC + c] = 1.0 / w
            for t in range(16):
                cf[p, C_RCN + c * 16 + t] = 1.0 / min(t + 1, w)
    cf[:, C_EPS] = EPS
    cf[:, C_NEG:C_NEG + 32] = -1e30
    cb = np.zeros((128, NCB), np.float32)
    cb[:, B_ID:B_ID + 128] = np.eye(128)
    cb[:, B_TRI:B_TRI + 128] = (rel >= 0).astype(np.float32)
    cb[:, B_ONE:B_ONE + 128] = 1.0
    for p in range(128):
        n = p % 64
        if n < 8:
            cb[p, B_E64 + n * 128:B_E64 + (n + 1) * 128] = -30000.0
    rot = np.zeros((2, 128, T), np.float32)
    for p in range(128):
        rot[0, p, :] = cos[:, p % 32]
        rot[1, p, :] = sin[:, p % 32]
    return cf, cb, rot


def pack_params(inp):
    par = np.zeros((128, NPAR), np.float32)

    def fm(v):
        v = np.asarray(v, np.float32)
        lead = v.shape[:-1]
        c = v.shape[-1] // 128
        return np.moveaxis(v.reshape(lead + (c, 128)), -1, 0)

    par[:, P_GMIX:P_GMIX + L * 8] = fm(inp["norm_mix_g"]).reshape(128, -1)
    par[:, P_GFFN:P_GFFN + L * 8] = fm(inp["norm_ffn_g"]).reshape(128, -1)
    par[:, P_GPLE:P_GPLE + L * 8] = fm(inp["norm_ple_g"]).reshape(128, -1)
    par[:, P_GFIN:P_GFIN + 8] = fm(inp["norm_final_g"]).reshape(128, -1)
    par[:, P_CW:P_CW + L * 3 * 44] = fm(inp["conv_w"]).reshape(128, -1)
    par[:, P_CB:P_CB + L * 44] = fm(inp["conv_b"]).reshape(128, -1)
    par[:, P_PS:P_PS + L * 2] = fm(inp["pool_scale"]).reshape(128, -1)
    pw = np.asarray(inp["pool_w"], np.float32)
    bd = np.zeros((128, L, 2, 128), np.float32)
    for l in range(L):
        for c in range(2):
            for gl in range(2):
                bd[gl * 64:(gl + 1) * 64, l, c, gl * 64:(gl + 1) * 64] = pw[l, 2 * c + gl]
    return par, bd.reshape(128, L * 2 * 128)


class StopBuild(Exception):
    pass


STOP_AT = [None]
SKIP = {}


def build_program(layers, final_norm, ngrun=NG, dbg=False):
    nc = bass.Bass("TRN2", target_bir_lowering=False)
    es = ExitStack()
    S = Sched()
    NL = len(layers)

    def din(name, shape):
        return nc.dram_tensor(name, shape, F32, kind="ExternalInput").ap()

    xT_d = din("xT", [D, T])
    pT_d = din("pT", [NL, 256, T])
    w_in_d = din("w_in", [NL, D, INW])
    w_ba_d = din("w_ba", [NL, 256, D])
    w_br_d = din("w_br", [NL, 512, D])
    w_bp_d = din("w_bp", [NL, 256, D])
    w_out_d = din("w_out", [NL, D, D])
    w_up_d = din("w_up", [NL, D, 2 * FF])
    w_dn_d = din("w_dn", [NL, FF, D])
    w_pg_d = din("w_pg", [NL, D, D])
    w_pp_d = din("w_pp", [NL, 256, D])
    par_d = din("par", [128, NPAR])
    bd_d = din("bd", [128, L * 256])
    cf_d = din("cf", [128, NCF])
    cb_d = din("cb", [128, NCB])
    rot_d = din("rot", [2, 128, T])
    out_d = nc.dram_tensor("outT", [D, T], F32, kind="ExternalOutput").ap()
    dbg_outs = {}

    def sb(name, shape, dt):
        return es.enter_context(nc.sbuf_tensor(name, shape, dt))

    xT = sb("xT_sb", [128, KC * T], F32)
    xT3 = v3(xT[:], T)
    xbuf = [[Buf(f"x{c}_{g}") for g in range(NG)] for c in range(KC)]
    kaT = sb("kaT", [128, 2 * T], BF16)
    kaT3 = v3(kaT[:], T)
    kabuf = [Buf(f"ka{g}") for g in range(NG)]
    vaS = sb("vaS", [128, 16 * 256], BF16)
    va3 = v3(vaS[:], 256)
    vabuf = [Buf(f"va{g}") for g in range(NG)]
    kmT = sb("kmT", [128, 2 * 16], BF16)
    kmT3 = v3(kmT[:], 16)
    kmbuf = Buf("km")
    Rst = sb("Rst", [128, 4 * 128], F32)
    Rb = sb("Rb", [128, 4 * 128], BF16)
    Rstbuf = [Buf(f"Rst{h}") for h in range(4)]
    Rbbuf = [Buf(f"Rb{h}") for h in range(4)]
    convh = sb("convh", [128, 44 * 2], F32)
    convh3 = v3(convh[:], 2)
    convhbuf = [Buf(f"ch{i}") for i in range(44)]
    ubuf_t = sb("ubuf", [128, 2 * 528], F32)
    u3 = v3(ubuf_t[:], 528)
    ubuf = [Buf("u0"), Buf("u1")]
    wslot = [sb(f"wslot{i}", [128, 8 * 512], BF16) for i in range(NW)]
    wbuf = [Buf(f"w{i}") for i in range(NW)]
    cF = sb("cF", [128, NCF], F32)
    cFb = Buf("cF")
    cB = sb("cB", [128, NCB], BF16)
    cBb = Buf("cB")
    par = sb("par_sb", [128, NPAR], F32)
    parb = Buf("par")
    bdS = sb("bdS", [128, L * 256], BF16)
    bdb = Buf("bd")
    AFC = 4352
    ABC = 20480
    aF_t = sb("arenaF", [128, AFC], F32)
    aB_t = sb("arenaB", [128, ABC], BF16)
    aF = Arena(aF_t, AFC, "F")
    aB = Arena(aB_t, ABC, "B")
    psb = []
    for i in range(8):
        t = es.enter_context(nc.psum_tensor(f"ps{i}", [128, 512], F32))
        psb.append((t, Buf(f"ps{i}", excl=True)))
    psfree = deque(range(8))

    def psalloc():
        i = psfree.popleft()
        return i, psb[i][0], psb[i][1]

    def psrel(i):
        psfree.append(i)

    def mm(ps_ap, lhsT, rhs, start, stop, r, w):
        S.op('pe', lambda e, a=ps_ap, b=lhsT, c=rhs, s=start, t=stop: e.matmul(a, b, c, start=s, stop=t), r=r, w=[w])

    def act(out, in_, func, r, w, bias=None, scale=None):
        kw = {}
        if bias is not None:
            kw['bias'] = bias
        if scale is not None:
            kw['scale'] = scale
        S.op('act', lambda e, o=out, i=in_, f=func, kw=kw: e.activation(out=o, in_=i, func=f, **kw), r=r, w=w)

    def tt(out, in0, in1, op, r, w, eng='dve'):
        S.op(eng, lambda e, o=out, a=in0, b=in1, p=op: e.tensor_tensor(out=o, in0=a, in1=b, op=p), r=r, w=w)

    def ts(out, in0, s1, s2, op0, op1, r, w, eng='dve'):
        if op1 is None:
            S.op(eng, lambda e, o=out, a=in0, x=s1, p=op0: e.tensor_scalar(out=o, in0=a, scalar1=x, scalar2=None, op0=p), r=r, w=w)
        else:
            S.op(eng, lambda e, o=out, a=in0, x=s1, y=s2, p=op0, q=op1: e.tensor_scalar(out=o, in0=a, scalar1=x, scalar2=y, op0=p, op1=q), r=r, w=w)

    def stt(out, in0, scalar, in1, op0, op1, r, w):
        S.op('dve', lambda e, o=out, a=in0, s=scalar, b=in1, p=op0, q=op1: e.scalar_tensor_tensor(out=o, in0=a, scalar=s, in1=b, op0=p, op1=q), r=r, w=w)

    def cp(out, in_, r, w, eng='dve'):
        S.op(eng, lambda e, o=out, i=in_: e.tensor_copy(out=o, in_=i), r=r, w=w)

    def recip(out, in_, r, w):
        S.op('dve', lambda e, o=out, i=in_: e.reciprocal(out=o, in_=i), r=r, w=w)

    def memset(ap, val, w, eng='dve'):
        S.op(eng, lambda e, a=ap, v=val: e.memset(a, v), w=w)

    def dma(eng, out, in_, key, r=(), w=()):
        S.op(eng, lambda e, o=out, i=in_: e.dma_start(out=o, in_=i), r=r, w=w, key=key)

    wctr = [0]

    def wload(src, P, kc, ncols):
        i = wctr[0] % NW
        wctr[0] += 1
        view = wslot[i][0:P, 0:kc * ncols].rearrange("p (k n) -> p k n", n=ncols)
        dma('pool', view, src, f'w{i}', w=[wbuf[i]])
        return view, wbuf[i]

    def dump(name, ap, buf, shape, dt=F32):
        if not dbg:
            return
        d = nc.dram_tensor(name, shape, dt, kind="ExternalOutput").ap()
        dbg_outs[name] = d
        dma('sp', d, ap, 'dbg', r=[buf])

    dma('sp', cF[:], cf_d[:, :], 'c0', w=[cFb])
    dma('sp', par[:], par_d[:, :], 'c1', w=[parb])
    dma('pool', cB[:], cb_d[:, :], 'c2', w=[cBb])
    dma('pool', bdS[:], bd_d[:, :], 'c3', w=[bdb])
    xsrc = xT_d.rearrange("(c p) t -> p c t", p=128)
    for c in range(KC):
        dma('sp', xT3[:, c, :], xsrc[:, c, :], f'x{c}', w=[xbuf[c][g] for g in range(NG)])
    memset(kmT[:], 0.0, w=[kmbuf])
    ident = cB[:, B_ID:B_ID + 128]
    tri = cB[:, B_TRI:B_TRI + 128]
    ones = cB[:, B_ONE:B_ONE + 128]
    E64 = v3(cB[:, B_E64:B_E64 + 1024], 128)
    decT = cF[:, C_DEC:C_DEC + 512]
    xiT = v3(cF[:, C_XI:C_XI + 256], 128)
    zs = cF[:, C_ZS:C_ZS + 4]
    costm = v3(cF[:, C_COS:C_COS + 512], 32)
    sintm = v3(cF[:, C_SIN:C_SIN + 512], 32)
    epsc = cF[:, C_EPS:C_EPS + 1]
    gam = [float(np.exp(128.0 * np.log1p(-np.exp2(-5.0 - h)))) for h in range(4)]

    def rmsnorm(g, gcol):
        t0 = g * TG
        pi, ps, pb = psalloc()
        for c in range(KC):
            sq = aB.alloc(TG, "sq")
            act(sq.ap, xT3[:, c, t0:t0 + TG], AF.Square, r=[xbuf[c][g]], w=[sq.buf])
            mm(ps[:, :], ones, sq.ap, c == 0, c == KC - 1, r=[sq.buf, cBb], w=pb)
            aB.release(sq)
        rstd = aF.alloc(TG, "rstd")
        act(rstd.ap, ps[:, :], AF.Sqrt, r=[pb, cFb], w=[rstd.buf], bias=epsc, scale=1.0 / D)
        recip(rstd.ap, rstd.ap, r=[rstd.buf], w=[rstd.buf])
        psrel(pi)
        hT = aB.alloc(KC * TG, "hT")
        h3 = v3(hT.ap, TG)
        for c in range(KC):
            stt(h3[:, c, :], xT3[:, c, t0:t0 + TG], par[:, gcol + c:gcol + c + 1], rstd.ap, ALU.mult, ALU.mult,
                r=[xbuf[c][g], parb, rstd.buf], w=[hT.buf])
        aF.release(rstd)
        return hT, h3

    def ckpt(name):
        if STOP_AT[0] == name:
            raise StopBuild()

    try:
      for li, l in enumerate(layers):
          wi = w_in_d[li].rearrange("(k p) n -> p k n", p=128)
          for h in range(4):
              hb = 64 * (h % 2)
              memset(Rst[hb:hb + 64, h * 128:(h + 1) * 128], 0.0, w=[Rstbuf[h]])
              memset(Rb[hb:hb + 64, h * 128:(h + 1) * 128], 0.0, w=[Rbbuf[h]])
          for i in range(44):
              memset(convh3[:, i, :], 0.0, w=[convhbuf[i]])
          for c in range(2):
              memset(u3[:, c, 0:16], 0.0, w=[ubuf[c]])
          for g in range(ngrun):
              t0 = g * TG
              S.epoch += 1
              hT, h3 = rmsnorm(g, P_GMIX + l * 8)
              if dbg and li == 0 and g == 0:
                  dump("d_h", hT.ap, hT.buf, [128, KC * TG], BF16)
              ckpt('norm')
              wv, wb = wload(wi[:, :, 0:512], 128, 8, 512)
              qaT = aB.alloc(2 * TG, "qaT")
              qa3 = v3(qaT.ap, TG)
              for c in range(2):
                  pi, ps, pb = psalloc()
                  for k in range(KC):
                      mm(ps[:, :], wv[:, k, c * 128:(c + 1) * 128], h3[:, k, :], k == 0, k == KC - 1, r=[wb, hT.buf], w=pb)
                  act(qa3[:, c, :], ps[:, :], AF.Identity, r=[pb], w=[qaT.buf])
                  psrel(pi)
              ckpt('w0')
              for c in range(2):
                  pi, ps, pb = psalloc()
                  for k in range(KC):
                      mm(ps[:, :], wv[:, k, 256 + c * 128:256 + (c + 1) * 128], h3[:, k, :], k == 0, k == KC - 1, r=[wb, hT.buf], w=pb)
                  act(kaT3[:, c, t0:t0 + TG], ps[:, :], AF.Identity, r=[pb], w=[kabuf[g]])
                  if SKIP.get('km'):
                      psrel(pi)
                      continue
                  km = aF.alloc(2, "km")
                  S.op('dve', lambda e, o=km.ap, i=v3(kaT3[:, c, t0:t0 + TG], 256): e.tensor_reduce(out=o, in_=i, axis=AX.X, op=ALU.add),
                       r=[kabuf[g]], w=[km.buf])
                  ts(kmT3[0:64, c, 2 * g:2 * g + 2], km.ap[0:64, :], 1.0 / 256.0, None, ALU.mult, None, r=[km.buf], w=[kmbuf])
                  ts(kmT3[64:128, c, 8 + 2 * g:8 + 2 * g + 2], km.ap[64:128, :], 1.0 / 256.0, None, ALU.mult, None, r=[km.buf], w=[kmbuf])
                  aF.release(km)
                  psrel(pi)
              ckpt('ka')
              wv, wb = wload(wi[:, :, 512:768], 128, 8, 256)
              for tl in range(4):
                  pi, ps, pb = psalloc()
                  for k in range(KC):
                      mm(ps[:, 0:256], h3[:, k, tl * 128:(tl + 1) * 128], wv[:, k, :], k == 0, k == KC - 1, r=[wb, hT.buf], w=pb)
                  act(va3[:, 4 * g + tl, :], ps[:, 0:256], AF.Identity, r=[pb], w=[vabuf[g]])
                  psrel(pi)
              if dbg and li == 0 and g == 0:
                  dump("d_qa", qaT.ap, qaT.buf, [128, 2 * TG], BF16)
              ckpt('qkv')
              nbT = None
              if g >= 2 and not SKIP.get('sel'):
                  nbT = [aB.alloc(TG, "nbT0"), aB.alloc(TG, "nbT1")]
                  gi, gps, gpb = psalloc()
                  for qt in range(4):
                      for c in range(2):
                          mm(gps[:, qt * 32 + c * 16:qt * 32 + c * 16 + 16], qa3[:, c, qt * 128:(qt + 1) * 128],
                             kmT3[:, c, :], True, True, r=[qaT.buf, kmbuf], w=gpb)
                  ckpt('sel1')
                  for qt in range(4):
                      b = 2 * g + qt // 2
                      gsb = aF.alloc(32, "gsb")
                      m8 = aF.alloc(32, "m8")
                      cp(gsb.ap, cF[:, C_NEG:C_NEG + 32], r=[cFb], w=[gsb.buf])
                      cp(v3(gsb.ap, 8)[:, :, 0:b], v3(gps[:, qt * 32:(qt + 1) * 32], 8)[:, :, 0:b], r=[gpb], w=[gsb.buf])
                      for h in range(4):
                          S.op('dve', lambda e, o=m8.ap[:, h * 8:(h + 1) * 8], i=gsb.ap[:, h * 8:(h + 1) * 8]: e.max(out=o, in_=i),
                               r=[gsb.buf], w=[m8.buf])
                      negb = aB.alloc(256, "negb")
                      memset(negb.ap, 0.0, w=[negb.buf])
                      tt(negb.ap.rearrange("p (h e) -> p h e", e=64)[:, :, 0:8], v3(gsb.ap, 8),
                         v3(m8.ap, 8)[:, :, 2:3].to_broadcast([128, 4, 8]), ALU.is_lt, r=[gsb.buf, m8.buf], w=[negb.buf])
                      ckpt('sel2')
                      for tl in range(2):
                          pi, ps, pb = psalloc()
                          mm(ps[:, 0:128], negb.ap[:, tl * 128:(tl + 1) * 128], ident, True, True, r=[negb.buf, cBb], w=pb)
                          act(nbT[tl].ap[:, qt * 128:(qt + 1) * 128], ps[:, 0:128], AF.Identity, r=[pb], w=[nbT[tl].buf])
                          psrel(pi)
                      aB.release(negb)
                      aF.release(gsb)
                      aF.release(m8)
                      ckpt('sel3')
                  psrel(gi)
              attnT = aB.alloc(4 * TG, "attnT")
              at3 = v3(attnT.ap, TG)
              nkt = 4 * g + 4
              SK = 2
              acc = {}
              pend = deque()

              def do_pv(h, kt, pt, q0):
                  if h not in acc:
                      acc[h] = psalloc() + psalloc()
                  oi, ops_, opb, li_, lps, lpb = acc[h]
                  mm(ops_[0:64, q0:TG], va3[:, kt, h * 64:(h + 1) * 64], pt.ap[:, q0:TG], kt == 0, kt == nkt - 1,
                     r=[vabuf[kt // 4], pt.buf], w=opb)
                  mm(lps[0:64, q0:TG], ones[:, 0:64], pt.ap[:, q0:TG], kt == 0, kt == nkt - 1, r=[cBb, pt.buf], w=lpb)
                  aB.release(pt)
                  if kt == nkt - 1:
                      rc_ = aF.alloc(TG, "rcp")
                      recip(rc_.ap[0:64, :], lps[0:64, :], r=[lpb], w=[rc_.buf])
                      tt(at3[0:64, h, :], ops_[0:64, :], rc_.ap[0:64, :], ALU.mult, r=[opb, rc_.buf], w=[attnT.buf])
                      aF.release(rc_)
                      psrel(oi)
                      psrel(li_)

              for h in range(4):
                  c, base = h // 2, 64 * (h % 2)
                  for kt in range(nkt):
                      r_ = kt - 4 * g
                      q0 = 128 * r_ if r_ > 0 else 0
                      n = kt // 2
                      bias_c0 = None
                      if g >= 2 and not SKIP.get('sel') and not SKIP.get('bias'):
                          if n < 2 * g:
                              bias_c0 = 0
                          elif n == 2 * g:
                              bias_c0 = 256
                      pi, ps, pb = psalloc()
                      mm(ps[:, q0:TG], kaT3[base:base + 64, c, kt * 128:(kt + 1) * 128], qa3[base:base + 64, c, q0:TG],
                         True, bias_c0 is None, r=[kabuf[kt // 4], qaT.buf], w=pb)
                      if bias_c0 is not None:
                          tl, sl = h // 2, 64 * (h % 2)
                          c0 = max(bias_c0, q0)
                          mm(ps[:, c0:TG], E64[sl:sl + 64, n, :], nbT[tl].ap[sl:sl + 64, c0:TG], False, True,
                             r=[cBb, nbT[tl].buf], w=pb)
                      pt = aB.alloc(TG, "pt")
                      act(pt.ap[:, q0:TG], ps[:, q0:TG], AF.Exp, r=[pb], w=[pt.buf], scale=0.125)
                      psrel(pi)
                      if r_ >= 0:
                          tt(pt.ap[:, q0:q0 + 128], pt.ap[:, q0:q0 + 128], tri, ALU.mult, r=[pt.buf, cBb], w=[pt.buf])
                      pend.append((h, kt, pt, q0))
                      if len(pend) > SK:
                          do_pv(*pend.popleft())
              while pend:
                  do_pv(*pend.popleft())
              aB.release(qaT)
              if nbT is not None:
                  aB.release(nbT[0])
                  aB.release(nbT[1])
              if dbg and li == 0 and g == 0:
                  dump("d_attn", attnT.ap, attnT.buf, [128, 4 * TG], BF16)
              ckpt('attn')
              wv, wb = wload(wi[:, :, 768:1280], 128, 8, 512)
              wrot = aB.alloc(8 * 512, "wrot")
              wr3 = v3(wrot.ap, 512)
              w5 = wv.rearrange("p k (h t f) -> p k h t f", t=2, f=32)
              r5 = wr3.rearrange("p k (h t f) -> p k h t f", t=2, f=32)
              for k in range(KC):
                  S.op('act', lambda e, o=r5[:, k, :, 0, :], i=w5[:, k, :, 1, :]: e.mul(out=o, in_=i, mul=-1.0), r=[wb], w=[wrot.buf])
                  cp(r5[:, k, :, 1, :], w5[:, k, :, 0, :], r=[wb], w=[wrot.buf])
              rt = aF.alloc(2 * TG, "rot")
              rt3 = v3(rt.ap, TG)
              dma('sp', rt3[:, 0, :], rot_d[0, :, t0:t0 + TG], 'rot', w=[rt.buf])
              dma('sp', rt3[:, 1, :], rot_d[1, :, t0:t0 + TG], 'rot', w=[rt.buf])
              qk = aB.alloc(4 * TG, "qk")
              qk3 = v3(qk.ap, TG)
              qx = aB.alloc(2 * TG, "qx")
              qx3 = v3(qx.ap, TG)
              for c4 in range(4):
                  pi, ps, pb = psalloc()
                  pj, ps2, pb2 = psalloc()
                  for k in range(KC):
                      mm(ps[:, :], wv[:, k, c4 * 128:(c4 + 1) * 128], h3[:, k, :], k == 0, k == KC - 1, r=[wb, hT.buf], w=pb)
                  for k in range(KC):
                      mm(ps2[:, :], wr3[:, k, c4 * 128:(c4 + 1) * 128], h3[:, k, :], k == 0, k == KC - 1, r=[wrot.buf, hT.buf], w=pb2)
                  t1 = aF.alloc(TG, "t1")
                  t2 = aF.alloc(TG, "t2")
                  tt(t1.ap, ps[:, :], rt3[:, 0, :], ALU.mult, r=[pb, rt.buf], w=[t1.buf])
                  tt(t2.ap, ps2[:, :], rt3[:, 1, :], ALU.mult, r=[pb2, rt.buf], w=[t2.buf])
                  psrel(pi)
                  psrel(pj)
                  tt(qk3[:, c4, :], t1.ap, t2.ap, ALU.add, r=[t1.buf, t2.buf], w=[qk.buf])
                  if c4 < 2:
                      tt(t1.ap, t1.ap, t2.ap, ALU.add, r=[t1.buf, t2.buf], w=[t1.buf])
                      tt(v3(qx3[:, c4, :], 128), v3(t1.ap, 128), xiT[:, c4, :].unsqueeze(1).to_broadcast([128, 4, 128]), ALU.mult,
                         r=[t1.buf, cFb], w=[qx.buf])
                  aF.release(t1)
                  aF.release(t2)
              aF.release(rt)
              kt_t = aB.alloc(4 * 256, "ktm")
              ktm3 = v3(kt_t.ap, 256)
              for tl in range(4):
                  pi, ps, pb = psalloc()
                  for k in range(KC):
                      mm(ps[:, 0:256], h3[:, k, tl * 128:(tl + 1) * 128], wv[:, k, 256:512], k == 0, k == KC - 1, r=[wb, hT.buf], w=pb)
                  for k in range(KC):
                      mm(ps[:, 256:512], h3[:, k, tl * 128:(tl + 1) * 128], wr3[:, k, 256:512], k == 0, k == KC - 1, r=[wrot.buf, hT.buf], w=pb)
                  t1 = aF.alloc(256, "k1")
                  t2 = aF.alloc(256, "k2")
                  cb_ = costm[:, 4 * g + tl, :].unsqueeze(1).to_broadcast([128, 8, 32])
                  sb_ = sintm[:, 4 * g + tl, :].unsqueeze(1).to_broadcast([128, 8, 32])
                  tt(v3(t1.ap, 32), v3(ps[:, 0:256], 32), cb_, ALU.mult, r=[pb, cFb], w=[t1.buf])
                  tt(v3(t2.ap, 32), v3(ps[:, 256:512], 32), sb_, ALU.mult, r=[pb, cFb], w=[t2.buf])
                  psrel(pi)
                  tt(t1.ap, t1.ap, t2.ap, ALU.add, r=[t1.buf, t2.buf], w=[t1.buf])
                  tt(v3(ktm3[:, tl, :], 64), v3(t1.ap, 64), zs.unsqueeze(2).to_broadcast([128, 4, 64]), ALU.mult,
                     r=[t1.buf, cFb], w=[kt_t.buf])
                  aF.release(t1)
                  aF.release(t2)
              aB.release(wrot)
              wv, wb = wload(wi[:, :, 1280:1792], 128, 8, 512)
              vr = aB.alloc(4 * 512, "vr")
              vr3 = v3(vr.ap, 512)
              for tl in range(4):
                  pi, ps, pb = psalloc()
                  for k in range(KC):
                      mm(ps[:, :], h3[:, k, tl * 128:(tl + 1) * 128], wv[:, k, :], k == 0, k == KC - 1, r=[wb, hT.buf], w=pb)
                  act(vr3[:, tl, :], ps[:, :], AF.Identity, r=[pb], w=[vr.buf])
                  psrel(pi)
              wv, wb = wload(wi[:, :, 1792:2304], 128, 8, 512)
              sg = aB.alloc(4 * TG, "silug")
              sg3 = v3(sg.ap, TG)
              for c in range(4):
                  pi, ps, pb = psalloc()
                  for k in range(KC):
                      mm(ps[:, :], wv[:, k, c * 128:(c + 1) * 128], h3[:, k, :], k == 0, k == KC - 1, r=[wb, hT.buf], w=pb)
                  act(sg3[:, c, :], ps[:, :], AF.Silu, r=[pb], w=[sg.buf])
                  psrel(pi)
              retT = aB.alloc(4 * TG, "retT")
              rT3 = v3(retT.ap, TG)
              scs = []
              for h in range(4):
                  c, base = h // 2, 64 * (h % 2)
                  si, sps, spb = psalloc()
                  for r_ in range(4):
                      mm(sps[:, r_ * 128:(r_ + 1) * 128], qk3[base:base + 64, 2 + c, r_ * 128:(r_ + 1) * 128],
                         qk3[base:base + 64, c, r_ * 128:(r_ + 1) * 128], True, True, r=[qk.buf], w=spb)
                  sc = aB.alloc(TG, "sc")
                  tt(v3(sc.ap, 128), v3(sps[:, :], 128), decT[:, h * 128:(h + 1) * 128].unsqueeze(1).to_broadcast([128, 4, 128]),
                     ALU.mult, r=[spb, cFb], w=[sc.buf])
                  psrel(si)
                  scs.append(sc)
              ybanks = [psalloc() for _ in range(4)]
              for r_ in range(4):
                  cs = slice(r_ * 128, (r_ + 1) * 128)
                  for h in range(4):
                      c, base = h // 2, 64 * (h % 2)
                      yi, yps, ypb = ybanks[h]
                      sc = scs[h]
                      mm(yps[:, cs], vr3[:, r_, h * 128:(h + 1) * 128], sc.ap[:, cs], True, False, r=[vr.buf, sc.buf], w=ypb)
                      mm(yps[:, cs], Rb[base:base + 64, h * 128:(h + 1) * 128], qx3[base:base + 64, c, cs], False, True,
                         r=[Rbbuf[h], qx.buf], w=ypb)
                      ui, ups, upb = psalloc()
                      mm(ups[:, 0:128], ktm3[:, r_, c * 128:(c + 1) * 128], vr3[:, r_, h * 128:(h + 1) * 128], True, True,
                         r=[kt_t.buf, vr.buf], w=upb)
                      stt(Rst[base:base + 64, h * 128:(h + 1) * 128], Rst[base:base + 64, h * 128:(h + 1) * 128], gam[h],
                          ups[base:base + 64, 0:128], ALU.mult, ALU.add, r=[Rstbuf[h], upb], w=[Rstbuf[h]])
                      psrel(ui)
                      cp(Rb[base:base + 64, h * 128:(h + 1) * 128], Rst[base:base + 64, h * 128:(h + 1) * 128],
                         r=[Rstbuf[h]], w=[Rbbuf[h]])
              for sc in scs:
                  aB.release(sc)
              for h in range(4):
                  c, base = h // 2, 64 * (h % 2)
                  yi, yps, ypb = ybanks[h]
                  yb = aB.alloc(TG, "yb")
                  ysq = aB.alloc(TG, "ysq")
                  act(yb.ap, yps[:, :], AF.Identity, r=[ypb], w=[yb.buf])
                  act(ysq.ap, yps[:, :], AF.Square, r=[ypb], w=[ysq.buf])
                  s1i, s1, s1b = psalloc()
                  s2i, s2, s2b = psalloc()
                  mm(s1[:, :], ones, yb.ap, True, True, r=[cBb, yb.buf], w=s1b)
                  mm(s2[:, :], ones, ysq.ap, True, True, r=[cBb, ysq.buf], w=s2b)
                  aB.release(yb)
                  aB.release(ysq)
                  mean = aF.alloc(TG, "mean")
                  var = aF.alloc(TG, "var")
                  ts(mean.ap, s1[:, :], 1.0 / 128.0, None, ALU.mult, None, r=[s1b], w=[mean.buf])
                  tt(var.ap, mean.ap, mean.ap, ALU.mult, r=[mean.buf], w=[var.buf])
                  stt(var.ap, s2[:, :], 1.0 / 128.0, var.ap, ALU.mult, ALU.subtract, r=[s2b, var.buf], w=[var.buf])
                  psrel(s1i)
                  psrel(s2i)
                  ts(var.ap, var.ap, 0.0, None, ALU.max, None, r=[var.buf], w=[var.buf])
                  act(var.ap, var.ap, AF.Sqrt, r=[var.buf, cFb], w=[var.buf], bias=epsc, scale=1.0)
                  recip(var.ap, var.ap, r=[var.buf], w=[var.buf])
                  tt(mean.ap, yps[:, :], mean.ap, ALU.subtract, r=[ypb, mean.buf], w=[mean.buf])
                  psrel(yi)
                  tt(mean.ap, mean.ap, var.ap, ALU.mult, r=[mean.buf, var.buf], w=[mean.buf])
                  tt(rT3[:, h, :], mean.ap, sg3[:, h, :], ALU.mult, r=[mean.buf, sg.buf], w=[retT.buf])
                  aF.release(mean)
                  aF.release(var)
              for t_ in (qk, qx, kt_t, vr, sg):
                  aB.release(t_)
              if dbg and li == 0 and g == 0:
                  dump("d_ret", retT.ap, retT.buf, [128, 4 * TG], BF16)
              ckpt('ret')
              wv, wb = wload(wi[:, :, 2304:2560], 128, 8, 256)
              mixed = aB.alloc(2 * TG, "mixed")
              mx3 = v3(mixed.ap, TG)
              for c in range(2):
                  pi, ps, pb = psalloc()
                  for k in range(KC):
                      mm(ps[:, :], wv[:, k, c * 128:(c + 1) * 128], h3[:, k, :], k == 0, k == KC - 1, r=[wb, hT.buf], w=pb)
                  act(u3[:, c, 16:528], ps[:, :], AF.Identity, r=[pb], w=[ubuf[c]])
                  psrel(pi)
                  sa = aF.alloc(528, "sa")
                  sbb = aF.alloc(528, "sb")
                  tt(sa.ap[:, 1:528], u3[:, c, 1:528], u3[:, c, 0:527], ALU.add, r=[ubuf[c]], w=[sa.buf])
                  if c == 0:
                      tt(sbb.ap[64:128, 3:528], sa.ap[64:128, 3:528], sa.ap[64:128, 1:526], ALU.add, r=[sa.buf], w=[sbb.buf])
                      lo_src, hi_src = sa, sbb
                  else:
                      tt(sbb.ap[:, 3:528], sa.ap[:, 3:528], sa.ap[:, 1:526], ALU.add, r=[sa.buf], w=[sbb.buf])
                      tt(sa.ap[:, 7:528], sbb.ap[:, 7:528], sbb.ap[:, 3:524], ALU.add, r=[sbb.buf], w=[sa.buf])
                      tt(sbb.ap[64:128, 15:528], sa.ap[64:128, 15:528], sa.ap[64:128, 7:520], ALU.add, r=[sa.buf], w=[sbb.buf])
                      lo_src, hi_src = sa, sbb
                  rcs = cF[:, C_RC + c:C_RC + c + 1]
                  for (p0, p1, src) in ((0, 64, lo_src), (64, 128, hi_src)):
                      stt(mx3[p0:p1, c, :], src.ap[p0:p1, 16:528], rcs[p0:p1, :], u3[p0:p1, c, 16:528], ALU.mult, ALU.subtract,
                          r=[src.buf, ubuf[c], cFb], w=[mixed.buf])
                      if g == 0:
                          tf = aF.alloc(16, "tf")
                          tt(tf.ap[p0:p1, :], src.ap[p0:p1, 16:32], cF[p0:p1, C_RCN + c * 16:C_RCN + (c + 1) * 16], ALU.mult,
                             r=[src.buf, cFb], w=[tf.buf])
                          tt(mx3[p0:p1, c, 0:15], tf.ap[p0:p1, 0:15], u3[p0:p1, c, 16:31], ALU.subtract, r=[tf.buf, ubuf[c]], w=[mixed.buf])
                          aF.release(tf)
                  aF.release(sa)
                  aF.release(sbb)
                  th = aF.alloc(16, "th")
                  cp(th.ap, u3[:, c, 512:528], r=[ubuf[c]], w=[th.buf])
                  cp(u3[:, c, 0:16], th.ap, r=[th.buf], w=[ubuf[c]])
                  aF.release(th)
              poolT = aB.alloc(2 * TG, "poolT")
              pl3 = v3(poolT.ap, TG)
              for c in range(2):
                  pi, ps, pb = psalloc()
                  mm(ps[:, :], bdS[:, (l * 2 + c) * 128:(l * 2 + c + 1) * 128], mx3[:, c, :], True, True, r=[bdb, mixed.buf], w=pb)
                  act(pl3[:, c, :], ps[:, :], AF.Identity, r=[pb, parb], w=[poolT.buf], scale=par[:, P_PS + l * 2 + c:P_PS + l * 2 + c + 1])
                  psrel(pi)
              aB.release(mixed)
              if dbg and li == 0 and g == 0:
                  dump("d_pool", poolT.ap, poolT.buf, [128, 2 * TG], BF16)
              ckpt('pool')
              merged = aB.alloc(KC * TG, "merged")
              mg3 = v3(merged.ap, TG)
              for ct in range(2):
                  macc = aF.alloc(4 * TG, "macc")
                  ma3 = v3(macc.ap, TG)
                  for br in range(3):
                      gv, gb = wload(wi[:, :, 2560 + br * 1024 + ct * 512:2560 + br * 1024 + (ct + 1) * 512], 128, 8, 512)
                      if br == 0:
                          src = w_ba_d[li].rearrange("(h p) n -> p h n", p=64)[:, :, ct * 512:(ct + 1) * 512]
                          bv, bb = wload(src, 64, 4, 512)
                          nk, bin_, bbuf, bp = 4, at3, attnT.buf, 64
                      elif br == 1:
                          src = w_br_d[li].rearrange("(h p) n -> p h n", p=128)[:, :, ct * 512:(ct + 1) * 512]
                          bv, bb = wload(src, 128, 4, 512)
                          nk, bin_, bbuf, bp = 4, rT3, retT.buf, 128
                      else:
                          src = w_bp_d[li].rearrange("(h p) n -> p h n", p=128)[:, :, ct * 512:(ct + 1) * 512]
                          bv, bb = wload(src, 128, 2, 512)
                          nk, bin_, bbuf, bp = 2, pl3, poolT.buf, 128
                      for j in range(4):
                          dc = ct * 4 + j
                          gi, gps, gpb = psalloc()
                          for k in range(KC):
                              mm(gps[:, :], gv[:, k, j * 128:(j + 1) * 128], h3[:, k, :], k == 0, k == KC - 1, r=[gb, hT.buf], w=gpb)
                          bi, bps, bpb = psalloc()
                          for k in range(nk):
                              mm(bps[:, :], bv[0:bp, k, j * 128:(j + 1) * 128], bin_[0:bp, k, :], k == 0, k == nk - 1, r=[bb, bbuf], w=bpb)
                          sig = aF.alloc(TG, "sig")
                          act(sig.ap, gps[:, :], AF.Sigmoid, r=[gpb], w=[sig.buf])
                          psrel(gi)
                          if br == 0:
                              tt(ma3[:, j, :], sig.ap, bps[:, :], ALU.mult, r=[sig.buf, bpb], w=[macc.buf])
                          else:
                              tt(sig.ap, sig.ap, bps[:, :], ALU.mult, r=[sig.buf, bpb], w=[sig.buf])
                              if br == 1:
                                  tt(ma3[:, j, :], ma3[:, j, :], sig.ap, ALU.add, r=[sig.buf, macc.buf], w=[macc.buf])
                              else:
                                  tt(mg3[:, dc, :], ma3[:, j, :], sig.ap, ALU.add, r=[sig.buf, macc.buf], w=[merged.buf])
                          psrel(bi)
                          aF.release(sig)
                  aF.release(macc)
              for t_ in (attnT, retT, poolT, hT):
                  aB.release(t_)
              if dbg and li == 0 and g == 0:
                  dump("d_merged", merged.ap, merged.buf, [128, KC * TG], BF16)
              wo = w_out_d[li].rearrange("(k p) n -> p k n", p=128)
              for ct in range(2):
                  wv, wb = wload(wo[:, :, ct * 512:(ct + 1) * 512], 128, 8, 512)
                  for j in range(4):
                      dc = ct * 4 + j
                      pi, ps, pb = psalloc()
                      for k in range(KC):
                          mm(ps[:, :], wv[:, k, j * 128:(j + 1) * 128], mg3[:, k, :], k == 0, k == KC - 1, r=[wb, merged.buf], w=pb)
                      tt(xT3[:, dc, t0:t0 + TG], xT3[:, dc, t0:t0 + TG], ps[:, :], ALU.add, r=[xbuf[dc][g], pb], w=[xbuf[dc][g]])
                      psrel(pi)
              aB.release(merged)
              ckpt('mix')
              hT, h3 = rmsnorm(g, P_GFFN + l * 8)
              wu = w_up_d[li].rearrange("(k p) n -> p k n", p=128)
              mT = aB.alloc(22 * TG, "mT")
              m3 = v3(mT.ap, TG)
              mbuf = [Buf(f"m{i}") for i in range(22)]

              def ffn_final(items):
                  for (ch_, y_) in items:
                      if ch_ < 22:
                          act(m3[:, ch_, :], y_.ap, AF.Gelu_apprx_tanh, r=[y_.buf], w=[mbuf[ch_]])
                      else:
                          tt(m3[:, ch_ - 22, :], m3[:, ch_ - 22, :], y_.ap, ALU.mult, r=[y_.buf, mbuf[ch_ - 22]], w=[mbuf[ch_ - 22]])
                      aF.release(y_)

              prev = []
              for it in range(22):
                  if it % 2 == 0:
                      wv, wb = wload(wu[:, :, (it // 2) * 512:(it // 2 + 1) * 512], 128, 8, 512)
                  chs = [2 * it, 2 * it + 1]
                  pss = []
                  for ch in chs:
                      j = ch % 4
                      pi, ps, pb = psalloc()
                      for k in range(KC):
                          mm(ps[:, :], wv[:, k, j * 128:(j + 1) * 128], h3[:, k, :], k == 0, k == KC - 1, r=[wb, hT.buf], w=pb)
                      pss.append((pi, ps, pb))
                  Us = []
                  for ch, (pi, ps, pb) in zip(chs, pss):
                      U = aF.alloc(514, "U", rot=True)
                      act(U.ap[:, 2:514], ps[:, :], AF.Identity, r=[pb], w=[U.buf])
                      psrel(pi)
                      Us.append(U)
                  ffn_final(prev)
                  for ch, U in zip(chs, Us):
                      act(U.ap[:, 0:2], convh3[:, ch, :], AF.Identity, r=[convhbuf[ch]], w=[U.buf])
                      act(convh3[:, ch, :], U.ap[:, 512:514], AF.Identity, r=[U.buf], w=[convhbuf[ch]])
                  cw = lambda kk, ch: par[:, P_CW + (l * 3 + kk) * 44 + ch:P_CW + (l * 3 + kk) * 44 + ch + 1]
                  ys = []
                  for ch, U in zip(chs, Us):
                      y = aF.alloc(TG, "y", rot=True)
                      cbias = par[:, P_CB + l * 44 + ch:P_CB + l * 44 + ch + 1]
                      act(y.ap, U.ap[:, 2:514], AF.Identity, r=[U.buf, parb], w=[y.buf], bias=cbias, scale=cw(2, ch))
                      ys.append(y)
                  for ch, U, y in zip(chs, Us, ys):
                      stt(y.ap, U.ap[:, 1:513], cw(1, ch), y.ap, ALU.mult, ALU.add, r=[U.buf, y.buf, parb], w=[y.buf])
                  for ch, U, y in zip(chs, Us, ys):
                      stt(y.ap, U.ap[:, 0:512], cw(0, ch), y.ap, ALU.mult, ALU.add, r=[U.buf, y.buf, parb], w=[y.buf])
                  for U in Us:
                      aF.release(U)
                  prev = list(zip(chs, ys))
              ffn_final(prev)
              aB.release(hT)
              wd = w_dn_d[li].rearrange("(k p) n -> p k n", p=128)
              for ct in range(2):
                  accs = [psalloc() for _ in range(4)]
                  for kg, (k0, k1) in enumerate(((0, 8), (8, 16), (16, 22))):
                      wv, wb = wload(wd[:, k0:k1, ct * 512:(ct + 1) * 512], 128, k1 - k0, 512)
                      for j in range(4):
                          for kk in range(k1 - k0):
                              mm(accs[j][1][:, :], wv[:, kk, j * 128:(j + 1) * 128], m3[:, k0 + kk, :], k0 + kk == 0, k0 + kk == 21,
                                 r=[wb, mbuf[k0 + kk]], w=accs[j][2])
                  for j in range(4):
                      dc = ct * 4 + j
                      tt(xT3[:, dc, t0:t0 + TG], xT3[:, dc, t0:t0 + TG], accs[j][1][:, :], ALU.add, r=[xbuf[dc][g], accs[j][2]], w=[xbuf[dc][g]])
                      psrel(accs[j][0])
              for b_ in mbuf:
                  for e_, o_ in b_.r.items():
                      p_ = mT.buf.r.get(e_)
                      if p_ is None or p_.idx < o_.idx:
                          mT.buf.r[e_] = o_
                  if b_.w is not None:
                      p_ = mT.buf.r.get(b_.w.eng)
                      if p_ is None or p_.idx < b_.w.idx:
                          mT.buf.r[b_.w.eng] = b_.w
              aB.release(mT)
              ckpt('ffn')
              hT, h3 = rmsnorm(g, P_GPLE + l * 8)
              pt_ = aB.alloc(2 * TG, "pT")
              p3 = v3(pt_.ap, TG)
              dma('pool', p3, pT_d[li].rearrange("(c p) t -> p c t", p=128)[:, :, t0:t0 + TG], 'pT', w=[pt_.buf])
              wg = w_pg_d[li].rearrange("(k p) n -> p k n", p=128)
              wp = w_pp_d[li].rearrange("(k p) n -> p k n", p=128)
              for ct in range(2):
                  gv, gb = wload(wg[:, :, ct * 512:(ct + 1) * 512], 128, 8, 512)
                  pv, pbb = wload(wp[:, :, ct * 512:(ct + 1) * 512], 128, 2, 512)
                  for j in range(4):
                      dc = ct * 4 + j
                      gi, gps, gpb = psalloc()
                      for k in range(KC):
                          mm(gps[:, :], gv[:, k, j * 128:(j + 1) * 128], h3[:, k, :], k == 0, k == KC - 1, r=[gb, hT.buf], w=gpb)
                      bi, bps, bpb = psalloc()
                      for k in range(2):
                          mm(bps[:, :], pv[:, k, j * 128:(j + 1) * 128], p3[:, k, :], k == 0, k == 1, r=[pbb, pt_.buf], w=bpb)
                      sig = aF.alloc(TG, "sig")
                      act(sig.ap, gps[:, :], AF.Sigmoid, r=[gpb], w=[sig.buf])
                      psrel(gi)
                      tt(sig.ap, sig.ap, bps[:, :], ALU.mult, r=[sig.buf, bpb], w=[sig.buf])
                      psrel(bi)
                      tt(xT3[:, dc, t0:t0 + TG], xT3[:, dc, t0:t0 + TG], sig.ap, ALU.add, r=[xbuf[dc][g], sig.buf], w=[xbuf[dc][g]])
                      aF.release(sig)
              aB.release(pt_)
              aB.release(hT)
    except StopBuild:
        pass
    S.epoch += 1
    osrc = out_d.rearrange("(c p) t -> p c t", p=128)
    for g in range(NG):
        t0 = g * TG
        if final_norm and g < ngrun:
            pi, ps, pb = psalloc()
            for c in range(KC):
                sq = aB.alloc(TG, "sq")
                act(sq.ap, xT3[:, c, t0:t0 + TG], AF.Square, r=[xbuf[c][g]], w=[sq.buf])
                mm(ps[:, :], ones, sq.ap, c == 0, c == KC - 1, r=[sq.buf, cBb], w=pb)
                aB.release(sq)
            rstd = aF.alloc(TG, "rstd")
            act(rstd.ap, ps[:, :], AF.Sqrt, r=[pb, cFb], w=[rstd.buf], bias=epsc, scale=1.0 / D)
            recip(rstd.ap, rstd.ap, r=[rstd.buf], w=[rstd.buf])
            psrel(pi)
            for c in range(KC):
                o = aF.alloc(TG, "o")
                stt(o.ap, xT3[:, c, t0:t0 + TG], par[:, P_GFIN + c:P_GFIN + c + 1], rstd.ap, ALU.mult, ALU.mult,
                    r=[xbuf[c][g], parb, rstd.buf], w=[o.buf])
                dma('sp', osrc[:, c, t0:t0 + TG], o.ap, 'out', r=[o.buf])
                aF.release(o)
            aF.release(rstd)
        else:
            for c in range(KC):
                dma('sp', osrc[:, c, t0:t0 + TG], xT3[:, c, t0:t0 + TG], 'out', r=[xbuf[c][g]])
    S.emit(nc, es)
    es.close()
    stats = dict(n_ops={e: len(S.q[e]) for e in ENGS}, nwaits=S.nwaits, aF_peak=aF.peak, aB_peak=aB.peak)
    return nc, stats, dbg_outs


_CONSTS = None


def prep_inputs(inp, layers):
    global _CONSTS
    if _CONSTS is None:
        _CONSTS = make_consts()
    cf, cb, rot = _CONSTS
    par, bd = pack_params(inp)
    ls = list(layers)
    f = lambda k: np.ascontiguousarray(np.asarray(inp[k], np.float32)[ls])
    shared = dict(
        w_in=f("w_in"), w_ba=f("w_branch_attn"), w_br=f("w_branch_ret"), w_bp=f("w_branch_pool"),
        w_out=f("w_out"), w_up=f("w_up"), w_dn=f("w_down"), w_pg=f("w_ple_gate"), w_pp=f("w_ple_proj"),
        par=par, bd=bd, cf=cf, cb=cb, rot=rot,
    )
    return shared


def run_layers(xT_all, inp, layers, final_norm, ngrun=NG, dbg=False, ncores=8):
    nc, stats, dbg_outs = build_program(layers, final_norm, ngrun, dbg)
    shared = prep_inputs(inp, layers)
    p = np.asarray(inp["p"], np.float32)
    in_maps = []
    for b in range(ncores):
        m = dict(shared)
        m["xT"] = np.ascontiguousarray(xT_all[b])
        m["pT"] = np.ascontiguousarray(p[list(layers), b].transpose(0, 2, 1))
        in_maps.append(m)
    res = run_bass_kernel_spmd(nc, in_maps, core_ids=list(range(ncores)))
    outs = np.stack([np.asarray(r["outT"]) for r in res.results], axis=0)
    return outs, res, stats


def kernel(**inputs):
    x = np.asarray(inputs["x"], np.float32)
    xT = np.ascontiguousarray(x.transpose(0, 2, 1))
    outT, _, _ = run_layers(xT, inputs, range(L), True)
    return np.ascontiguousarray(outT.transpose(0, 2, 1)).astype(np.float32)
```

```python
import numpy as np
from collections import deque
from contextlib import ExitStack
import concourse.bass as bass
import concourse.mybir as mybir
from concourse.bass_utils import run_bass_kernel_spmd

F32 = mybir.dt.float32
BF16 = mybir.dt.bfloat16
AF = mybir.ActivationFunctionType
ALU = mybir.AluOpType
AX = mybir.AxisListType

ENGS = ('pe', 'act', 'dve', 'pool', 'sp')

T = 2048
D = 1024
L = 4
TG = 512
NG = T // TG
KC = D // 128
FF = 2816
INW = 5632
EPS = 1e-6
NW = 4


class Buf:
    __slots__ = ('name', 'w', 'r', 'rd', 'excl')

    def __init__(self, name='', excl=False):
        self.name = name
        self.excl = excl
        self.w = None
        self.r = {}
        self.rd = []


class Op:
    __slots__ = ('eng', 'fn', 'deps', 'idx', 'inc', 'semval', 'dma_key', 'dma_val', 'dwait', 'ep')


class Sched:
    def __init__(self):
        self.q = {e: [] for e in ENGS}
        self.dma_cnt = {}
        self.epoch = 0

    def _track(self, op, reads, writes):
        deps = []
        for b in reads:
            if b.w is not None:
                deps.append(b.w)
            if b.excl:
                for e_, o_ in b.r.items():
                    if e_ != op.eng:
                        deps.append(o_)
        for b in writes:
            if b.w is not None:
                deps.append(b.w)
            deps.extend(b.r.values())
            deps.extend(b.rd)
        for b in reads:
            if op.dma_key is not None:
                b.rd.append(op)
            else:
                b.r[op.eng] = op
        for b in writes:
            b.w = op
            b.r = {}
            b.rd = []
        return deps

    def op(self, eng, fn, r=(), w=(), key=None):
        o = Op()
        o.eng = eng
        o.ep = self.epoch
        o.fn = fn
        o.inc = False
        o.semval = 0
        o.dma_key = key
        o.dma_val = 0
        if key is not None:
            self.dma_cnt[key] = self.dma_cnt.get(key, 0) + 1
            o.dma_val = 16 * self.dma_cnt[key]
        o.deps = self._track(o, r, w)
        o.dwait = {}
        for d in o.deps:
            if d.dma_key is not None:
                o.dwait[d.dma_key] = 16 * self.dma_cnt[d.dma_key] - (16 if d.dma_key == key else 0)
        o.idx = len(self.q[eng])
        self.q[eng].append(o)
        return o

    def emit(self, nc, es):
        for e in ENGS:
            for o in self.q[e]:
                for d in o.deps:
                    if d.dma_key is None and not (d.eng == o.eng == 'pe'):
                        d.inc = True
        semh = {}
        for e in ENGS:
            cnt = {}
            for o in self.q[e]:
                if o.dma_key is None and o.inc:
                    cnt[o.ep] = cnt.get(o.ep, 0) + 1
                    o.semval = cnt[o.ep]
                    if ('eng', e, o.ep) not in semh:
                        semh[('eng', e, o.ep)] = es.enter_context(nc.semaphore(f's_{e}_{o.ep}'))
        for k in self.dma_cnt:
            semh[('dma', k)] = es.enter_context(nc.semaphore('d_' + str(k)))
        self.nwaits = 0

        def run(eobj, ename):
            seen = {}
            for o in self.q[ename]:
                waits = {}
                for d in o.deps:
                    if d.dma_key is not None:
                        k = ('dma', d.dma_key)
                        v = o.dwait[d.dma_key]
                    else:
                        if d.eng == ename == 'pe':
                            continue
                        k = ('eng', d.eng, d.ep)
                        v = d.semval
                    if v > waits.get(k, 0):
                        waits[k] = v
                for k, v in waits.items():
                    if seen.get(k, 0) >= v:
                        continue
                    seen[k] = v
                    eobj.wait_ge(semh[k], v)
                    self.nwaits += 1
                ins = o.fn(eobj)
                if o.dma_key is not None:
                    ins.then_inc(semh[('dma', o.dma_key)], 16)
                elif o.inc:
                    ins.then_inc(semh[('eng', ename, o.ep)], 1)
            last = {}
            for o in self.q[ename]:
                if o.dma_key is not None:
                    last[o.dma_key] = max(last.get(o.dma_key, 0), o.dma_val)
            for k, v in last.items():
                if seen.get(('dma', k), 0) < v:
                    eobj.wait_ge(semh[('dma', k)], v)

        with nc.Block() as block:
            @block.tensor
            def _(e):
                run(e, 'pe')

            @block.scalar
            def _(e):
                run(e, 'act')

            @block.vector
            def _(e):
                run(e, 'dve')

            @block.gpsimd
            def _(e):
                run(e, 'pool')

            @block.sync
            def _(e):
                run(e, 'sp')


class Tile:
    __slots__ = ('ap', 'buf', 'lo', 'hi', 'arena')


class Arena:
    def __init__(self, tensor, ncols, name):
        self.t = tensor
        self.n = ncols
        self.free = [(0, ncols)]
        self.ghosts = []
        self.name = name
        self.peak = 0
        self.used = 0
        self.rover = 0

    def alloc(self, n, name='', rot=None):
        n0 = n
        n = (n + 31) // 32 * 32
        if rot is None:
            rot = n <= 640
        cand = [(lo, hi) for (lo, hi) in self.free if hi - lo >= n]
        pick = None
        for (lo, hi) in (cand if rot else []):
            if hi > self.rover and hi - max(lo, self.rover) >= n:
                pick = (lo, hi, max(lo, self.rover))
                break
        if pick is None and cand:
            pick = (cand[0][0], cand[0][1], cand[0][0])
        if pick is not None:
            flo, fhi, lo = pick
            self.free.remove((flo, fhi))
            if lo > flo:
                self.free.append((flo, lo))
            if lo + n < fhi:
                self.free.append((lo + n, fhi))
            self.free.sort()
            if rot:
                self.rover = lo + n
            for _ in (0,):
                t = Tile()
                t.lo, t.hi, t.arena = lo, lo + n, self
                t.buf = Buf(name)
                t.ap = self.t[:, lo:lo + n0]
                keep = []
                for (glo, ghi, ops) in self.ghosts:
                    if glo < t.hi and ghi > t.lo:
                        for o in ops:
                            if o.dma_key is not None:
                                t.buf.rd.append(o)
                            else:
                                p = t.buf.r.get(o.eng)
                                if p is None or p.idx < o.idx:
                                    t.buf.r[o.eng] = o
                        if glo >= t.lo and ghi <= t.hi:
                            continue
                    keep.append((glo, ghi, ops))
                self.ghosts = keep
                self.used += n
                self.peak = max(self.peak, self.used)
                return t
        raise RuntimeError(f"arena {self.name} out of space for {name} n={n} used={self.used} free={self.free}")

    def release(self, t):
        ops = list(t.buf.r.values()) + list(t.buf.rd)
        if t.buf.w is not None:
            ops.append(t.buf.w)
        self.ghosts.append((t.lo, t.hi, ops))
        self.used -= (t.hi - t.lo)
        fl = self.free + [(t.lo, t.hi)]
        fl.sort()
        out = []
        for lo, hi in fl:
            if out and out[-1][1] == lo:
                out[-1] = (out[-1][0], hi)
            else:
                out.append((lo, hi))
        self.free = out


def v3(ap, b):
    return ap.rearrange("p (a b) -> p a b", b=b)


C_DEC = 0
C_XI = C_DEC + 512
C_ZS = C_XI + 256
C_COS = C_ZS + 4
C_SIN = C_COS + 512
C_RC = C_SIN + 512
C_RCN = C_RC + 2
C_EPS = C_RCN + 32
C_NEG = C_EPS + 1
NCF = C_NEG + 32
B_ID = 0
B_TRI = 128
B_ONE = 256
B_E64 = 384
NCB = B_E64 + 1024
P_GMIX = 0
P_GFFN = P_GMIX + L * 8
P_GPLE = P_GFFN + L * 8
P_GFIN = P_GPLE + L * 8
P_CW = P_GFIN + 8
P_CB = P_CW + L * 3 * 44
P_PS = P_CB + L * 44
NPAR = P_PS + L * 2


def make_consts():
    f64 = np.float64
    H = 4
    lg = np.log1p(-np.exp2(-5.0 - np.arange(H, dtype=f64)))
    cf = np.zeros((128, NCF), np.float32)
    i = np.arange(128)
    rel = i[None, :] - i[:, None]
    dec = np.zeros((128, H, 128), f64)
    for h in range(H):
        dec[:, h, :] = np.where(rel >= 0, np.exp(np.maximum(rel, 0) * lg[h]), 0.0) * 0.125
    cf[:, C_DEC:C_DEC + 512] = dec.reshape(128, 512)
    xi = np.zeros((128, 2, 128), f64)
    for p in range(128):
        for c in range(2):
            h = 2 * c + p // 64
            xi[p, c, :] = np.exp((i + 1.0) * lg[h]) * 0.125
    cf[:, C_XI:C_XI + 256] = xi.reshape(128, 256)
    for h in range(H):
        cf[:, C_ZS + h] = np.exp((127.0 - i) * lg[h])
    half = 32
    inv_freq = (np.float32(10000.0) ** (-(np.arange(half, dtype=np.float32)) / np.float32(half))).astype(np.float32)
    pos = np.arange(T, dtype=np.float32)
    ang = (pos[:, None] * inv_freq[None, :]).astype(np.float32).astype(f64)
    cos = np.cos(ang)
    sin = np.sin(ang)
    cf[:, C_COS:C_COS + 512] = cos.reshape(16, 128, 32).transpose(1, 0, 2).reshape(128, 512)
    cf[:, C_SIN:C_SIN + 512] = sin.reshape(16, 128, 32).transpose(1, 0, 2).reshape(128, 512)
    wins = [2, 4, 8, 16]
    for p in range(128):
        for c in range(2):
            w = wins[2 * c + p // 64]
            cf[p, C_RC + c] = 1.0 / w
            for t in range(16):
                cf[p, C_RCN + c * 16 + t] = 1.0 / min(t + 1, w)
    cf[:, C_EPS] = EPS
    cf[:, C_NEG:C_NEG + 32] = -1e30
    cb = np.zeros((128, NCB), np.float32)
    cb[:, B_ID:B_ID + 128] = np.eye(128)
    cb[:, B_TRI:B_TRI + 128] = (rel >= 0).astype(np.float32)
    cb[:, B_ONE:B_ONE + 128] = 1.0
    for p in range(128):
        n = p % 64
        if n < 8:
            cb[p, B_E64 + n * 128:B_E64 + (n + 1) * 128] = -30000.0
    rot = np.zeros((2, 128, T), np.float32)
    for p in range(128):
        rot[0, p, :] = cos[:, p % 32]
        rot[1, p, :] = sin[:, p % 32]
    return cf, cb, rot


def pack_params(inp):
    par = np.zeros((128, NPAR), np.float32)

    def fm(v):
        v = np.asarray(v, np.float32)
        lead = v.shape[:-1]
        c = v.shape[-1] // 128
        return np.moveaxis(v.reshape(lead + (c, 128)), -1, 0)

    par[:, P_GMIX:P_GMIX + L * 8] = fm(inp["norm_mix_g"]).reshape(128, -1)
    par[:, P_GFFN:P_GFFN + L * 8] = fm(inp["norm_ffn_g"]).reshape(128, -1)
    par[:, P_GPLE:P_GPLE + L * 8] = fm(inp["norm_ple_g"]).reshape(128, -1)
    par[:, P_GFIN:P_GFIN + 8] = fm(inp["norm_final_g"]).reshape(128, -1)
    par[:, P_CW:P_CW + L * 3 * 44] = fm(inp["conv_w"]).reshape(128, -1)
    par[:, P_CB:P_CB + L * 44] = fm(inp["conv_b"]).reshape(128, -1)
    par[:, P_PS:P_PS + L * 2] = fm(inp["pool_scale"]).reshape(128, -1)
    pw = np.asarray(inp["pool_w"], np.float32)
    bd = np.zeros((128, L, 2, 128), np.float32)
    for l in range(L):
        for c in range(2):
            for gl in range(2):
                bd[gl * 64:(gl + 1) * 64, l, c, gl * 64:(gl + 1) * 64] = pw[l, 2 * c + gl]
    return par, bd.reshape(128, L * 2 * 128)


class StopBuild(Exception):
    pass


STOP_AT = [None]
SKIP = {}


def build_program(layers, final_norm, ngrun=NG, dbg=False):
    nc = bass.Bass("TRN2", target_bir_lowering=False)
    es = ExitStack()
    S = Sched()
    NL = len(layers)

    def din(name, shape):
        return nc.dram_tensor(name, shape, F32, kind="ExternalInput").ap()

    xT_d = din("xT", [D, T])
    pT_d = din("pT", [NL, 256, T])
    w_in_d = din("w_in", [NL, D, INW])
    w_ba_d = din("w_ba", [NL, 256, D])
    w_br_d = din("w_br", [NL, 512, D])
    w_bp_d = din("w_bp", [NL, 256, D])
    w_out_d = din("w_out", [NL, D, D])
    w_up_d = din("w_up", [NL, D, 2 * FF])
    w_dn_d = din("w_dn", [NL, FF, D])
    w_pg_d = din("w_pg", [NL, D, D])
    w_pp_d = din("w_pp", [NL, 256, D])
    par_d = din("par", [128, NPAR])
    bd_d = din("bd", [128, L * 256])
    cf_d = din("cf", [128, NCF])
    cb_d = din("cb", [128, NCB])
    rot_d = din("rot", [2, 128, T])
    out_d = nc.dram_tensor("outT", [D, T], F32, kind="ExternalOutput").ap()
    dbg_outs = {}

    def sb(name, shape, dt):
        return es.enter_context(nc.sbuf_tensor(name, shape, dt))

    xT = sb("xT_sb", [128, KC * T], F32)
    xT3 = v3(xT[:], T)
    xbuf = [[Buf(f"x{c}_{g}") for g in range(NG)] for c in range(KC)]
    kaT = sb("kaT", [128, 2 * T], BF16)
    kaT3 = v3(kaT[:], T)
    kabuf = [Buf(f"ka{g}") for g in range(NG)]
    vaS = sb("vaS", [128, 16 * 256], BF16)
    va3 = v3(vaS[:], 256)
    vabuf = [Buf(f"va{g}") for g in range(NG)]
    kmT = sb("kmT", [128, 2 * 16], BF16)
    kmT3 = v3(kmT[:], 16)
    kmbuf = Buf("km")
    Rst = sb("Rst", [128, 4 * 128], F32)
    Rb = sb("Rb", [128, 4 * 128], BF16)
    Rstbuf = [Buf(f"Rst{h}") for h in range(4)]
    Rbbuf = [Buf(f"Rb{h}") for h in range(4)]
    convh = sb("convh", [128, 44 * 2], F32)
    convh3 = v3(convh[:], 2)
    convhbuf = [Buf(f"ch{i}") for i in range(44)]
    ubuf_t = sb("ubuf", [128, 2 * 528], F32)
    u3 = v3(ubuf_t[:], 528)
    ubuf = [Buf("u0"), Buf("u1")]
    wslot = [sb(f"wslot{i}", [128, 8 * 512], BF16) for i in range(NW)]
    wbuf = [Buf(f"w{i}") for i in range(NW)]
    cF = sb("cF", [128, NCF], F32)
    cFb = Buf("cF")
    cB = sb("cB", [128, NCB], BF16)
    cBb = Buf("cB")
    par = sb("par_sb", [128, NPAR], F32)
    parb = Buf("par")
    bdS = sb("bdS", [128, L * 256], BF16)
    bdb = Buf("bd")
    AFC = 4352
    ABC = 20480
    aF_t = sb("arenaF", [128, AFC], F32)
    aB_t = sb("arenaB", [128, ABC], BF16)
    aF = Arena(aF_t, AFC, "F")
    aB = Arena(aB_t, ABC, "B")
    psb = []
    for i in range(8):
        t = es.enter_context(nc.psum_tensor(f"ps{i}", [128, 512], F32))
        psb.append((t, Buf(f"ps{i}", excl=True)))
    psfree = deque(range(8))

    def psalloc():
        i = psfree.popleft()
        return i, psb[i][0], psb[i][1]

    def psrel(i):
        psfree.append(i)

    def mm(ps_ap, lhsT, rhs, start, stop, r, w):
        S.op('pe', lambda e, a=ps_ap, b=lhsT, c=rhs, s=start, t=stop: e.matmul(a, b, c, start=s, stop=t), r=r, w=[w])

    def act(out, in_, func, r, w, bias=None, scale=None):
        kw = {}
        if bias is not None:
            kw['bias'] = bias
        if scale is not None:
            kw['scale'] = scale
        S.op('act', lambda e, o=out, i=in_, f=func, kw=kw: e.activation(out=o, in_=i, func=f, **kw), r=r, w=w)

    def tt(out, in0, in1, op, r, w, eng='dve'):
        S.op(eng, lambda e, o=out, a=in0, b=in1, p=op: e.tensor_tensor(out=o, in0=a, in1=b, op=p), r=r, w=w)

    def ts(out, in0, s1, s2, op0, op1, r, w, eng='dve'):
        if op1 is None:
            S.op(eng, lambda e, o=out, a=in0, x=s1, p=op0: e.tensor_scalar(out=o, in0=a, scalar1=x, scalar2=None, op0=p), r=r, w=w)
        else:
            S.op(eng, lambda e, o=out, a=in0, x=s1, y=s2, p=op0, q=op1: e.tensor_scalar(out=o, in0=a, scalar1=x, scalar2=y, op0=p, op1=q), r=r, w=w)

    def stt(out, in0, scalar, in1, op0, op1, r, w):
        S.op('dve', lambda e, o=out, a=in0, s=scalar, b=in1, p=op0, q=op1: e.scalar_tensor_tensor(out=o, in0=a, scalar=s, in1=b, op0=p, op1=q), r=r, w=w)

    def cp(out, in_, r, w, eng='dve'):
        S.op(eng, lambda e, o=out, i=in_: e.tensor_copy(out=o, in_=i), r=r, w=w)

    def recip(out, in_, r, w):
        S.op('dve', lambda e, o=out, i=in_: e.reciprocal(out=o, in_=i), r=r, w=w)

    def memset(ap, val, w, eng='dve'):
        S.op(eng, lambda e, a=ap, v=val: e.memset(a, v), w=w)

    def dma(eng, out, in_, key, r=(), w=()):
        S.op(eng, lambda e, o=out, i=in_: e.dma_start(out=o, in_=i), r=r, w=w, key=key)

    wctr = [0]

    def wload(src, P, kc, ncols):
        i = wctr[0] % NW
        wctr[0] += 1
        view = wslot[i][0:P, 0:kc * ncols].rearrange("p (k n) -> p k n", n=ncols)
        dma('pool', view, src, f'w{i}', w=[wbuf[i]])
        return view, wbuf[i]

    def dump(name, ap, buf, shape, dt=F32):
        if not dbg:
            return
        d = nc.dram_tensor(name, shape, dt, kind="ExternalOutput").ap()
        dbg_outs[name] = d
        dma('sp', d, ap, 'dbg', r=[buf])

    dma('sp', cF[:], cf_d[:, :], 'c0', w=[cFb])
    dma('sp', par[:], par_d[:, :], 'c1', w=[parb])
    dma('pool', cB[:], cb_d[:, :], 'c2', w=[cBb])
    dma('pool', bdS[:], bd_d[:, :], 'c3', w=[bdb])
    xsrc = xT_d.rearrange("(c p) t -> p c t", p=128)
    for c in range(KC):
        dma('sp', xT3[:, c, :], xsrc[:, c, :], f'x{c}', w=[xbuf[c][g] for g in range(NG)])
    memset(kmT[:], 0.0, w=[kmbuf])
    ident = cB[:, B_ID:B_ID + 128]
    tri = cB[:, B_TRI:B_TRI + 128]
    ones = cB[:, B_ONE:B_ONE + 128]
    E64 = v3(cB[:, B_E64:B_E64 + 1024], 128)
    decT = cF[:, C_DEC:C_DEC + 512]
    xiT = v3(cF[:, C_XI:C_XI + 256], 128)
    zs = cF[:, C_ZS:C_ZS + 4]
    costm = v3(cF[:, C_COS:C_COS + 512], 32)
    sintm = v3(cF[:, C_SIN:C_SIN + 512], 32)
    epsc = cF[:, C_EPS:C_EPS + 1]
    gam = [float(np.exp(128.0 * np.log1p(-np.exp2(-5.0 - h)))) for h in range(4)]

    def rmsnorm(g, gcol):
        t0 = g * TG
        pi, ps, pb = psalloc()
        for c in range(KC):
            sq = aB.alloc(TG, "sq")
            act(sq.ap, xT3[:, c, t0:t0 + TG], AF.Square, r=[xbuf[c][g]], w=[sq.buf])
            mm(ps[:, :], ones, sq.ap, c == 0, c == KC - 1, r=[sq.buf, cBb], w=pb)
            aB.release(sq)
        rstd = aF.alloc(TG, "rstd")
        act(rstd.ap, ps[:, :], AF.Sqrt, r=[pb, cFb], w=[rstd.buf], bias=epsc, scale=1.0 / D)
        recip(rstd.ap, rstd.ap, r=[rstd.buf], w=[rstd.buf])
        psrel(pi)
        hT = aB.alloc(KC * TG, "hT")
        h3 = v3(hT.ap, TG)
        hb = [Buf(f"h{c}") for c in range(KC)]
        for c in range(KC):
            for e_, o_ in hT.buf.r.items():
                hb[c].r[e_] = o_
            hb[c].rd = list(hT.buf.rd)
            stt(h3[:, c, :], xT3[:, c, t0:t0 + TG], par[:, gcol + c:gcol + c + 1], rstd.ap, ALU.mult, ALU.mult,
                r=[xbuf[c][g], parb, rstd.buf], w=[hb[c]])
        aF.release(rstd)
        return hT, h3, hb

    def merge_bufs(tile, bufs):
        for b_ in bufs:
            for e_, o_ in b_.r.items():
                p_ = tile.buf.r.get(e_)
                if p_ is None or p_.idx < o_.idx:
                    tile.buf.r[e_] = o_
            tile.buf.rd.extend(b_.rd)
            if b_.w is not None:
                p_ = tile.buf.r.get(b_.w.eng)
                if p_ is None or p_.idx < b_.w.idx:
                    tile.buf.r[b_.w.eng] = b_.w

    def ckpt(name):
        if STOP_AT[0] == name:
            raise StopBuild()

    try:
      for li, l in enumerate(layers):
          wi = w_in_d[li].rearrange("(k p) n -> p k n", p=128)
          for h in range(4):
              hb = 64 * (h % 2)
              memset(Rst[hb:hb + 64, h * 128:(h + 1) * 128], 0.0, w=[Rstbuf[h]])
              memset(Rb[hb:hb + 64, h * 128:(h + 1) * 128], 0.0, w=[Rbbuf[h]])
          for i in range(44):
              memset(convh3[:, i, :], 0.0, w=[convhbuf[i]])
          for c in range(2):
              memset(u3[:, c, 0:16], 0.0, w=[ubuf[c]])
          for g in range(ngrun):
              t0 = g * TG
              S.epoch += 1
              hT, h3, hb = rmsnorm(g, P_GMIX + l * 8)
              if dbg and li == 0 and g == 0:
                  dump("d_h", hT.ap, hT.buf, [128, KC * TG], BF16)
              ckpt('norm')
              wv, wb = wload(wi[:, :, 0:512], 128, 8, 512)
              qaT = aB.alloc(2 * TG, "qaT")
              qa3 = v3(qaT.ap, TG)
              for c in range(2):
                  pi, ps, pb = psalloc()
                  for k in range(KC):
                      mm(ps[:, :], wv[:, k, c * 128:(c + 1) * 128], h3[:, k, :], k == 0, k == KC - 1, r=[wb, hb[k]], w=pb)
                  act(qa3[:, c, :], ps[:, :], AF.Identity, r=[pb], w=[qaT.buf])
                  psrel(pi)
              ckpt('w0')
              for c in range(2):
                  pi, ps, pb = psalloc()
                  for k in range(KC):
                      mm(ps[:, :], wv[:, k, 256 + c * 128:256 + (c + 1) * 128], h3[:, k, :], k == 0, k == KC - 1, r=[wb, hb[k]], w=pb)
                  act(kaT3[:, c, t0:t0 + TG], ps[:, :], AF.Identity, r=[pb], w=[kabuf[g]])
                  if SKIP.get('km'):
                      psrel(pi)
                      continue
                  km = aF.alloc(2, "km")
                  S.op('dve', lambda e, o=km.ap, i=v3(kaT3[:, c, t0:t0 + TG], 256): e.tensor_reduce(out=o, in_=i, axis=AX.X, op=ALU.add),
                       r=[kabuf[g]], w=[km.buf])
                  ts(kmT3[0:64, c, 2 * g:2 * g + 2], km.ap[0:64, :], 1.0 / 256.0, None, ALU.mult, None, r=[km.buf], w=[kmbuf])
                  ts(kmT3[64:128, c, 8 + 2 * g:8 + 2 * g + 2], km.ap[64:128, :], 1.0 / 256.0, None, ALU.mult, None, r=[km.buf], w=[kmbuf])
                  aF.release(km)
                  psrel(pi)
              ckpt('ka')
              wv, wb = wload(wi[:, :, 512:768], 128, 8, 256)
              for tl in range(4):
                  pi, ps, pb = psalloc()
                  for k in range(KC):
                      mm(ps[:, 0:256], h3[:, k, tl * 128:(tl + 1) * 128], wv[:, k, :], k == 0, k == KC - 1, r=[wb, hb[k]], w=pb)
                  act(va3[:, 4 * g + tl, :], ps[:, 0:256], AF.Identity, r=[pb], w=[vabuf[g]])
                  psrel(pi)
              if dbg and li == 0 and g == 0:
                  dump("d_qa", qaT.ap, qaT.buf, [128, 2 * TG], BF16)
              ckpt('qkv')
              nbT = None
              if g >= 2 and not SKIP.get('sel'):
                  nbT = [aB.alloc(TG, "nbT0"), aB.alloc(TG, "nbT1")]
                  gi, gps, gpb = psalloc()
                  for qt in range(4):
                      for c in range(2):
                          mm(gps[:, qt * 32 + c * 16:qt * 32 + c * 16 + 16], qa3[:, c, qt * 128:(qt + 1) * 128],
                             kmT3[:, c, :], True, True, r=[qaT.buf, kmbuf], w=gpb)
                  ckpt('sel1')
                  for qt in range(4):
                      b = 2 * g + qt // 2
                      gsb = aF.alloc(32, "gsb")
                      m8 = aF.alloc(32, "m8")
                      cp(gsb.ap, cF[:, C_NEG:C_NEG + 32], r=[cFb], w=[gsb.buf])
                      cp(v3(gsb.ap, 8)[:, :, 0:b], v3(gps[:, qt * 32:(qt + 1) * 32], 8)[:, :, 0:b], r=[gpb], w=[gsb.buf])
                      for h in range(4):
                          S.op('dve', lambda e, o=m8.ap[:, h * 8:(h + 1) * 8], i=gsb.ap[:, h * 8:(h + 1) * 8]: e.max(out=o, in_=i),
                               r=[gsb.buf], w=[m8.buf])
                      negb = aB.alloc(256, "negb")
                      memset(negb.ap, 0.0, w=[negb.buf])
                      tt(negb.ap.rearrange("p (h e) -> p h e", e=64)[:, :, 0:8], v3(gsb.ap, 8),
                         v3(m8.ap, 8)[:, :, 2:3].to_broadcast([128, 4, 8]), ALU.is_lt, r=[gsb.buf, m8.buf], w=[negb.buf])
                      ckpt('sel2')
                      for tl in range(2):
                          pi, ps, pb = psalloc()
                          mm(ps[:, 0:128], negb.ap[:, tl * 128:(tl + 1) * 128], ident, True, True, r=[negb.buf, cBb], w=pb)
                          act(nbT[tl].ap[:, qt * 128:(qt + 1) * 128], ps[:, 0:128], AF.Identity, r=[pb], w=[nbT[tl].buf])
                          psrel(pi)
                      aB.release(negb)
                      aF.release(gsb)
                      aF.release(m8)
                      ckpt('sel3')
                  psrel(gi)
              attnT = aB.alloc(4 * TG, "attnT")
              at3 = v3(attnT.ap, TG)
              nkt = 4 * g + 4
              SK = 2
              acc = {}
              pend = deque()

              def do_pv(h, kt, pt, q0):
                  if h not in acc:
                      acc[h] = psalloc() + psalloc()
                  oi, ops_, opb, li_, lps, lpb = acc[h]
                  mm(ops_[0:64, q0:TG], va3[:, kt, h * 64:(h + 1) * 64], pt.ap[:, q0:TG], kt == 0, kt == nkt - 1,
                     r=[vabuf[kt // 4], pt.buf], w=opb)
                  mm(lps[0:64, q0:TG], ones[:, 0:64], pt.ap[:, q0:TG], kt == 0, kt == nkt - 1, r=[cBb, pt.buf], w=lpb)
                  aB.release(pt)
                  if kt == nkt - 1:
                      rc_ = aF.alloc(TG, "rcp")
                      recip(rc_.ap[0:64, :], lps[0:64, :], r=[lpb], w=[rc_.buf])
                      tt(at3[0:64, h, :], ops_[0:64, :], rc_.ap[0:64, :], ALU.mult, r=[opb, rc_.buf], w=[attnT.buf])
                      aF.release(rc_)
                      psrel(oi)
                      psrel(li_)

              for h in range(4):
                  c, base = h // 2, 64 * (h % 2)
                  for kt in range(nkt):
                      r_ = kt - 4 * g
                      q0 = 128 * r_ if r_ > 0 else 0
                      n = kt // 2
                      bias_c0 = None
                      if g >= 2 and not SKIP.get('sel') and not SKIP.get('bias'):
                          if n < 2 * g:
                              bias_c0 = 0
                          elif n == 2 * g:
                              bias_c0 = 256
                      pi, ps, pb = psalloc()
                      mm(ps[:, q0:TG], kaT3[base:base + 64, c, kt * 128:(kt + 1) * 128], qa3[base:base + 64, c, q0:TG],
                         True, bias_c0 is None, r=[kabuf[kt // 4], qaT.buf], w=pb)
                      if bias_c0 is not None:
                          tl, sl = h // 2, 64 * (h % 2)
                          c0 = max(bias_c0, q0)
                          mm(ps[:, c0:TG], E64[sl:sl + 64, n, :], nbT[tl].ap[sl:sl + 64, c0:TG], False, True,
                             r=[cBb, nbT[tl].buf], w=pb)
                      pt = aB.alloc(TG, "pt")
                      act(pt.ap[:, q0:TG], ps[:, q0:TG], AF.Exp, r=[pb], w=[pt.buf], scale=0.125)
                      psrel(pi)
                      if r_ >= 0:
                          tt(pt.ap[:, q0:q0 + 128], pt.ap[:, q0:q0 + 128], tri, ALU.mult, r=[pt.buf, cBb], w=[pt.buf])
                      pend.append((h, kt, pt, q0))
                      if len(pend) > SK:
                          do_pv(*pend.popleft())
              while pend:
                  do_pv(*pend.popleft())
              aB.release(qaT)
              if nbT is not None:
                  aB.release(nbT[0])
                  aB.release(nbT[1])
              if dbg and li == 0 and g == 0:
                  dump("d_attn", attnT.ap, attnT.buf, [128, 4 * TG], BF16)
              ckpt('attn')
              wv, wb = wload(wi[:, :, 768:1280], 128, 8, 512)
              wrot = aB.alloc(8 * 512, "wrot")
              wr3 = v3(wrot.ap, 512)
              w5 = wv.rearrange("p k (h t f) -> p k h t f", t=2, f=32)
              r5 = wr3.rearrange("p k (h t f) -> p k h t f", t=2, f=32)
              for k in range(KC):
                  S.op('act', lambda e, o=r5[:, k, :, 0, :], i=w5[:, k, :, 1, :]: e.mul(out=o, in_=i, mul=-1.0), r=[wb], w=[wrot.buf])
                  cp(r5[:, k, :, 1, :], w5[:, k, :, 0, :], r=[wb], w=[wrot.buf])
              rt = aF.alloc(2 * TG, "rot")
              rt3 = v3(rt.ap, TG)
              dma('sp', rt3[:, 0, :], rot_d[0, :, t0:t0 + TG], 'rot', w=[rt.buf])
              dma('sp', rt3[:, 1, :], rot_d[1, :, t0:t0 + TG], 'rot', w=[rt.buf])
              qk = aB.alloc(4 * TG, "qk")
              qk3 = v3(qk.ap, TG)
              qx = aB.alloc(2 * TG, "qx")
              qx3 = v3(qx.ap, TG)
              for c4 in range(4):
                  pi, ps, pb = psalloc()
                  pj, ps2, pb2 = psalloc()
                  for k in range(KC):
                      mm(ps[:, :], wv[:, k, c4 * 128:(c4 + 1) * 128], h3[:, k, :], k == 0, k == KC - 1, r=[wb, hb[k]], w=pb)
                  for k in range(KC):
                      mm(ps2[:, :], wr3[:, k, c4 * 128:(c4 + 1) * 128], h3[:, k, :], k == 0, k == KC - 1, r=[wrot.buf, hb[k]], w=pb2)
                  t1 = aF.alloc(TG, "t1")
                  t2 = aF.alloc(TG, "t2")
                  tt(t1.ap, ps[:, :], rt3[:, 0, :], ALU.mult, r=[pb, rt.buf], w=[t1.buf])
                  tt(t2.ap, ps2[:, :], rt3[:, 1, :], ALU.mult, r=[pb2, rt.buf], w=[t2.buf])
                  psrel(pi)
                  psrel(pj)
                  tt(qk3[:, c4, :], t1.ap, t2.ap, ALU.add, r=[t1.buf, t2.buf], w=[qk.buf])
                  if c4 < 2:
                      tt(t1.ap, t1.ap, t2.ap, ALU.add, r=[t1.buf, t2.buf], w=[t1.buf])
                      tt(v3(qx3[:, c4, :], 128), v3(t1.ap, 128), xiT[:, c4, :].unsqueeze(1).to_broadcast([128, 4, 128]), ALU.mult,
                         r=[t1.buf, cFb], w=[qx.buf])
                  aF.release(t1)
                  aF.release(t2)
              aF.release(rt)
              kt_t = aB.alloc(4 * 256, "ktm")
              ktm3 = v3(kt_t.ap, 256)
              for tl in range(4):
                  pi, ps, pb = psalloc()
                  for k in range(KC):
                      mm(ps[:, 0:256], h3[:, k, tl * 128:(tl + 1) * 128], wv[:, k, 256:512], k == 0, k == KC - 1, r=[wb, hb[k]], w=pb)
                  for k in range(KC):
                      mm(ps[:, 256:512], h3[:, k, tl * 128:(tl + 1) * 128], wr3[:, k, 256:512], k == 0, k == KC - 1, r=[wrot.buf, hb[k]], w=pb)
                  t1 = aF.alloc(256, "k1")
                  t2 = aF.alloc(256, "k2")
                  cb_ = costm[:, 4 * g + tl, :].unsqueeze(1).to_broadcast([128, 8, 32])
                  sb_ = sintm[:, 4 * g + tl, :].unsqueeze(1).to_broadcast([128, 8, 32])
                  tt(v3(t1.ap, 32), v3(ps[:, 0:256], 32), cb_, ALU.mult, r=[pb, cFb], w=[t1.buf])
                  tt(v3(t2.ap, 32), v3(ps[:, 256:512], 32), sb_, ALU.mult, r=[pb, cFb], w=[t2.buf])
                  psrel(pi)
                  tt(t1.ap, t1.ap, t2.ap, ALU.add, r=[t1.buf, t2.buf], w=[t1.buf])
                  tt(v3(ktm3[:, tl, :], 64), v3(t1.ap, 64), zs.unsqueeze(2).to_broadcast([128, 4, 64]), ALU.mult,
                     r=[t1.buf, cFb], w=[kt_t.buf])
                  aF.release(t1)
                  aF.release(t2)
              aB.release(wrot)
              wv, wb = wload(wi[:, :, 1280:1792], 128, 8, 512)
              vr = aB.alloc(4 * 512, "vr")
              vr3 = v3(vr.ap, 512)
              for tl in range(4):
                  pi, ps, pb = psalloc()
                  for k in range(KC):
                      mm(ps[:, :], h3[:, k, tl * 128:(tl + 1) * 128], wv[:, k, :], k == 0, k == KC - 1, r=[wb, hb[k]], w=pb)
                  act(vr3[:, tl, :], ps[:, :], AF.Identity, r=[pb], w=[vr.buf])
                  psrel(pi)
              wv, wb = wload(wi[:, :, 1792:2304], 128, 8, 512)
              sg = aB.alloc(4 * TG, "silug")
              sg3 = v3(sg.ap, TG)
              for c in range(4):
                  pi, ps, pb = psalloc()
                  for k in range(KC):
                      mm(ps[:, :], wv[:, k, c * 128:(c + 1) * 128], h3[:, k, :], k == 0, k == KC - 1, r=[wb, hb[k]], w=pb)
                  act(sg3[:, c, :], ps[:, :], AF.Silu, r=[pb], w=[sg.buf])
                  psrel(pi)
              retT = aB.alloc(4 * TG, "retT")
              rT3 = v3(retT.ap, TG)
              scs = []
              for h in range(4):
                  c, base = h // 2, 64 * (h % 2)
                  si, sps, spb = psalloc()
                  for r_ in range(4):
                      mm(sps[:, r_ * 128:(r_ + 1) * 128], qk3[base:base + 64, 2 + c, r_ * 128:(r_ + 1) * 128],
                         qk3[base:base + 64, c, r_ * 128:(r_ + 1) * 128], True, True, r=[qk.buf], w=spb)
                  sc = aB.alloc(TG, "sc")
                  tt(v3(sc.ap, 128), v3(sps[:, :], 128), decT[:, h * 128:(h + 1) * 128].unsqueeze(1).to_broadcast([128, 4, 128]),
                     ALU.mult, r=[spb, cFb], w=[sc.buf])
                  psrel(si)
                  scs.append(sc)
              ybanks = [psalloc() for _ in range(4)]
              for r_ in range(4):
                  cs = slice(r_ * 128, (r_ + 1) * 128)
                  for h in range(4):
                      c, base = h // 2, 64 * (h % 2)
                      yi, yps, ypb = ybanks[h]
                      sc = scs[h]
                      mm(yps[:, cs], vr3[:, r_, h * 128:(h + 1) * 128], sc.ap[:, cs], True, False, r=[vr.buf, sc.buf], w=ypb)
                      mm(yps[:, cs], Rb[base:base + 64, h * 128:(h + 1) * 128], qx3[base:base + 64, c, cs], False, True,
                         r=[Rbbuf[h], qx.buf], w=ypb)
                      ui, ups, upb = psalloc()
                      mm(ups[:, 0:128], ktm3[:, r_, c * 128:(c + 1) * 128], vr3[:, r_, h * 128:(h + 1) * 128], True, True,
                         r=[kt_t.buf, vr.buf], w=upb)
                      stt(Rst[base:base + 64, h * 128:(h + 1) * 128], Rst[base:base + 64, h * 128:(h + 1) * 128], gam[h],
                          ups[base:base + 64, 0:128], ALU.mult, ALU.add, r=[Rstbuf[h], upb], w=[Rstbuf[h]])
                      psrel(ui)
                      cp(Rb[base:base + 64, h * 128:(h + 1) * 128], Rst[base:base + 64, h * 128:(h + 1) * 128],
                         r=[Rstbuf[h]], w=[Rbbuf[h]])
              for sc in scs:
                  aB.release(sc)
              for hp in ((0, 1), (2, 3)):
                  st = {}
                  for h in hp:
                      yi, yps, ypb = ybanks[h]
                      yb = aB.alloc(TG, "yb")
                      ysq = aB.alloc(TG, "ysq")
                      act(yb.ap, yps[:, :], AF.Identity, r=[ypb], w=[yb.buf])
                      act(ysq.ap, yps[:, :], AF.Square, r=[ypb], w=[ysq.buf])
                      st[h] = dict(yb=yb, ysq=ysq)
                  for h in hp:
                      d_ = st[h]
                      d_['s1'] = psalloc()
                      d_['s2'] = psalloc()
                      mm(d_['s1'][1][:, :], ones, d_['yb'].ap, True, True, r=[cBb, d_['yb'].buf], w=d_['s1'][2])
                      mm(d_['s2'][1][:, :], ones, d_['ysq'].ap, True, True, r=[cBb, d_['ysq'].buf], w=d_['s2'][2])
                      aB.release(d_['yb'])
                      aB.release(d_['ysq'])
                  for h in hp:
                      d_ = st[h]
                      d_['mean'] = aF.alloc(TG, "mean")
                      d_['var'] = aF.alloc(TG, "var")
                      ts(d_['mean'].ap, d_['s1'][1][:, :], 1.0 / 128.0, None, ALU.mult, None, r=[d_['s1'][2]], w=[d_['mean'].buf])
                  for h in hp:
                      d_ = st[h]
                      tt(d_['var'].ap, d_['mean'].ap, d_['mean'].ap, ALU.mult, r=[d_['mean'].buf], w=[d_['var'].buf])
                  for h in hp:
                      d_ = st[h]
                      stt(d_['var'].ap, d_['s2'][1][:, :], 1.0 / 128.0, d_['var'].ap, ALU.mult, ALU.subtract,
                          r=[d_['s2'][2], d_['var'].buf], w=[d_['var'].buf])
                      psrel(d_['s1'][0])
                      psrel(d_['s2'][0])
                  for h in hp:
                      d_ = st[h]
                      ts(d_['var'].ap, d_['var'].ap, 0.0, None, ALU.max, None, r=[d_['var'].buf], w=[d_['var'].buf])
                  for h in hp:
                      d_ = st[h]
                      act(d_['var'].ap, d_['var'].ap, AF.Sqrt, r=[d_['var'].buf, cFb], w=[d_['var'].buf], bias=epsc, scale=1.0)
                  for h in hp:
                      d_ = st[h]
                      yi, yps, ypb = ybanks[h]
                      tt(d_['mean'].ap, yps[:, :], d_['mean'].ap, ALU.subtract, r=[ypb, d_['mean'].buf], w=[d_['mean'].buf])
                      psrel(yi)
                  for h in hp:
                      d_ = st[h]
                      recip(d_['var'].ap, d_['var'].ap, r=[d_['var'].buf], w=[d_['var'].buf])
                  for h in hp:
                      d_ = st[h]
                      tt(d_['mean'].ap, d_['mean'].ap, d_['var'].ap, ALU.mult, r=[d_['mean'].buf, d_['var'].buf], w=[d_['mean'].buf])
                  for h in hp:
                      d_ = st[h]
                      tt(rT3[:, h, :], d_['mean'].ap, sg3[:, h, :], ALU.mult, r=[d_['mean'].buf, sg.buf], w=[retT.buf])
                      aF.release(d_['mean'])
                      aF.release(d_['var'])
              for t_ in (qk, qx, kt_t, vr, sg):
                  aB.release(t_)
              if dbg and li == 0 and g == 0:
                  dump("d_ret", retT.ap, retT.buf, [128, 4 * TG], BF16)
              ckpt('ret')
              wv, wb = wload(wi[:, :, 2304:2560], 128, 8, 256)
              mixed = aB.alloc(2 * TG, "mixed")
              mx3 = v3(mixed.ap, TG)
              for c in range(2):
                  pi, ps, pb = psalloc()
                  for k in range(KC):
                      mm(ps[:, :], wv[:, k, c * 128:(c + 1) * 128], h3[:, k, :], k == 0, k == KC - 1, r=[wb, hb[k]], w=pb)
                  act(u3[:, c, 16:528], ps[:, :], AF.Identity, r=[pb], w=[ubuf[c]])
                  psrel(pi)
                  sa = aF.alloc(528, "sa")
                  sbb = aF.alloc(528, "sb")
                  tt(sa.ap[:, 1:528], u3[:, c, 1:528], u3[:, c, 0:527], ALU.add, r=[ubuf[c]], w=[sa.buf])
                  if c == 0:
                      tt(sbb.ap[64:128, 3:528], sa.ap[64:128, 3:528], sa.ap[64:128, 1:526], ALU.add, r=[sa.buf], w=[sbb.buf])
                      lo_src, hi_src = sa, sbb
                  else:
                      tt(sbb.ap[:, 3:528], sa.ap[:, 3:528], sa.ap[:, 1:526], ALU.add, r=[sa.buf], w=[sbb.buf])
                      tt(sa.ap[:, 7:528], sbb.ap[:, 7:528], sbb.ap[:, 3:524], ALU.add, r=[sbb.buf], w=[sa.buf])
                      tt(sbb.ap[64:128, 15:528], sa.ap[64:128, 15:528], sa.ap[64:128, 7:520], ALU.add, r=[sa.buf], w=[sbb.buf])
                      lo_src, hi_src = sa, sbb
                  rcs = cF[:, C_RC + c:C_RC + c + 1]
                  for (p0, p1, src) in ((0, 64, lo_src), (64, 128, hi_src)):
                      stt(mx3[p0:p1, c, :], src.ap[p0:p1, 16:528], rcs[p0:p1, :], u3[p0:p1, c, 16:528], ALU.mult, ALU.subtract,
                          r=[src.buf, ubuf[c], cFb], w=[mixed.buf])
                      if g == 0:
                          tf = aF.alloc(16, "tf")
                          tt(tf.ap[p0:p1, :], src.ap[p0:p1, 16:32], cF[p0:p1, C_RCN + c * 16:C_RCN + (c + 1) * 16], ALU.mult,
                             r=[src.buf, cFb], w=[tf.buf])
                          tt(mx3[p0:p1, c, 0:15], tf.ap[p0:p1, 0:15], u3[p0:p1, c, 16:31], ALU.subtract, r=[tf.buf, ubuf[c]], w=[mixed.buf])
                          aF.release(tf)
                  aF.release(sa)
                  aF.release(sbb)
                  th = aF.alloc(16, "th")
                  cp(th.ap, u3[:, c, 512:528], r=[ubuf[c]], w=[th.buf])
                  cp(u3[:, c, 0:16], th.ap, r=[th.buf], w=[ubuf[c]])
                  aF.release(th)
              poolT = aB.alloc(2 * TG, "poolT")
              pl3 = v3(poolT.ap, TG)
              for c in range(2):
                  pi, ps, pb = psalloc()
                  mm(ps[:, :], bdS[:, (l * 2 + c) * 128:(l * 2 + c + 1) * 128], mx3[:, c, :], True, True, r=[bdb, mixed.buf], w=pb)
                  act(pl3[:, c, :], ps[:, :], AF.Identity, r=[pb, parb], w=[poolT.buf], scale=par[:, P_PS + l * 2 + c:P_PS + l * 2 + c + 1])
                  psrel(pi)
              aB.release(mixed)
              if dbg and li == 0 and g == 0:
                  dump("d_pool", poolT.ap, poolT.buf, [128, 2 * TG], BF16)
              ckpt('pool')
              merged = aB.alloc(KC * TG, "merged")
              mg3 = v3(merged.ap, TG)
              for ct in range(2):
                  macc = aF.alloc(4 * TG, "macc")
                  ma3 = v3(macc.ap, TG)
                  for br in range(3):
                      gv, gb = wload(wi[:, :, 2560 + br * 1024 + ct * 512:2560 + br * 1024 + (ct + 1) * 512], 128, 8, 512)
                      if br == 0:
                          src = w_ba_d[li].rearrange("(h p) n -> p h n", p=64)[:, :, ct * 512:(ct + 1) * 512]
                          bv, bb = wload(src, 64, 4, 512)
                          nk, bin_, bbuf, bp = 4, at3, attnT.buf, 64
                      elif br == 1:
                          src = w_br_d[li].rearrange("(h p) n -> p h n", p=128)[:, :, ct * 512:(ct + 1) * 512]
                          bv, bb = wload(src, 128, 4, 512)
                          nk, bin_, bbuf, bp = 4, rT3, retT.buf, 128
                      else:
                          src = w_bp_d[li].rearrange("(h p) n -> p h n", p=128)[:, :, ct * 512:(ct + 1) * 512]
                          bv, bb = wload(src, 128, 2, 512)
                          nk, bin_, bbuf, bp = 2, pl3, poolT.buf, 128
                      for j in range(4):
                          dc = ct * 4 + j
                          gi, gps, gpb = psalloc()
                          for k in range(KC):
                              mm(gps[:, :], gv[:, k, j * 128:(j + 1) * 128], h3[:, k, :], k == 0, k == KC - 1, r=[gb, hb[k]], w=gpb)
                          bi, bps, bpb = psalloc()
                          for k in range(nk):
                              mm(bps[:, :], bv[0:bp, k, j * 128:(j + 1) * 128], bin_[0:bp, k, :], k == 0, k == nk - 1, r=[bb, bbuf], w=bpb)
                          sig = aF.alloc(TG, "sig")
                          act(sig.ap, gps[:, :], AF.Sigmoid, r=[gpb], w=[sig.buf])
                          psrel(gi)
                          if br == 0:
                              tt(ma3[:, j, :], sig.ap, bps[:, :], ALU.mult, r=[sig.buf, bpb], w=[macc.buf])
                          else:
                              tt(sig.ap, sig.ap, bps[:, :], ALU.mult, r=[sig.buf, bpb], w=[sig.buf])
                              if br == 1:
                                  tt(ma3[:, j, :], ma3[:, j, :], sig.ap, ALU.add, r=[sig.buf, macc.buf], w=[macc.buf])
                              else:
                                  tt(mg3[:, dc, :], ma3[:, j, :], sig.ap, ALU.add, r=[sig.buf, macc.buf], w=[merged.buf])
                          psrel(bi)
                          aF.release(sig)
                  aF.release(macc)
              merge_bufs(hT, hb)
              for t_ in (attnT, retT, poolT, hT):
                  aB.release(t_)
              if dbg and li == 0 and g == 0:
                  dump("d_merged", merged.ap, merged.buf, [128, KC * TG], BF16)
              wo = w_out_d[li].rearrange("(k p) n -> p k n", p=128)
              for ct in range(2):
                  wv, wb = wload(wo[:, :, ct * 512:(ct + 1) * 512], 128, 8, 512)
                  for j in range(4):
                      dc = ct * 4 + j
                      pi, ps, pb = psalloc()
                      for k in range(KC):
                          mm(ps[:, :], wv[:, k, j * 128:(j + 1) * 128], mg3[:, k, :], k == 0, k == KC - 1, r=[wb, merged.buf], w=pb)
                      tt(xT3[:, dc, t0:t0 + TG], xT3[:, dc, t0:t0 + TG], ps[:, :], ALU.add, r=[xbuf[dc][g], pb], w=[xbuf[dc][g]])
                      psrel(pi)
              aB.release(merged)
              ckpt('mix')
              hT, h3, hb = rmsnorm(g, P_GFFN + l * 8)
              wu = w_up_d[li].rearrange("(k p) n -> p k n", p=128)
              mT = aB.alloc(22 * TG, "mT")
              m3 = v3(mT.ap, TG)
              mbuf = [Buf(f"m{i}") for i in range(22)]

              def ffn_final(items):
                  for (ch_, y_) in items:
                      if ch_ < 22:
                          act(m3[:, ch_, :], y_.ap, AF.Gelu_apprx_tanh, r=[y_.buf], w=[mbuf[ch_]])
                      else:
                          tt(m3[:, ch_ - 22, :], m3[:, ch_ - 22, :], y_.ap, ALU.mult, r=[y_.buf, mbuf[ch_ - 22]], w=[mbuf[ch_ - 22]])
                      aF.release(y_)

              prev = []
              for it in range(22):
                  if it % 2 == 0:
                      wv, wb = wload(wu[:, :, (it // 2) * 512:(it // 2 + 1) * 512], 128, 8, 512)
                  chs = [2 * it, 2 * it + 1]
                  pss = []
                  for ch in chs:
                      j = ch % 4
                      pi, ps, pb = psalloc()
                      for k in range(KC):
                          mm(ps[:, :], wv[:, k, j * 128:(j + 1) * 128], h3[:, k, :], k == 0, k == KC - 1, r=[wb, hb[k]], w=pb)
                      pss.append((pi, ps, pb))
                  Us = []
                  for ch, (pi, ps, pb) in zip(chs, pss):
                      U = aF.alloc(514, "U", rot=True)
                      act(U.ap[:, 2:514], ps[:, :], AF.Identity, r=[pb], w=[U.buf])
                      psrel(pi)
                      Us.append(U)
                  ffn_final(prev)
                  for ch, U in zip(chs, Us):
                      act(U.ap[:, 0:2], convh3[:, ch, :], AF.Identity, r=[convhbuf[ch]], w=[U.buf])
                      act(convh3[:, ch, :], U.ap[:, 512:514], AF.Identity, r=[U.buf], w=[convhbuf[ch]])
                  cw = lambda kk, ch: par[:, P_CW + (l * 3 + kk) * 44 + ch:P_CW + (l * 3 + kk) * 44 + ch + 1]
                  ys = []
                  for ch, U in zip(chs, Us):
                      y = aF.alloc(TG, "y", rot=True)
                      cbias = par[:, P_CB + l * 44 + ch:P_CB + l * 44 + ch + 1]
                      act(y.ap, U.ap[:, 2:514], AF.Identity, r=[U.buf, parb], w=[y.buf], bias=cbias, scale=cw(2, ch))
                      ys.append(y)
                  for ch, U, y in zip(chs, Us, ys):
                      stt(y.ap, U.ap[:, 1:513], cw(1, ch), y.ap, ALU.mult, ALU.add, r=[U.buf, y.buf, parb], w=[y.buf])
                  for ch, U, y in zip(chs, Us, ys):
                      stt(y.ap, U.ap[:, 0:512], cw(0, ch), y.ap, ALU.mult, ALU.add, r=[U.buf, y.buf, parb], w=[y.buf])
                  for U in Us:
                      aF.release(U)
                  prev = list(zip(chs, ys))
              ffn_final(prev)
              merge_bufs(hT, hb)
              aB.release(hT)
              wd = w_dn_d[li].rearrange("(k p) n -> p k n", p=128)
              for ct in range(2):
                  accs = [psalloc() for _ in range(4)]
                  for kg, (k0, k1) in enumerate(((0, 8), (8, 16), (16, 22))):
                      wv, wb = wload(wd[:, k0:k1, ct * 512:(ct + 1) * 512], 128, k1 - k0, 512)
                      for j in range(4):
                          for kk in range(k1 - k0):
                              mm(accs[j][1][:, :], wv[:, kk, j * 128:(j + 1) * 128], m3[:, k0 + kk, :], k0 + kk == 0, k0 + kk == 21,
                                 r=[wb, mbuf[k0 + kk]], w=accs[j][2])
                  for j in range(4):
                      dc = ct * 4 + j
                      tt(xT3[:, dc, t0:t0 + TG], xT3[:, dc, t0:t0 + TG], accs[j][1][:, :], ALU.add, r=[xbuf[dc][g], accs[j][2]], w=[xbuf[dc][g]])
                      psrel(accs[j][0])
              for b_ in mbuf:
                  for e_, o_ in b_.r.items():
                      p_ = mT.buf.r.get(e_)
                      if p_ is None or p_.idx < o_.idx:
                          mT.buf.r[e_] = o_
                  if b_.w is not None:
                      p_ = mT.buf.r.get(b_.w.eng)
                      if p_ is None or p_.idx < b_.w.idx:
                          mT.buf.r[b_.w.eng] = b_.w
              aB.release(mT)
              ckpt('ffn')
              hT, h3, hb = rmsnorm(g, P_GPLE + l * 8)
              pt_ = aB.alloc(2 * TG, "pT")
              p3 = v3(pt_.ap, TG)
              dma('pool', p3, pT_d[li].rearrange("(c p) t -> p c t", p=128)[:, :, t0:t0 + TG], 'pT', w=[pt_.buf])
              wg = w_pg_d[li].rearrange("(k p) n -> p k n", p=128)
              wp = w_pp_d[li].rearrange("(k p) n -> p k n", p=128)
              for ct in range(2):
                  gv, gb = wload(wg[:, :, ct * 512:(ct + 1) * 512], 128, 8, 512)
                  pv, pbb = wload(wp[:, :, ct * 512:(ct + 1) * 512], 128, 2, 512)
                  for j in range(4):
                      dc = ct * 4 + j
                      gi, gps, gpb = psalloc()
                      for k in range(KC):
                          mm(gps[:, :], gv[:, k, j * 128:(j + 1) * 128], h3[:, k, :], k == 0, k == KC - 1, r=[gb, hb[k]], w=gpb)
                      bi, bps, bpb = psalloc()
                      for k in range(2):
                          mm(bps[:, :], pv[:, k, j * 128:(j + 1) * 128], p3[:, k, :], k == 0, k == 1, r=[pbb, pt_.buf], w=bpb)
                      sig = aF.alloc(TG, "sig")
                      act(sig.ap, gps[:, :], AF.Sigmoid, r=[gpb], w=[sig.buf])
                      psrel(gi)
                      tt(sig.ap, sig.ap, bps[:, :], ALU.mult, r=[sig.buf, bpb], w=[sig.buf])
                      psrel(bi)
                      tt(xT3[:, dc, t0:t0 + TG], xT3[:, dc, t0:t0 + TG], sig.ap, ALU.add, r=[xbuf[dc][g], sig.buf], w=[xbuf[dc][g]])
                      aF.release(sig)
              aB.release(pt_)
              merge_bufs(hT, hb)
              aB.release(hT)
    except StopBuild:
        pass
    S.epoch += 1
    osrc = out_d.rearrange("(c p) t -> p c t", p=128)
    for g in range(NG):
        t0 = g * TG
        if final_norm and g < ngrun:
            pi, ps, pb = psalloc()
            for c in range(KC):
                sq = aB.alloc(TG, "sq")
                act(sq.ap, xT3[:, c, t0:t0 + TG], AF.Square, r=[xbuf[c][g]], w=[sq.buf])
                mm(ps[:, :], ones, sq.ap, c == 0, c == KC - 1, r=[sq.buf, cBb], w=pb)
                aB.release(sq)
            rstd = aF.alloc(TG, "rstd")
            act(rstd.ap, ps[:, :], AF.Sqrt, r=[pb, cFb], w=[rstd.buf], bias=epsc, scale=1.0 / D)
            recip(rstd.ap, rstd.ap, r=[rstd.buf], w=[rstd.buf])
            psrel(pi)
            for c in range(KC):
                o = aF.alloc(TG, "o")
                stt(o.ap, xT3[:, c, t0:t0 + TG], par[:, P_GFIN + c:P_GFIN + c + 1], rstd.ap, ALU.mult, ALU.mult,
                    r=[xbuf[c][g], parb, rstd.buf], w=[o.buf])
                dma('sp', osrc[:, c, t0:t0 + TG], o.ap, 'out', r=[o.buf])
                aF.release(o)
            aF.release(rstd)
        else:
            for c in range(KC):
                dma('sp', osrc[:, c, t0:t0 + TG], xT3[:, c, t0:t0 + TG], 'out', r=[xbuf[c][g]])
    S.emit(nc, es)
    es.close()
    stats = dict(n_ops={e: len(S.q[e]) for e in ENGS}, nwaits=S.nwaits, aF_peak=aF.peak, aB_peak=aB.peak)
    return nc, stats, dbg_outs


_CONSTS = None


def prep_inputs(inp, layers):
    global _CONSTS
    if _CONSTS is None:
        _CONSTS = make_consts()
    cf, cb, rot = _CONSTS
    par, bd = pack_params(inp)
    ls = list(layers)
    f = lambda k: np.ascontiguousarray(np.asarray(inp[k], np.float32)[ls])
    shared = dict(
        w_in=f("w_in"), w_ba=f("w_branch_attn"), w_br=f("w_branch_ret"), w_bp=f("w_branch_pool"),
        w_out=f("w_out"), w_up=f("w_up"), w_dn=f("w_down"), w_pg=f("w_ple_gate"), w_pp=f("w_ple_proj"),
        par=par, bd=bd, cf=cf, cb=cb, rot=rot,
    )
    return shared


def run_layers(xT_all, inp, layers, final_norm, ngrun=NG, dbg=False, ncores=8):
    nc, stats, dbg_outs = build_program(layers, final_norm, ngrun, dbg)
    shared = prep_inputs(inp, layers)
    p = np.asarray(inp["p"], np.float32)
    in_maps = []
    for b in range(ncores):
        m = dict(shared)
        m["xT"] = np.ascontiguousarray(xT_all[b])
        m["pT"] = np.ascontiguousarray(p[list(layers), b].transpose(0, 2, 1))
        in_maps.append(m)
    res = run_bass_kernel_spmd(nc, in_maps, core_ids=list(range(ncores)))
    outs = np.stack([np.asarray(r["outT"]) for r in res.results], axis=0)
    return outs, res, stats


def kernel(**inputs):
    x = np.asarray(inputs["x"], np.float32)
    xT = np.ascontiguousarray(x.transpose(0, 2, 1))
    outT, _, _ = run_layers(xT, inputs, range(L), True)
    return np.ascontiguousarray(outT.transpose(0, 2, 1)).astype(np.float32)
```

```python
import numpy as np
from collections import deque
from contextlib import ExitStack
import concourse.bass as bass
import concourse.mybir as mybir
from concourse.bass_utils import run_bass_kernel_spmd

F32 = mybir.dt.float32
BF16 = mybir.dt.bfloat16
AF = mybir.ActivationFunctionType
ALU = mybir.AluOpType
AX = mybir.AxisListType

ENGS = ('pe', 'act', 'dve', 'pool', 'sp')

T = 2048
D = 1024
L = 4
TG = 512
NG = T // TG
KC = D // 128
FF = 2816
INW = 5632
EPS = 1e-6
NW = 4


class Buf:
    __slots__ = ('name', 'w', 'r', 'rd', 'excl')

    def __init__(self, name='', excl=False):
        self.name = name
        self.excl = excl
        self.w = None
        self.r = {}
        self.rd = []


class Op:
    __slots__ = ('eng', 'fn', 'deps', 'idx', 'inc', 'semval', 'dma_key', 'dma_val', 'dwait', 'ep')


class Sched:
    def __init__(self):
        self.q = {e: [] for e in ENGS}
        self.dma_cnt = {}
        self.epoch = 0

    def _track(self, op, reads, writes):
        deps = []
        for b in reads:
            if b.w is not None:
                deps.append(b.w)
            if b.excl:
                for e_, o_ in b.r.items():
                    if e_ != op.eng:
                        deps.append(o_)
        for b in writes:
            if b.w is not None:
                deps.append(b.w)
            deps.extend(b.r.values())
            deps.extend(b.rd)
        for b in reads:
            if op.dma_key is not None:
                b.rd.append(op)
            else:
                b.r[op.eng] = op
        for b in writes:
            b.w = op
            b.r = {}
            b.rd = []
        return deps

    def op(self, eng, fn, r=(), w=(), key=None):
        o = Op()
        o.eng = eng
        o.ep = self.epoch
        o.fn = fn
        o.inc = False
        o.semval = 0
        o.dma_key = key
        o.dma_val = 0
        if key is not None:
            self.dma_cnt[key] = self.dma_cnt.get(key, 0) + 1
            o.dma_val = 16 * self.dma_cnt[key]
        o.deps = self._track(o, r, w)
        o.dwait = {}
        for d in o.deps:
            if d.dma_key is not None:
                o.dwait[d.dma_key] = 16 * self.dma_cnt[d.dma_key] - (16 if d.dma_key == key else 0)
        o.idx = len(self.q[eng])
        self.q[eng].append(o)
        return o

    def emit(self, nc, es):
        for e in ENGS:
            for o in self.q[e]:
                for d in o.deps:
                    if d.dma_key is None and not (d.eng == o.eng == 'pe'):
                        d.inc = True
        semh = {}
        for e in ENGS:
            cnt = {}
            for o in self.q[e]:
                if o.dma_key is None and o.inc:
                    cnt[o.ep] = cnt.get(o.ep, 0) + 1
                    o.semval = cnt[o.ep]
                    if ('eng', e, o.ep) not in semh:
                        semh[('eng', e, o.ep)] = es.enter_context(nc.semaphore(f's_{e}_{o.ep}'))
        for k in self.dma_cnt:
            semh[('dma', k)] = es.enter_context(nc.semaphore('d_' + str(k)))
        self.nwaits = 0

        def run(eobj, ename):
            seen = {}
            for o in self.q[ename]:
                waits = {}
                for d in o.deps:
                    if d.dma_key is not None:
                        k = ('dma', d.dma_key)
                        v = o.dwait[d.dma_key]
                    else:
                        if d.eng == ename == 'pe':
                            continue
                        k = ('eng', d.eng, d.ep)
                        v = d.semval
                    if v > waits.get(k, 0):
                        waits[k] = v
                for k, v in waits.items():
                    if seen.get(k, 0) >= v:
                        continue
                    seen[k] = v
                    eobj.wait_ge(semh[k], v)
                    self.nwaits += 1
                ins = o.fn(eobj)
                if o.dma_key is not None:
                    ins.then_inc(semh[('dma', o.dma_key)], 16)
                elif o.inc:
                    ins.then_inc(semh[('eng', ename, o.ep)], 1)
            last = {}
            for o in self.q[ename]:
                if o.dma_key is not None:
                    last[o.dma_key] = max(last.get(o.dma_key, 0), o.dma_val)
            for k, v in last.items():
                if seen.get(('dma', k), 0) < v:
                    eobj.wait_ge(semh[('dma', k)], v)

        with nc.Block() as block:
            @block.tensor
            def _(e):
                run(e, 'pe')

            @block.scalar
            def _(e):
                run(e, 'act')

            @block.vector
            def _(e):
                run(e, 'dve')

            @block.gpsimd
            def _(e):
                run(e, 'pool')

            @block.sync
            def _(e):
                run(e, 'sp')


class Tile:
    __slots__ = ('ap', 'buf', 'lo', 'hi', 'arena')


class Arena:
    def __init__(self, tensor, ncols, name):
        self.t = tensor
        self.n = ncols
        self.free = [(0, ncols)]
        self.ghosts = []
        self.name = name
        self.peak = 0
        self.used = 0
        self.rover = 0

    def alloc(self, n, name='', rot=None):
        n0 = n
        n = (n + 31) // 32 * 32
        if rot is None:
            rot = n <= 640
        cand = [(lo, hi) for (lo, hi) in self.free if hi - lo >= n]
        pick = None
        for (lo, hi) in (cand if rot else []):
            if hi > self.rover and hi - max(lo, self.rover) >= n:
                pick = (lo, hi, max(lo, self.rover))
                break
        if pick is None and cand:
            pick = (cand[0][0], cand[0][1], cand[0][0])
        if pick is not None:
            flo, fhi, lo = pick
            self.free.remove((flo, fhi))
            if lo > flo:
                self.free.append((flo, lo))
            if lo + n < fhi:
                self.free.append((lo + n, fhi))
            self.free.sort()
            if rot:
                self.rover = lo + n
            for _ in (0,):
                t = Tile()
                t.lo, t.hi, t.arena = lo, lo + n, self
                t.buf = Buf(name)
                t.ap = self.t[:, lo:lo + n0]
                keep = []
                for (glo, ghi, ops) in self.ghosts:
                    if glo < t.hi and ghi > t.lo:
                        for o in ops:
                            if o.dma_key is not None:
                                t.buf.rd.append(o)
                            else:
                                p = t.buf.r.get(o.eng)
                                if p is None or p.idx < o.idx:
                                    t.buf.r[o.eng] = o
                        if glo >= t.lo and ghi <= t.hi:
                            continue
                    keep.append((glo, ghi, ops))
                self.ghosts = keep
                self.used += n
                self.peak = max(self.peak, self.used)
                return t
        raise RuntimeError(f"arena {self.name} out of space for {name} n={n} used={self.used} free={self.free}")

    def release(self, t):
        ops = list(t.buf.r.values()) + list(t.buf.rd)
        if t.buf.w is not None:
            ops.append(t.buf.w)
        self.ghosts.append((t.lo, t.hi, ops))
        self.used -= (t.hi - t.lo)
        fl = self.free + [(t.lo, t.hi)]
        fl.sort()
        out = []
        for lo, hi in fl:
            if out and out[-1][1] == lo:
                out[-1] = (out[-1][0], hi)
            else:
                out.append((lo, hi))
        self.free = out


def v3(ap, b):
    return ap.rearrange("p (a b) -> p a b", b=b)


C_DEC = 0
C_XI = C_DEC + 512
C_ZS = C_XI + 256
C_COS = C_ZS + 4
C_SIN = C_COS + 512
C_RC = C_SIN + 512
C_RCN = C_RC + 2
C_EPS = C_RCN + 32
C_NEG = C_EPS + 1
NCF = C_NEG + 32
B_ID = 0
B_TRI = 128
B_ONE = 256
B_E64 = 384
NCB = B_E64 + 1024
P_GMIX = 0
P_GFFN = P_GMIX + L * 8
P_GPLE = P_GFFN + L * 8
P_GFIN = P_GPLE + L * 8
P_CW = P_GFIN + 8
P_CB = P_CW + L * 3 * 44
P_PS = P_CB + L * 44
NPAR = P_PS + L * 2


def make_consts():
    f64 = np.float64
    H = 4
    lg = np.log1p(-np.exp2(-5.0 - np.arange(H, dtype=f64)))
    cf = np.zeros((128, NCF), np.float32)
    i = np.arange(128)
    rel = i[None, :] - i[:, None]
    dec = np.zeros((128, H, 128), f64)
    for h in range(H):
        dec[:, h, :] = np.where(rel >= 0, np.exp(np.maximum(rel, 0) * lg[h]), 0.0) * 0.125
    cf[:, C_DEC:C_DEC + 512] = dec.reshape(128, 512)
    xi = np.zeros((128, 2, 128), f64)
    for p in range(128):
        for c in range(2):
            h = 2 * c + p // 64
            xi[p, c, :] = np.exp((i + 1.0) * lg[h]) * 0.125
    cf[:, C_XI:C_XI + 256] = xi.reshape(128, 256)
    for h in range(H):
        cf[:, C_ZS + h] = np.exp((127.0 - i) * lg[h])
    half = 32
    inv_freq = (np.float32(10000.0) ** (-(np.arange(half, dtype=np.float32)) / np.float32(half))).astype(np.float32)
    pos = np.arange(T, dtype=np.float32)
    ang = (pos[:, None] * inv_freq[None, :]).astype(np.float32).astype(f64)
    cos = np.cos(ang)
    sin = np.sin(ang)
    cf[:, C_COS:C_COS + 512] = cos.reshape(16, 128, 32).transpose(1, 0, 2).reshape(128, 512)
    cf[:, C_SIN:C_SIN + 512] = sin.reshape(16, 128, 32).transpose(1, 0, 2).reshape(128, 512)
    wins = [2, 4, 8, 16]
    for p in range(128):
        for c in range(2):
            w = wins[2 * c + p // 64]
            cf[p, C_RC + c] = 1.0 / w
            for t in range(16):
                cf[p, C_RCN + c * 16 + t] = 1.0 / min(t + 1, w)
    cf[:, C_EPS] = EPS
    cf[:, C_NEG:C_NEG + 32] = -1e30
    cb = np.zeros((128, NCB), np.float32)
    cb[:, B_ID:B_ID + 128] = np.eye(128)
    cb[:, B_TRI:B_TRI + 128] = (rel >= 0).astype(np.float32)
    cb[:, B_ONE:B_ONE + 128] = 1.0
    for p in range(128):
        n = p % 64
        if n < 8:
            cb[p, B_E64 + n * 128:B_E64 + (n + 1) * 128] = -30000.0
    rot = np.zeros((2, 128, T), np.float32)
    for p in range(128):
        rot[0, p, :] = cos[:, p % 32]
        rot[1, p, :] = sin[:, p % 32]
    return cf, cb, rot


def pack_params(inp):
    par = np.zeros((128, NPAR), np.float32)

    def fm(v):
        v = np.asarray(v, np.float32)
        lead = v.shape[:-1]
        c = v.shape[-1] // 128
        return np.moveaxis(v.reshape(lead + (c, 128)), -1, 0)

    par[:, P_GMIX:P_GMIX + L * 8] = fm(inp["norm_mix_g"]).reshape(128, -1)
    par[:, P_GFFN:P_GFFN + L * 8] = fm(inp["norm_ffn_g"]).reshape(128, -1)
    par[:, P_GPLE:P_GPLE + L * 8] = fm(inp["norm_ple_g"]).reshape(128, -1)
    par[:, P_GFIN:P_GFIN + 8] = fm(inp["norm_final_g"]).reshape(128, -1)
    par[:, P_CW:P_CW + L * 3 * 44] = fm(inp["conv_w"]).reshape(128, -1)
    par[:, P_CB:P_CB + L * 44] = fm(inp["conv_b"]).reshape(128, -1)
    par[:, P_PS:P_PS + L * 2] = fm(inp["pool_scale"]).reshape(128, -1)
    pw = np.asarray(inp["pool_w"], np.float32)
    bd = np.zeros((128, L, 2, 128), np.float32)
    for l in range(L):
        for c in range(2):
            for gl in range(2):
                bd[gl * 64:(gl + 1) * 64, l, c, gl * 64:(gl + 1) * 64] = pw[l, 2 * c + gl]
    return par, bd.reshape(128, L * 2 * 128)


class StopBuild(Exception):
    pass


STOP_AT = [None]
SKIP = {}


def build_program(layers, final_norm, ngrun=NG, dbg=False):
    nc = bass.Bass("TRN2", target_bir_lowering=False)
    es = ExitStack()
    S = Sched()
    NL = len(layers)

    def din(name, shape):
        return nc.dram_tensor(name, shape, F32, kind="ExternalInput").ap()

    xT_d = din("xT", [D, T])
    pT_d = din("pT", [NL, 256, T])
    w_in_d = din("w_in", [NL, D, INW])
    w_ba_d = din("w_ba", [NL, 256, D])
    w_br_d = din("w_br", [NL, 512, D])
    w_bp_d = din("w_bp", [NL, 256, D])
    w_out_d = din("w_out", [NL, D, D])
    w_up_d = din("w_up", [NL, D, 2 * FF])
    w_dn_d = din("w_dn", [NL, FF, D])
    w_pg_d = din("w_pg", [NL, D, D])
    w_pp_d = din("w_pp", [NL, 256, D])
    par_d = din("par", [128, NPAR])
    bd_d = din("bd", [128, L * 256])
    cf_d = din("cf", [128, NCF])
    cb_d = din("cb", [128, NCB])
    rot_d = din("rot", [2, 128, T])
    out_d = nc.dram_tensor("outT", [D, T], F32, kind="ExternalOutput").ap()
    dbg_outs = {}

    def sb(name, shape, dt):
        return es.enter_context(nc.sbuf_tensor(name, shape, dt))

    xT = sb("xT_sb", [128, KC * T], F32)
    xT3 = v3(xT[:], T)
    xbuf = [[Buf(f"x{c}_{g}") for g in range(NG)] for c in range(KC)]
    kaT = sb("kaT", [128, 2 * T], BF16)
    kaT3 = v3(kaT[:], T)
    kabuf = [Buf(f"ka{g}") for g in range(NG)]
    vaS = sb("vaS", [128, 16 * 256], BF16)
    va3 = v3(vaS[:], 256)
    vabuf = [Buf(f"va{g}") for g in range(NG)]
    kmT = sb("kmT", [128, 2 * 16], BF16)
    kmT3 = v3(kmT[:], 16)
    kmbuf = Buf("km")
    Rst = sb("Rst", [128, 4 * 128], F32)
    Rb = sb("Rb", [128, 4 * 128], BF16)
    Rstbuf = [Buf(f"Rst{h}") for h in range(4)]
    Rbbuf = [Buf(f"Rb{h}") for h in range(4)]
    convh = sb("convh", [128, 44 * 2], F32)
    convh3 = v3(convh[:], 2)
    convhbuf = [Buf(f"ch{i}") for i in range(44)]
    ubuf_t = sb("ubuf", [128, 2 * 528], F32)
    u3 = v3(ubuf_t[:], 528)
    ubuf = [Buf("u0"), Buf("u1")]
    wslot = [sb(f"wslot{i}", [128, 8 * 512], BF16) for i in range(NW)]
    wbuf = [Buf(f"w{i}") for i in range(NW)]
    cF = sb("cF", [128, NCF], F32)
    cFb = Buf("cF")
    cB = sb("cB", [128, NCB], BF16)
    cBb = Buf("cB")
    par = sb("par_sb", [128, NPAR], F32)
    parb = Buf("par")
    bdS = sb("bdS", [128, L * 256], BF16)
    bdb = Buf("bd")
    AFC = 4352
    ABC = 20480
    aF_t = sb("arenaF", [128, AFC], F32)
    aB_t = sb("arenaB", [128, ABC], BF16)
    aF = Arena(aF_t, AFC, "F")
    aB = Arena(aB_t, ABC, "B")
    psb = []
    for i in range(8):
        t = es.enter_context(nc.psum_tensor(f"ps{i}", [128, 512], F32))
        psb.append((t, Buf(f"ps{i}", excl=True)))
    psfree = deque(range(8))

    def psalloc():
        i = psfree.popleft()
        return i, psb[i][0], psb[i][1]

    def psrel(i):
        psfree.append(i)

    def mm(ps_ap, lhsT, rhs, start, stop, r, w):
        S.op('pe', lambda e, a=ps_ap, b=lhsT, c=rhs, s=start, t=stop: e.matmul(a, b, c, start=s, stop=t), r=r, w=[w])

    def act(out, in_, func, r, w, bias=None, scale=None):
        kw = {}
        if bias is not None:
            kw['bias'] = bias
        if scale is not None:
            kw['scale'] = scale
        S.op('act', lambda e, o=out, i=in_, f=func, kw=kw: e.activation(out=o, in_=i, func=f, **kw), r=r, w=w)

    def tt(out, in0, in1, op, r, w, eng='dve'):
        S.op(eng, lambda e, o=out, a=in0, b=in1, p=op: e.tensor_tensor(out=o, in0=a, in1=b, op=p), r=r, w=w)

    def ts(out, in0, s1, s2, op0, op1, r, w, eng='dve'):
        if op1 is None:
            S.op(eng, lambda e, o=out, a=in0, x=s1, p=op0: e.tensor_scalar(out=o, in0=a, scalar1=x, scalar2=None, op0=p), r=r, w=w)
        else:
            S.op(eng, lambda e, o=out, a=in0, x=s1, y=s2, p=op0, q=op1: e.tensor_scalar(out=o, in0=a, scalar1=x, scalar2=y, op0=p, op1=q), r=r, w=w)

    def stt(out, in0, scalar, in1, op0, op1, r, w):
        S.op('dve', lambda e, o=out, a=in0, s=scalar, b=in1, p=op0, q=op1: e.scalar_tensor_tensor(out=o, in0=a, scalar=s, in1=b, op0=p, op1=q), r=r, w=w)

    def cp(out, in_, r, w, eng='dve'):
        S.op(eng, lambda e, o=out, i=in_: e.tensor_copy(out=o, in_=i), r=r, w=w)

    def recip(out, in_, r, w):
        S.op('dve', lambda e, o=out, i=in_: e.reciprocal(out=o, in_=i), r=r, w=w)

    def memset(ap, val, w, eng='dve'):
        S.op(eng, lambda e, a=ap, v=val: e.memset(a, v), w=w)

    def dma(eng, out, in_, key, r=(), w=()):
        S.op(eng, lambda e, o=out, i=in_: e.dma_start(out=o, in_=i), r=r, w=w, key=key)

    wctr = [0]

    def wload(src, P, kc, ncols):
        i = wctr[0] % NW
        wctr[0] += 1
        view = wslot[i][0:P, 0:kc * ncols].rearrange("p (k n) -> p k n", n=ncols)
        dma('pool', view, src, f'w{i}', w=[wbuf[i]])
        return view, wbuf[i]

    def dump(name, ap, buf, shape, dt=F32):
        if not dbg:
            return
        d = nc.dram_tensor(name, shape, dt, kind="ExternalOutput").ap()
        dbg_outs[name] = d
        dma('sp', d, ap, 'dbg', r=[buf])

    dma('sp', cF[:], cf_d[:, :], 'c0', w=[cFb])
    dma('sp', par[:], par_d[:, :], 'c1', w=[parb])
    dma('pool', cB[:], cb_d[:, :], 'c2', w=[cBb])
    dma('pool', bdS[:], bd_d[:, :], 'c3', w=[bdb])
    xsrc = xT_d.rearrange("(c p) t -> p c t", p=128)
    for c in range(KC):
        dma('sp', xT3[:, c, :], xsrc[:, c, :], f'x{c}', w=[xbuf[c][g] for g in range(NG)])
    memset(kmT[:], 0.0, w=[kmbuf])
    ident = cB[:, B_ID:B_ID + 128]
    tri = cB[:, B_TRI:B_TRI + 128]
    ones = cB[:, B_ONE:B_ONE + 128]
    E64 = v3(cB[:, B_E64:B_E64 + 1024], 128)
    decT = cF[:, C_DEC:C_DEC + 512]
    xiT = v3(cF[:, C_XI:C_XI + 256], 128)
    zs = cF[:, C_ZS:C_ZS + 4]
    costm = v3(cF[:, C_COS:C_COS + 512], 32)
    sintm = v3(cF[:, C_SIN:C_SIN + 512], 32)
    epsc = cF[:, C_EPS:C_EPS + 1]
    gam = [float(np.exp(128.0 * np.log1p(-np.exp2(-5.0 - h)))) for h in range(4)]

    def rmsnorm(g, gcol):
        t0 = g * TG
        pi, ps, pb = psalloc()
        for c in range(KC):
            sq = aB.alloc(TG, "sq")
            act(sq.ap, xT3[:, c, t0:t0 + TG], AF.Square, r=[xbuf[c][g]], w=[sq.buf])
            mm(ps[:, :], ones, sq.ap, c == 0, c == KC - 1, r=[sq.buf, cBb], w=pb)
            aB.release(sq)
        rstd = aF.alloc(TG, "rstd")
        act(rstd.ap, ps[:, :], AF.Sqrt, r=[pb, cFb], w=[rstd.buf], bias=epsc, scale=1.0 / D)
        recip(rstd.ap, rstd.ap, r=[rstd.buf], w=[rstd.buf])
        psrel(pi)
        hT = aB.alloc(KC * TG, "hT")
        h3 = v3(hT.ap, TG)
        hb = [Buf(f"h{c}") for c in range(KC)]
        for c in range(KC):
            for e_, o_ in hT.buf.r.items():
                hb[c].r[e_] = o_
            hb[c].rd = list(hT.buf.rd)
            stt(h3[:, c, :], xT3[:, c, t0:t0 + TG], par[:, gcol + c:gcol + c + 1], rstd.ap, ALU.mult, ALU.mult,
                r=[xbuf[c][g], parb, rstd.buf], w=[hb[c]])
        aF.release(rstd)
        return hT, h3, hb

    def merge_bufs(tile, bufs):
        for b_ in bufs:
            for e_, o_ in b_.r.items():
                p_ = tile.buf.r.get(e_)
                if p_ is None or p_.idx < o_.idx:
                    tile.buf.r[e_] = o_
            tile.buf.rd.extend(b_.rd)
            if b_.w is not None:
                p_ = tile.buf.r.get(b_.w.eng)
                if p_ is None or p_.idx < b_.w.idx:
                    tile.buf.r[b_.w.eng] = b_.w

    def ckpt(name):
        if STOP_AT[0] == name:
            raise StopBuild()

    try:
      for li, l in enumerate(layers):
          wi = w_in_d[li].rearrange("(k p) n -> p k n", p=128)
          for h in range(4):
              hb = 64 * (h % 2)
              memset(Rst[hb:hb + 64, h * 128:(h + 1) * 128], 0.0, w=[Rstbuf[h]])
              memset(Rb[hb:hb + 64, h * 128:(h + 1) * 128], 0.0, w=[Rbbuf[h]])
          for i in range(44):
              memset(convh3[:, i, :], 0.0, w=[convhbuf[i]])
          for c in range(2):
              memset(u3[:, c, 0:16], 0.0, w=[ubuf[c]])
          for g in range(ngrun):
              t0 = g * TG
              S.epoch += 1
              hT, h3, hb = rmsnorm(g, P_GMIX + l * 8)
              if dbg and li == 0 and g == 0:
                  dump("d_h", hT.ap, hT.buf, [128, KC * TG], BF16)
              ckpt('norm')
              wv, wb = wload(wi[:, :, 0:512], 128, 8, 512)
              qaT = aB.alloc(2 * TG, "qaT")
              qa3 = v3(qaT.ap, TG)
              for c in range(2):
                  pi, ps, pb = psalloc()
                  for k in range(KC):
                      mm(ps[:, :], wv[:, k, c * 128:(c + 1) * 128], h3[:, k, :], k == 0, k == KC - 1, r=[wb, hb[k]], w=pb)
                  act(qa3[:, c, :], ps[:, :], AF.Identity, r=[pb], w=[qaT.buf])
                  psrel(pi)
              ckpt('w0')
              for c in range(2):
                  pi, ps, pb = psalloc()
                  for k in range(KC):
                      mm(ps[:, :], wv[:, k, 256 + c * 128:256 + (c + 1) * 128], h3[:, k, :], k == 0, k == KC - 1, r=[wb, hb[k]], w=pb)
                  act(kaT3[:, c, t0:t0 + TG], ps[:, :], AF.Identity, r=[pb], w=[kabuf[g]])
                  if SKIP.get('km'):
                      psrel(pi)
                      continue
                  km = aF.alloc(2, "km")
                  S.op('dve', lambda e, o=km.ap, i=v3(kaT3[:, c, t0:t0 + TG], 256): e.tensor_reduce(out=o, in_=i, axis=AX.X, op=ALU.add),
                       r=[kabuf[g]], w=[km.buf])
                  ts(kmT3[0:64, c, 2 * g:2 * g + 2], km.ap[0:64, :], 1.0 / 256.0, None, ALU.mult, None, r=[km.buf], w=[kmbuf])
                  ts(kmT3[64:128, c, 8 + 2 * g:8 + 2 * g + 2], km.ap[64:128, :], 1.0 / 256.0, None, ALU.mult, None, r=[km.buf], w=[kmbuf])
                  aF.release(km)
                  psrel(pi)
              ckpt('ka')
              wv, wb = wload(wi[:, :, 512:768], 128, 8, 256)
              for tl in range(4):
                  pi, ps, pb = psalloc()
                  for k in range(KC):
                      mm(ps[:, 0:256], h3[:, k, tl * 128:(tl + 1) * 128], wv[:, k, :], k == 0, k == KC - 1, r=[wb, hb[k]], w=pb)
                  act(va3[:, 4 * g + tl, :], ps[:, 0:256], AF.Identity, r=[pb], w=[vabuf[g]])
                  psrel(pi)
              if dbg and li == 0 and g == 0:
                  dump("d_qa", qaT.ap, qaT.buf, [128, 2 * TG], BF16)
              ckpt('qkv')
              nbT = None
              if g >= 2 and not SKIP.get('sel'):
                  nbT = [aB.alloc(TG, "nbT0"), aB.alloc(TG, "nbT1")]
                  gi, gps, gpb = psalloc()
                  for qt in range(4):
                      for c in range(2):
                          mm(gps[:, qt * 32 + c * 16:qt * 32 + c * 16 + 16], qa3[:, c, qt * 128:(qt + 1) * 128],
                             kmT3[:, c, :], True, True, r=[qaT.buf, kmbuf], w=gpb)
                  ckpt('sel1')
                  for qt in range(4):
                      b = 2 * g + qt // 2
                      gsb = aF.alloc(32, "gsb")
                      m8 = aF.alloc(32, "m8")
                      cp(gsb.ap, cF[:, C_NEG:C_NEG + 32], r=[cFb], w=[gsb.buf])
                      cp(v3(gsb.ap, 8)[:, :, 0:b], v3(gps[:, qt * 32:(qt + 1) * 32], 8)[:, :, 0:b], r=[gpb], w=[gsb.buf])
                      for h in range(4):
                          S.op('dve', lambda e, o=m8.ap[:, h * 8:(h + 1) * 8], i=gsb.ap[:, h * 8:(h + 1) * 8]: e.max(out=o, in_=i),
                               r=[gsb.buf], w=[m8.buf])
                      negb = aB.alloc(256, "negb")
                      memset(negb.ap, 0.0, w=[negb.buf])
                      tt(negb.ap.rearrange("p (h e) -> p h e", e=64)[:, :, 0:8], v3(gsb.ap, 8),
                         v3(m8.ap, 8)[:, :, 2:3].to_broadcast([128, 4, 8]), ALU.is_lt, r=[gsb.buf, m8.buf], w=[negb.buf])
                      ckpt('sel2')
                      for tl in range(2):
                          pi, ps, pb = psalloc()
                          mm(ps[:, 0:128], negb.ap[:, tl * 128:(tl + 1) * 128], ident, True, True, r=[negb.buf, cBb], w=pb)
                          act(nbT[tl].ap[:, qt * 128:(qt + 1) * 128], ps[:, 0:128], AF.Identity, r=[pb], w=[nbT[tl].buf])
                          psrel(pi)
                      aB.release(negb)
                      aF.release(gsb)
                      aF.release(m8)
                      ckpt('sel3')
                  psrel(gi)
              attnT = aB.alloc(2 * TG, "attnT")
              at3 = v3(attnT.ap, TG)
              nkt = 4 * g + 4
              SK = 2
              acc = {}
              pend = deque()

              def do_pv(h, kt, pt, q0):
                  if h not in acc:
                      acc[h] = psalloc() + psalloc()
                  oi, ops_, opb, li_, lps, lpb = acc[h]
                  hc, hb_ = h // 2, 64 * (h % 2)
                  mm(ops_[:, q0:TG], va3[:, kt, hc * 128:(hc + 1) * 128], pt.ap[:, q0:TG], kt == 0, kt == nkt - 1,
                     r=[vabuf[kt // 4], pt.buf], w=opb)
                  mm(lps[:, q0:TG], ones, pt.ap[:, q0:TG], kt == 0, kt == nkt - 1, r=[cBb, pt.buf], w=lpb)
                  aB.release(pt)
                  if kt == nkt - 1:
                      rc_ = aF.alloc(TG, "rcp")
                      recip(rc_.ap[hb_:hb_ + 64, :], lps[hb_:hb_ + 64, :], r=[lpb], w=[rc_.buf])
                      tt(at3[hb_:hb_ + 64, hc, :], ops_[hb_:hb_ + 64, :], rc_.ap[hb_:hb_ + 64, :], ALU.mult,
                         r=[opb, rc_.buf], w=[attnT.buf])
                      aF.release(rc_)
                      psrel(oi)
                      psrel(li_)

              for h in range(4):
                  c, base = h // 2, 64 * (h % 2)
                  for kt in range(nkt):
                      r_ = kt - 4 * g
                      q0 = 128 * r_ if r_ > 0 else 0
                      n = kt // 2
                      bias_c0 = None
                      if g >= 2 and not SKIP.get('sel') and not SKIP.get('bias'):
                          if n < 2 * g:
                              bias_c0 = 0
                          elif n == 2 * g:
                              bias_c0 = 256
                      pi, ps, pb = psalloc()
                      mm(ps[:, q0:TG], kaT3[base:base + 64, c, kt * 128:(kt + 1) * 128], qa3[base:base + 64, c, q0:TG],
                         True, bias_c0 is None, r=[kabuf[kt // 4], qaT.buf], w=pb)
                      if bias_c0 is not None:
                          tl, sl = h // 2, 64 * (h % 2)
                          c0 = max(bias_c0, q0)
                          mm(ps[:, c0:TG], E64[sl:sl + 64, n, :], nbT[tl].ap[sl:sl + 64, c0:TG], False, True,
                             r=[cBb, nbT[tl].buf], w=pb)
                      pt = aB.alloc(TG, "pt")
                      act(pt.ap[:, q0:TG], ps[:, q0:TG], AF.Exp, r=[pb], w=[pt.buf], scale=0.125)
                      psrel(pi)
                      if r_ >= 0:
                          tt(pt.ap[:, q0:q0 + 128], pt.ap[:, q0:q0 + 128], tri, ALU.mult, r=[pt.buf, cBb], w=[pt.buf])
                      pend.append((h, kt, pt, q0))
                      if len(pend) > SK:
                          do_pv(*pend.popleft())
              while pend:
                  do_pv(*pend.popleft())
              aB.release(qaT)
              if nbT is not None:
                  aB.release(nbT[0])
                  aB.release(nbT[1])
              if dbg and li == 0 and g == 0:
                  dump("d_attn", attnT.ap, attnT.buf, [128, 2 * TG], BF16)
              ckpt('attn')
              wv, wb = wload(wi[:, :, 768:1280], 128, 8, 512)
              wrot = aB.alloc(8 * 512, "wrot")
              wr3 = v3(wrot.ap, 512)
              w5 = wv.rearrange("p k (h t f) -> p k h t f", t=2, f=32)
              r5 = wr3.rearrange("p k (h t f) -> p k h t f", t=2, f=32)
              for k in range(KC):
                  S.op('act', lambda e, o=r5[:, k, :, 0, :], i=w5[:, k, :, 1, :]: e.mul(out=o, in_=i, mul=-1.0), r=[wb], w=[wrot.buf])
                  cp(r5[:, k, :, 1, :], w5[:, k, :, 0, :], r=[wb], w=[wrot.buf])
              rt = aF.alloc(2 * TG, "rot")
              rt3 = v3(rt.ap, TG)
              dma('sp', rt3[:, 0, :], rot_d[0, :, t0:t0 + TG], 'rot', w=[rt.buf])
              dma('sp', rt3[:, 1, :], rot_d[1, :, t0:t0 + TG], 'rot', w=[rt.buf])
              qk = aB.alloc(4 * TG, "qk")
              qk3 = v3(qk.ap, TG)
              qx = aB.alloc(2 * TG, "qx")
              qx3 = v3(qx.ap, TG)
              for c4 in range(4):
                  pi, ps, pb = psalloc()
                  pj, ps2, pb2 = psalloc()
                  for k in range(KC):
                      mm(ps[:, :], wv[:, k, c4 * 128:(c4 + 1) * 128], h3[:, k, :], k == 0, k == KC - 1, r=[wb, hb[k]], w=pb)
                  for k in range(KC):
                      mm(ps2[:, :], wr3[:, k, c4 * 128:(c4 + 1) * 128], h3[:, k, :], k == 0, k == KC - 1, r=[wrot.buf, hb[k]], w=pb2)
                  t1 = aF.alloc(TG, "t1")
                  t2 = aF.alloc(TG, "t2")
                  tt(t1.ap, ps[:, :], rt3[:, 0, :], ALU.mult, r=[pb, rt.buf], w=[t1.buf])
                  tt(t2.ap, ps2[:, :], rt3[:, 1, :], ALU.mult, r=[pb2, rt.buf], w=[t2.buf])
                  psrel(pi)
                  psrel(pj)
                  tt(qk3[:, c4, :], t1.ap, t2.ap, ALU.add, r=[t1.buf, t2.buf], w=[qk.buf])
                  if c4 < 2:
                      tt(t1.ap, t1.ap, t2.ap, ALU.add, r=[t1.buf, t2.buf], w=[t1.buf])
                      tt(v3(qx3[:, c4, :], 128), v3(t1.ap, 128), xiT[:, c4, :].unsqueeze(1).to_broadcast([128, 4, 128]), ALU.mult,
                         r=[t1.buf, cFb], w=[qx.buf])
                  aF.release(t1)
                  aF.release(t2)
              aF.release(rt)
              kt_t = aB.alloc(4 * 256, "ktm")
              ktm3 = v3(kt_t.ap, 256)
              for tl in range(4):
                  pi, ps, pb = psalloc()
                  for k in range(KC):
                      mm(ps[:, 0:256], h3[:, k, tl * 128:(tl + 1) * 128], wv[:, k, 256:512], k == 0, k == KC - 1, r=[wb, hb[k]], w=pb)
                  for k in range(KC):
                      mm(ps[:, 256:512], h3[:, k, tl * 128:(tl + 1) * 128], wr3[:, k, 256:512], k == 0, k == KC - 1, r=[wrot.buf, hb[k]], w=pb)
                  t1 = aF.alloc(256, "k1")
                  t2 = aF.alloc(256, "k2")
                  cb_ = costm[:, 4 * g + tl, :].unsqueeze(1).to_broadcast([128, 8, 32])
                  sb_ = sintm[:, 4 * g + tl, :].unsqueeze(1).to_broadcast([128, 8, 32])
                  tt(v3(t1.ap, 32), v3(ps[:, 0:256], 32), cb_, ALU.mult, r=[pb, cFb], w=[t1.buf])
                  tt(v3(t2.ap, 32), v3(ps[:, 256:512], 32), sb_, ALU.mult, r=[pb, cFb], w=[t2.buf])
                  psrel(pi)
                  tt(t1.ap, t1.ap, t2.ap, ALU.add, r=[t1.buf, t2.buf], w=[t1.buf])
                  tt(v3(ktm3[:, tl, :], 64), v3(t1.ap, 64), zs.unsqueeze(2).to_broadcast([128, 4, 64]), ALU.mult,
                     r=[t1.buf, cFb], w=[kt_t.buf])
                  aF.release(t1)
                  aF.release(t2)
              aB.release(wrot)
              wv, wb = wload(wi[:, :, 1280:1792], 128, 8, 512)
              vr = aB.alloc(4 * 512, "vr")
              vr3 = v3(vr.ap, 512)
              for tl in range(4):
                  pi, ps, pb = psalloc()
                  for k in range(KC):
                      mm(ps[:, :], h3[:, k, tl * 128:(tl + 1) * 128], wv[:, k, :], k == 0, k == KC - 1, r=[wb, hb[k]], w=pb)
                  act(vr3[:, tl, :], ps[:, :], AF.Identity, r=[pb], w=[vr.buf])
                  psrel(pi)
              wv, wb = wload(wi[:, :, 1792:2304], 128, 8, 512)
              sg = aB.alloc(4 * TG, "silug")
              sg3 = v3(sg.ap, TG)
              for c in range(4):
                  pi, ps, pb = psalloc()
                  for k in range(KC):
                      mm(ps[:, :], wv[:, k, c * 128:(c + 1) * 128], h3[:, k, :], k == 0, k == KC - 1, r=[wb, hb[k]], w=pb)
                  act(sg3[:, c, :], ps[:, :], AF.Silu, r=[pb], w=[sg.buf])
                  psrel(pi)
              retT = aB.alloc(4 * TG, "retT")
              rT3 = v3(retT.ap, TG)
              scs = []
              for h in range(4):
                  c, base = h // 2, 64 * (h % 2)
                  si, sps, spb = psalloc()
                  for r_ in range(4):
                      mm(sps[:, r_ * 128:(r_ + 1) * 128], qk3[base:base + 64, 2 + c, r_ * 128:(r_ + 1) * 128],
                         qk3[base:base + 64, c, r_ * 128:(r_ + 1) * 128], True, True, r=[qk.buf], w=spb)
                  sc = aB.alloc(TG, "sc")
                  tt(v3(sc.ap, 128), v3(sps[:, :], 128), decT[:, h * 128:(h + 1) * 128].unsqueeze(1).to_broadcast([128, 4, 128]),
                     ALU.mult, r=[spb, cFb], w=[sc.buf])
                  psrel(si)
                  scs.append(sc)
              ybanks = [psalloc() for _ in range(4)]
              for r_ in range(4):
                  cs = slice(r_ * 128, (r_ + 1) * 128)
                  for h in range(4):
                      c, base = h // 2, 64 * (h % 2)
                      yi, yps, ypb = ybanks[h]
                      sc = scs[h]
                      mm(yps[:, cs], vr3[:, r_, h * 128:(h + 1) * 128], sc.ap[:, cs], True, False, r=[vr.buf, sc.buf], w=ypb)
                      mm(yps[:, cs], Rb[base:base + 64, h * 128:(h + 1) * 128], qx3[base:base + 64, c, cs], False, True,
                         r=[Rbbuf[h], qx.buf], w=ypb)
                      ui, ups, upb = psalloc()
                      mm(ups[:, 0:128], ktm3[:, r_, c * 128:(c + 1) * 128], vr3[:, r_, h * 128:(h + 1) * 128], True, True,
                         r=[kt_t.buf, vr.buf], w=upb)
                      stt(Rst[base:base + 64, h * 128:(h + 1) * 128], Rst[base:base + 64, h * 128:(h + 1) * 128], gam[h],
                          ups[base:base + 64, 0:128], ALU.mult, ALU.add, r=[Rstbuf[h], upb], w=[Rstbuf[h]])
                      psrel(ui)
                      cp(Rb[base:base + 64, h * 128:(h + 1) * 128], Rst[base:base + 64, h * 128:(h + 1) * 128],
                         r=[Rstbuf[h]], w=[Rbbuf[h]])
              for sc in scs:
                  aB.release(sc)
              for hp in ((0, 1), (2, 3)):
                  st = {}
                  for h in hp:
                      yi, yps, ypb = ybanks[h]
                      yb = aB.alloc(TG, "yb")
                      ysq = aB.alloc(TG, "ysq")
                      act(yb.ap, yps[:, :], AF.Identity, r=[ypb], w=[yb.buf])
                      act(ysq.ap, yps[:, :], AF.Square, r=[ypb], w=[ysq.buf])
                      st[h] = dict(yb=yb, ysq=ysq)
                  for h in hp:
                      d_ = st[h]
                      d_['s1'] = psalloc()
                      d_['s2'] = psalloc()
                      mm(d_['s1'][1][:, :], ones, d_['yb'].ap, True, True, r=[cBb, d_['yb'].buf], w=d_['s1'][2])
                      mm(d_['s2'][1][:, :], ones, d_['ysq'].ap, True, True, r=[cBb, d_['ysq'].buf], w=d_['s2'][2])
                      aB.release(d_['yb'])
                      aB.release(d_['ysq'])
                  for h in hp:
                      d_ = st[h]
                      d_['mean'] = aF.alloc(TG, "mean")
                      d_['var'] = aF.alloc(TG, "var")
                      ts(d_['mean'].ap, d_['s1'][1][:, :], 1.0 / 128.0, None, ALU.mult, None, r=[d_['s1'][2]], w=[d_['mean'].buf])
                  for h in hp:
                      d_ = st[h]
                      tt(d_['var'].ap, d_['mean'].ap, d_['mean'].ap, ALU.mult, r=[d_['mean'].buf], w=[d_['var'].buf])
                  for h in hp:
                      d_ = st[h]
                      stt(d_['var'].ap, d_['s2'][1][:, :], 1.0 / 128.0, d_['var'].ap, ALU.mult, ALU.subtract,
                          r=[d_['s2'][2], d_['var'].buf], w=[d_['var'].buf])
                      psrel(d_['s1'][0])
                      psrel(d_['s2'][0])
                  for h in hp:
                      d_ = st[h]
                      ts(d_['var'].ap, d_['var'].ap, 0.0, None, ALU.max, None, r=[d_['var'].buf], w=[d_['var'].buf])
                  for h in hp:
                      d_ = st[h]
                      act(d_['var'].ap, d_['var'].ap, AF.Sqrt, r=[d_['var'].buf, cFb], w=[d_['var'].buf], bias=epsc, scale=1.0)
                  for h in hp:
                      d_ = st[h]
                      yi, yps, ypb = ybanks[h]
                      tt(d_['mean'].ap, yps[:, :], d_['mean'].ap, ALU.subtract, r=[ypb, d_['mean'].buf], w=[d_['mean'].buf])
                      psrel(yi)
                  for h in hp:
                      d_ = st[h]
                      recip(d_['var'].ap, d_['var'].ap, r=[d_['var'].buf], w=[d_['var'].buf])
                  for h in hp:
                      d_ = st[h]
                      tt(d_['mean'].ap, d_['mean'].ap, d_['var'].ap, ALU.mult, r=[d_['mean'].buf, d_['var'].buf], w=[d_['mean'].buf])
                  for h in hp:
                      d_ = st[h]
                      tt(rT3[:, h, :], d_['mean'].ap, sg3[:, h, :], ALU.mult, r=[d_['mean'].buf, sg.buf], w=[retT.buf])
                      aF.release(d_['mean'])
                      aF.release(d_['var'])
              for t_ in (qk, qx, kt_t, vr, sg):
                  aB.release(t_)
              if dbg and li == 0 and g == 0:
                  dump("d_ret", retT.ap, retT.buf, [128, 4 * TG], BF16)
              ckpt('ret')
              wv, wb = wload(wi[:, :, 2304:2560], 128, 8, 256)
              mixed = aB.alloc(2 * TG, "mixed")
              mx3 = v3(mixed.ap, TG)
              for c in range(2):
                  pi, ps, pb = psalloc()
                  for k in range(KC):
                      mm(ps[:, :], wv[:, k, c * 128:(c + 1) * 128], h3[:, k, :], k == 0, k == KC - 1, r=[wb, hb[k]], w=pb)
                  act(u3[:, c, 16:528], ps[:, :], AF.Identity, r=[pb], w=[ubuf[c]])
                  psrel(pi)
                  sa = aF.alloc(528, "sa")
                  sbb = aF.alloc(528, "sb")
                  tt(sa.ap[:, 1:528], u3[:, c, 1:528], u3[:, c, 0:527], ALU.add, r=[ubuf[c]], w=[sa.buf])
                  if c == 0:
                      tt(sbb.ap[64:128, 3:528], sa.ap[64:128, 3:528], sa.ap[64:128, 1:526], ALU.add, r=[sa.buf], w=[sbb.buf])
                      lo_src, hi_src = sa, sbb
                  else:
                      tt(sbb.ap[:, 3:528], sa.ap[:, 3:528], sa.ap[:, 1:526], ALU.add, r=[sa.buf], w=[sbb.buf])
                      tt(sa.ap[:, 7:528], sbb.ap[:, 7:528], sbb.ap[:, 3:524], ALU.add, r=[sbb.buf], w=[sa.buf])
                      tt(sbb.ap[64:128, 15:528], sa.ap[64:128, 15:528], sa.ap[64:128, 7:520], ALU.add, r=[sa.buf], w=[sbb.buf])
                      lo_src, hi_src = sa, sbb
                  rcs = cF[:, C_RC + c:C_RC + c + 1]
                  for (p0, p1, src) in ((0, 64, lo_src), (64, 128, hi_src)):
                      stt(mx3[p0:p1, c, :], src.ap[p0:p1, 16:528], rcs[p0:p1, :], u3[p0:p1, c, 16:528], ALU.mult, ALU.subtract,
                          r=[src.buf, ubuf[c], cFb], w=[mixed.buf])
                      if g == 0:
                          tf = aF.alloc(16, "tf")
                          tt(tf.ap[p0:p1, :], src.ap[p0:p1, 16:32], cF[p0:p1, C_RCN + c * 16:C_RCN + (c + 1) * 16], ALU.mult,
                             r=[src.buf, cFb], w=[tf.buf])
                          tt(mx3[p0:p1, c, 0:15], tf.ap[p0:p1, 0:15], u3[p0:p1, c, 16:31], ALU.subtract, r=[tf.buf, ubuf[c]], w=[mixed.buf])
                          aF.release(tf)
                  aF.release(sa)
                  aF.release(sbb)
                  th = aF.alloc(16, "th")
                  cp(th.ap, u3[:, c, 512:528], r=[ubuf[c]], w=[th.buf])
                  cp(u3[:, c, 0:16], th.ap, r=[th.buf], w=[ubuf[c]])
                  aF.release(th)
              poolT = aB.alloc(2 * TG, "poolT")
              pl3 = v3(poolT.ap, TG)
              for c in range(2):
                  pi, ps, pb = psalloc()
                  mm(ps[:, :], bdS[:, (l * 2 + c) * 128:(l * 2 + c + 1) * 128], mx3[:, c, :], True, True, r=[bdb, mixed.buf], w=pb)
                  act(pl3[:, c, :], ps[:, :], AF.Identity, r=[pb, parb], w=[poolT.buf], scale=par[:, P_PS + l * 2 + c:P_PS + l * 2 + c + 1])
                  psrel(pi)
              aB.release(mixed)
              if dbg and li == 0 and g == 0:
                  dump("d_pool", poolT.ap, poolT.buf, [128, 2 * TG], BF16)
              ckpt('pool')
              merged = aB.alloc(KC * TG, "merged")
              mg3 = v3(merged.ap, TG)
              for ct in range(2):
                  macc = aF.alloc(4 * TG, "macc")
                  ma3 = v3(macc.ap, TG)
                  for br in range(3):
                      gv, gb = wload(wi[:, :, 2560 + br * 1024 + ct * 512:2560 + br * 1024 + (ct + 1) * 512], 128, 8, 512)
                      if br == 0:
                          src = w_ba_d[li].rearrange("(h p) n -> p h n", p=128)[:, :, ct * 512:(ct + 1) * 512]
                          bv, bb = wload(src, 128, 2, 512)
                          nk, bin_, bbuf, bp = 2, at3, attnT.buf, 128
                      elif br == 1:
                          src = w_br_d[li].rearrange("(h p) n -> p h n", p=128)[:, :, ct * 512:(ct + 1) * 512]
                          bv, bb = wload(src, 128, 4, 512)
                          nk, bin_, bbuf, bp = 4, rT3, retT.buf, 128
                      else:
                          src = w_bp_d[li].rearrange("(h p) n -> p h n", p=128)[:, :, ct * 512:(ct + 1) * 512]
                          bv, bb = wload(src, 128, 2, 512)
                          nk, bin_, bbuf, bp = 2, pl3, poolT.buf, 128
                      for j in range(4):
                          dc = ct * 4 + j
                          gi, gps, gpb = psalloc()
                          for k in range(KC):
                              mm(gps[:, :], gv[:, k, j * 128:(j + 1) * 128], h3[:, k, :], k == 0, k == KC - 1, r=[gb, hb[k]], w=gpb)
                          bi, bps, bpb = psalloc()
                          for k in range(nk):
                              mm(bps[:, :], bv[0:bp, k, j * 128:(j + 1) * 128], bin_[0:bp, k, :], k == 0, k == nk - 1, r=[bb, bbuf], w=bpb)
                          sig = aF.alloc(TG, "sig")
                          act(sig.ap, gps[:, :], AF.Sigmoid, r=[gpb], w=[sig.buf])
                          psrel(gi)
                          if br == 0:
                              tt(ma3[:, j, :], sig.ap, bps[:, :], ALU.mult, r=[sig.buf, bpb], w=[macc.buf])
                          else:
                              tt(sig.ap, sig.ap, bps[:, :], ALU.mult, r=[sig.buf, bpb], w=[sig.buf])
                              if br == 1:
                                  tt(ma3[:, j, :], ma3[:, j, :], sig.ap, ALU.add, r=[sig.buf, macc.buf], w=[macc.buf])
                              else:
                                  tt(mg3[:, dc, :], ma3[:, j, :], sig.ap, ALU.add, r=[sig.buf, macc.buf], w=[merged.buf])
                          psrel(bi)
                          aF.release(sig)
                  aF.release(macc)
              merge_bufs(hT, hb)
              for t_ in (attnT, retT, poolT, hT):
                  aB.release(t_)
              if dbg and li == 0 and g == 0:
                  dump("d_merged", merged.ap, merged.buf, [128, KC * TG], BF16)
              wo = w_out_d[li].rearrange("(k p) n -> p k n", p=128)
              for ct in range(2):
                  wv, wb = wload(wo[:, :, ct * 512:(ct + 1) * 512], 128, 8, 512)
                  for j in range(4):
                      dc = ct * 4 + j
                      pi, ps, pb = psalloc()
                      for k in range(KC):
                          mm(ps[:, :], wv[:, k, j * 128:(j + 1) * 128], mg3[:, k, :], k == 0, k == KC - 1, r=[wb, merged.buf], w=pb)
                      tt(xT3[:, dc, t0:t0 + TG], xT3[:, dc, t0:t0 + TG], ps[:, :], ALU.add, r=[xbuf[dc][g], pb], w=[xbuf[dc][g]])
                      psrel(pi)
              aB.release(merged)
              ckpt('mix')
              hT, h3, hb = rmsnorm(g, P_GFFN + l * 8)
              wu = w_up_d[li].rearrange("(k p) n -> p k n", p=128)
              mT = aB.alloc(22 * TG, "mT")
              m3 = v3(mT.ap, TG)
              mbuf = [Buf(f"m{i}") for i in range(22)]

              def ffn_final(items):
                  for (ch_, y_) in items:
                      if ch_ < 22:
                          act(m3[:, ch_, :], y_.ap, AF.Gelu_apprx_tanh, r=[y_.buf], w=[mbuf[ch_]])
                      else:
                          tt(m3[:, ch_ - 22, :], m3[:, ch_ - 22, :], y_.ap, ALU.mult, r=[y_.buf, mbuf[ch_ - 22]], w=[mbuf[ch_ - 22]])
                      aF.release(y_)

              prev = []
              for it in range(22):
                  if it % 2 == 0:
                      wv, wb = wload(wu[:, :, (it // 2) * 512:(it // 2 + 1) * 512], 128, 8, 512)
                  chs = [2 * it, 2 * it + 1]
                  pss = []
                  for ch in chs:
                      j = ch % 4
                      pi, ps, pb = psalloc()
                      for k in range(KC):
                          mm(ps[:, :], wv[:, k, j * 128:(j + 1) * 128], h3[:, k, :], k == 0, k == KC - 1, r=[wb, hb[k]], w=pb)
                      pss.append((pi, ps, pb))
                  Us = []
                  for ch, (pi, ps, pb) in zip(chs, pss):
                      U = aF.alloc(514, "U", rot=True)
                      act(U.ap[:, 2:514], ps[:, :], AF.Identity, r=[pb], w=[U.buf])
                      psrel(pi)
                      Us.append(U)
                  ffn_final(prev)
                  for ch, U in zip(chs, Us):
                      if ch < 22:
                          cp(U.ap[:, 0:2], convh3[:, ch, :], r=[convhbuf[ch]], w=[U.buf])
                          cp(convh3[:, ch, :], U.ap[:, 512:514], r=[U.buf], w=[convhbuf[ch]])
                      else:
                          act(U.ap[:, 0:2], convh3[:, ch, :], AF.Identity, r=[convhbuf[ch]], w=[U.buf])
                          act(convh3[:, ch, :], U.ap[:, 512:514], AF.Identity, r=[U.buf], w=[convhbuf[ch]])
                  cw = lambda kk, ch: par[:, P_CW + (l * 3 + kk) * 44 + ch:P_CW + (l * 3 + kk) * 44 + ch + 1]
                  ys = []
                  for ch, U in zip(chs, Us):
                      y = aF.alloc(TG, "y", rot=True)
                      cbias = par[:, P_CB + l * 44 + ch:P_CB + l * 44 + ch + 1]
                      act(y.ap, U.ap[:, 2:514], AF.Identity, r=[U.buf, parb], w=[y.buf], bias=cbias, scale=cw(2, ch))
                      ys.append(y)
                  for ch, U, y in zip(chs, Us, ys):
                      stt(y.ap, U.ap[:, 1:513], cw(1, ch), y.ap, ALU.mult, ALU.add, r=[U.buf, y.buf, parb], w=[y.buf])
                  for ch, U, y in zip(chs, Us, ys):
                      stt(y.ap, U.ap[:, 0:512], cw(0, ch), y.ap, ALU.mult, ALU.add, r=[U.buf, y.buf, parb], w=[y.buf])
                  for U in Us:
                      aF.release(U)
                  prev = list(zip(chs, ys))
              ffn_final(prev)
              merge_bufs(hT, hb)
              aB.release(hT)
              wd = w_dn_d[li].rearrange("(k p) n -> p k n", p=128)
              for ct in range(2):
                  accs = [psalloc() for _ in range(4)]
                  for kg, (k0, k1) in enumerate(((0, 8), (8, 16), (16, 22))):
                      wv, wb = wload(wd[:, k0:k1, ct * 512:(ct + 1) * 512], 128, k1 - k0, 512)
                      for j in range(4):
                          for kk in range(k1 - k0):
                              mm(accs[j][1][:, :], wv[:, kk, j * 128:(j + 1) * 128], m3[:, k0 + kk, :], k0 + kk == 0, k0 + kk == 21,
                                 r=[wb, mbuf[k0 + kk]], w=accs[j][2])
                  for j in range(4):
                      dc = ct * 4 + j
                      tt(xT3[:, dc, t0:t0 + TG], xT3[:, dc, t0:t0 + TG], accs[j][1][:, :], ALU.add, r=[xbuf[dc][g], accs[j][2]], w=[xbuf[dc][g]])
                      psrel(accs[j][0])
              for b_ in mbuf:
                  for e_, o_ in b_.r.items():
                      p_ = mT.buf.r.get(e_)
                      if p_ is None or p_.idx < o_.idx:
                          mT.buf.r[e_] = o_
                  if b_.w is not None:
                      p_ = mT.buf.r.get(b_.w.eng)
                      if p_ is None or p_.idx < b_.w.idx:
                          mT.buf.r[b_.w.eng] = b_.w
              aB.release(mT)
              ckpt('ffn')
              hT, h3, hb = rmsnorm(g, P_GPLE + l * 8)
              pt_ = aB.alloc(2 * TG, "pT")
              p3 = v3(pt_.ap, TG)
              dma('pool', p3, pT_d[li].rearrange("(c p) t -> p c t", p=128)[:, :, t0:t0 + TG], 'pT', w=[pt_.buf])
              wg = w_pg_d[li].rearrange("(k p) n -> p k n", p=128)
              wp = w_pp_d[li].rearrange("(k p) n -> p k n", p=128)
              for ct in range(2):
                  gv, gb = wload(wg[:, :, ct * 512:(ct + 1) * 512], 128, 8, 512)
                  pv, pbb = wload(wp[:, :, ct * 512:(ct + 1) * 512], 128, 2, 512)
                  for j in range(4):
                      dc = ct * 4 + j
                      gi, gps, gpb = psalloc()
                      for k in range(KC):
                          mm(gps[:, :], gv[:, k, j * 128:(j + 1) * 128], h3[:, k, :], k == 0, k == KC - 1, r=[gb, hb[k]], w=gpb)
                      bi, bps, bpb = psalloc()
                      for k in range(2):
                          mm(bps[:, :], pv[:, k, j * 128:(j + 1) * 128], p3[:, k, :], k == 0, k == 1, r=[pbb, pt_.buf], w=bpb)
                      sig = aF.alloc(TG, "sig")
                      act(sig.ap, gps[:, :], AF.Sigmoid, r=[gpb], w=[sig.buf])
                      psrel(gi)
                      tt(sig.ap, sig.ap, bps[:, :], ALU.mult, r=[sig.buf, bpb], w=[sig.buf])
                      psrel(bi)
                      tt(xT3[:, dc, t0:t0 + TG], xT3[:, dc, t0:t0 + TG], sig.ap, ALU.add, r=[xbuf[dc][g], sig.buf], w=[xbuf[dc][g]])
                      aF.release(sig)
              aB.release(pt_)
              merge_bufs(hT, hb)
              aB.release(hT)
    except StopBuild:
        pass
    S.epoch += 1
    osrc = out_d.rearrange("(c p) t -> p c t", p=128)
    for g in range(NG):
        t0 = g * TG
        if final_norm and g < ngrun:
            pi, ps, pb = psalloc()
            for c in range(KC):
                sq = aB.alloc(TG, "sq")
                act(sq.ap, xT3[:, c, t0:t0 + TG], AF.Square, r=[xbuf[c][g]], w=[sq.buf])
                mm(ps[:, :], ones, sq.ap, c == 0, c == KC - 1, r=[sq.buf, cBb], w=pb)
                aB.release(sq)
            rstd = aF.alloc(TG, "rstd")
            act(rstd.ap, ps[:, :], AF.Sqrt, r=[pb, cFb], w=[rstd.buf], bias=epsc, scale=1.0 / D)
            recip(rstd.ap, rstd.ap, r=[rstd.buf], w=[rstd.buf])
            psrel(pi)
            for c in range(KC):
                o = aF.alloc(TG, "o")
                stt(o.ap, xT3[:, c, t0:t0 + TG], par[:, P_GFIN + c:P_GFIN + c + 1], rstd.ap, ALU.mult, ALU.mult,
                    r=[xbuf[c][g], parb, rstd.buf], w=[o.buf])
                dma('sp', osrc[:, c, t0:t0 + TG], o.ap, 'out', r=[o.buf])
                aF.release(o)
            aF.release(rstd)
        else:
            for c in range(KC):
                dma('sp', osrc[:, c, t0:t0 + TG], xT3[:, c, t0:t0 + TG], 'out', r=[xbuf[c][g]])
    S.emit(nc, es)
    es.close()
    stats = dict(n_ops={e: len(S.q[e]) for e in ENGS}, nwaits=S.nwaits, aF_peak=aF.peak, aB_peak=aB.peak)
    return nc, stats, dbg_outs


_CONSTS = None


def prep_inputs(inp, layers):
    global _CONSTS
    if _CONSTS is None:
        _CONSTS = make_consts()
    cf, cb, rot = _CONSTS
    par, bd = pack_params(inp)
    ls = list(layers)
    f = lambda k: np.ascontiguousarray(np.asarray(inp[k], np.float32)[ls])
    shared = dict(
        w_in=f("w_in"), w_ba=f("w_branch_attn"), w_br=f("w_branch_ret"), w_bp=f("w_branch_pool"),
        w_out=f("w_out"), w_up=f("w_up"), w_dn=f("w_down"), w_pg=f("w_ple_gate"), w_pp=f("w_ple_proj"),
        par=par, bd=bd, cf=cf, cb=cb, rot=rot,
    )
    return shared


def run_layers(xT_all, inp, layers, final_norm, ngrun=NG, dbg=False, ncores=8):
    nc, stats, dbg_outs = build_program(layers, final_norm, ngrun, dbg)
    shared = prep_inputs(inp, layers)
    p = np.asarray(inp["p"], np.float32)
    in_maps = []
    for b in range(ncores):
        m = dict(shared)
        m["xT"] = np.ascontiguousarray(xT_all[b])
        m["pT"] = np.ascontiguousarray(p[list(layers), b].transpose(0, 2, 1))
        in_maps.append(m)
    res = run_bass_kernel_spmd(nc, in_maps, core_ids=list(range(ncores)))
    outs = np.stack([np.asarray(r["outT"]) for r in res.results], axis=0)
    return outs, res, stats


def kernel(**inputs):
    x = np.asarray(inputs["x"], np.float32)
    xT = np.ascontiguousarray(x.transpose(0, 2, 1))
    outT, _, _ = run_layers(xT, inputs, range(L), True)
    return np.ascontiguousarray(outT.transpose(0, 2, 1)).astype(np.float32)
```
